# Optimizing a Trainium2 kernel written in Bass

```python
import jax, jax.numpy as jnp
from jax import lax
import numpy as np

D_MODEL = 1024
BATCH = 2
SEQ = 16384
DEPTH = 1
DEC_BATCH = 32
DEC_SEQ = 64
PAST_LEN = 2048

CHUNK = 64
H_A = 4
DK_A = 128
DV_A = 128
K_A = H_A * DK_A
V_A = H_A * DV_A
H_B = 8
DH_B = 64
W_B = H_B * DH_B
N_PAST_CHUNKS = 8
WINDOW = N_PAST_CHUNKS * CHUNK
REL_CLIP = 128
NUM_REL = CHUNK + REL_CLIP
D_FF = -(-8 * D_MODEL // (3 * 256)) * 256
SPLIT_SIZES = [K_A, K_A, V_A, V_A, W_B, W_B, W_B, D_MODEL, D_MODEL]
IN_COLS = int(sum(SPLIT_SIZES))
EPS = 1e-6
NEG = -1e30

kernel_name = 'hgrn2_chunkband_attn_hybrid_step'


def rmsnorm(x, g):
    xf = x.astype(jnp.float32)
    y = xf * lax.rsqrt(jnp.mean(xf * xf, axis=-1, keepdims=True) + EPS) * g.astype(jnp.float32)
    return y.astype(x.dtype)


def hgrn2_recurrence(q, logf, k, v, s0):
    b, l = q.shape[:2]
    c = CHUNK if l % CHUNK == 0 else l
    n = l // c

    def to_chunks(t):
        return t.reshape(b, n, c, *t.shape[2:]).swapaxes(0, 1)

    tri = jnp.tril(jnp.ones((c, c), dtype=bool))

    def step(s, inp):
        qc, lfc, kc, vc = inp
        cum = jnp.cumsum(lfc, axis=1)
        tot = cum[:, -1]
        ref = cum[:, c // 2][:, None]
        a = jnp.einsum('bthk,bshk->bhts', qc * jnp.exp(cum - ref), kc * jnp.exp(ref - cum))
        a = jnp.where(tri, a, 0.0)
        o = (jnp.einsum('bhts,bshv->bthv', a, vc)
             + jnp.einsum('bthk,bhkv->bthv', qc * jnp.exp(cum), s))
        s_new = (jnp.exp(tot)[..., None] * s
                 + jnp.einsum('bshk,bshv->bhkv', kc * jnp.exp(tot[:, None] - cum), vc))
        return s_new, o

    s_fin, o = lax.scan(step, s0, (to_chunks(q), to_chunks(logf), to_chunks(k), to_chunks(v)))
    o = o.swapaxes(0, 1).reshape(b, l, *o.shape[3:])
    return o, s_fin


def band_attention(q, k, v, q_pos, k_pos, rel_bias):
    s = jnp.einsum('bqhd,bkhd->bhqk', q, k).astype(jnp.float32) * (DH_B ** -0.5)
    dist = q_pos[:, None] - k_pos[None, :]
    bias = rel_bias.astype(jnp.float32)[:, jnp.clip(dist, -(CHUNK - 1), REL_CLIP) + (CHUNK - 1)]
    qc = (q_pos // CHUNK)[:, None]
    kc = (k_pos // CHUNK)[None, :]
    mask = (k_pos[None, :] >= 0) & (kc <= qc) & (qc - kc <= N_PAST_CHUNKS)
    s = jnp.where(mask[None, None], s + bias[None], NEG)
    p = jax.nn.softmax(s, axis=-1).astype(v.dtype)
    return jnp.einsum('bhqk,bkhd->bqhd', p, v)


def prompt_band_attention(q, k, v, rel_bias):
    b, l, h, d = q.shape
    n = l // CHUNK
    pad = ((0, 0), (WINDOW, 0), (0, 0), (0, 0))
    kp = jnp.pad(k, pad)
    vp = jnp.pad(v, pad)

    def one_chunk(i):
        start = i * CHUNK
        qc = lax.dynamic_slice_in_dim(q, start, CHUNK, axis=1)
        kc = lax.dynamic_slice_in_dim(kp, start, WINDOW + CHUNK, axis=1)
        vc = lax.dynamic_slice_in_dim(vp, start, WINDOW + CHUNK, axis=1)
        q_pos = start + jnp.arange(CHUNK, dtype=jnp.int32)
        k_pos = start - WINDOW + jnp.arange(WINDOW + CHUNK, dtype=jnp.int32)
        return band_attention(qc, kc, vc, q_pos, k_pos, rel_bias)

    o = lax.map(one_chunk, jnp.arange(n, dtype=jnp.int32))
    return o.swapaxes(0, 1).reshape(b, l, h, d)


def mixer(h, lb, w_in, hgrn_out_norm, w_branch_a, rel_bias, w_branch_b, w_out, s0, kv_cache):
    b, l, _ = h.shape
    p = h @ w_in
    qa, fa, ia, ga, qb, kb, vb, gate_a, gate_b = jnp.split(
        p, list(np.cumsum(SPLIT_SIZES)[:-1]), axis=-1)
    f = lb + (1.0 - lb) * jax.nn.sigmoid(fa.astype(jnp.float32))
    logf = jnp.log(f).reshape(b, l, H_A, DK_A)
    k_a = (1.0 - f).reshape(b, l, H_A, DK_A)
    q_a = jax.nn.silu(qa.astype(jnp.float32)).reshape(b, l, H_A, DK_A)
    v_a = ia.astype(jnp.float32).reshape(b, l, H_A, DV_A)
    o_a, s_fin = hgrn2_recurrence(q_a, logf, k_a, v_a, s0.astype(jnp.float32))
    o_a = o_a * lax.rsqrt(jnp.mean(o_a * o_a, axis=-1, keepdims=True) + EPS)
    o_a = (o_a.reshape(b, l, V_A) * hgrn_out_norm.astype(jnp.float32)).astype(h.dtype)
    y_a = (o_a * jax.nn.silu(ga)) @ w_branch_a
    q_b = qb.reshape(b, l, H_B, DH_B)
    k_b = kb.reshape(b, l, H_B, DH_B)
    v_b = vb.reshape(b, l, H_B, DH_B)
    if kv_cache is None:
        o_b = prompt_band_attention(q_b, k_b, v_b, rel_bias)
        keep = min(WINDOW, l)
        k_rows, v_rows = k_b[:, l - keep:], v_b[:, l - keep:]
    else:
        ck, cv = kv_cache
        w = ck.shape[1]
        q_pos = PAST_LEN + jnp.arange(l, dtype=jnp.int32)
        k_pos = jnp.concatenate([PAST_LEN - w + jnp.arange(w, dtype=jnp.int32), q_pos])
        o_b = band_attention(q_b, jnp.concatenate([ck, k_b], axis=1),
                             jnp.concatenate([cv, v_b], axis=1), q_pos, k_pos, rel_bias)
        k_rows, v_rows = k_b, v_b
    y_b = o_b.reshape(b, l, W_B) @ w_branch_b
    y = (jax.nn.sigmoid(gate_a) * y_a + jax.nn.sigmoid(gate_b) * y_b) @ w_out
    return y, s_fin.astype(h.dtype), k_rows, v_rows


def trunk(x, c, states, caches_k, caches_v, w_ada, b_ada, norm_mix, w_in, hgrn_lb_logits,
          hgrn_out_norm, w_branch_a, rel_bias, w_branch_b, w_out, norm_ffn, w_ffn_in,
          w_ffn_out, norm_final):
    b = x.shape[0]
    lb_all = jnp.cumsum(jax.nn.softmax(hgrn_lb_logits.astype(jnp.float32), axis=0), axis=0)
    new_s, new_k, new_v = [], [], []
    for layer in range(DEPTH):
        mod = jax.nn.silu(c) @ w_ada[layer] + b_ada[layer]
        sh1, sc1, g1, sh2, sc2, g2 = jnp.split(mod[:, None, :], 6, axis=-1)
        h = rmsnorm(x, norm_mix[layer]) * (1.0 + sc1) + sh1
        if states is None:
            s0 = jnp.zeros((b, H_A, DK_A, DV_A), jnp.float32)
            kv = None
        else:
            s0 = states[layer]
            kv = (caches_k[layer], caches_v[layer])
        y, s_fin, k_rows, v_rows = mixer(h, lb_all[layer], w_in[layer], hgrn_out_norm[layer],
                                         w_branch_a[layer], rel_bias[layer], w_branch_b[layer],
                                         w_out[layer], s0, kv)
        x = x + g1 * y
        h = rmsnorm(x, norm_ffn[layer]) * (1.0 + sc2) + sh2
        a, u = jnp.split(h @ w_ffn_in[layer], 2, axis=-1)
        x = x + g2 * ((jax.nn.silu(a) * u) @ w_ffn_out[layer])
        new_s.append(s_fin)
        new_k.append(k_rows)
        new_v.append(v_rows)
    return rmsnorm(x, norm_final), jnp.stack(new_s), jnp.stack(new_k), jnp.stack(new_v)


def setup_inputs(seed: int = 0) -> dict:
    key = jax.random.key(seed)
    ks = jax.random.split(key, 24)
    f32 = jnp.float32

    def nrm(k, shape, scale=1.0):
        return jax.random.normal(k, shape, f32) * scale

    cache_rows = min(WINDOW, PAST_LEN)
    return {
        'x_prompt': nrm(ks[0], (BATCH, SEQ, D_MODEL)),
        'x_sample': nrm(ks[1], (DEC_BATCH, DEC_SEQ, D_MODEL)),
        'c_prompt': nrm(ks[2], (BATCH, D_MODEL)),
        'c_sample': nrm(ks[3], (DEC_BATCH, D_MODEL)),
        'state_hgrn': nrm(ks[4], (DEPTH, DEC_BATCH, H_A, DK_A, DV_A), 0.5),
        'cache_k': nrm(ks[5], (DEPTH, DEC_BATCH, cache_rows, H_B, DH_B)),
        'cache_v': nrm(ks[6], (DEPTH, DEC_BATCH, cache_rows, H_B, DH_B)),
        'w_ada': nrm(ks[7], (DEPTH, D_MODEL, 6 * D_MODEL), 0.5 * D_MODEL ** -0.5),
        'b_ada': nrm(ks[8], (DEPTH, 6 * D_MODEL), 0.02),
        'norm_mix': 1.0 + nrm(ks[9], (DEPTH, D_MODEL), 0.05),
        'w_in': nrm(ks[10], (DEPTH, D_MODEL, IN_COLS), D_MODEL ** -0.5),
        'hgrn_lb_logits': nrm(ks[11], (DEPTH + 1, K_A), 0.1),
        'hgrn_out_norm': 1.0 + nrm(ks[12], (DEPTH, V_A), 0.05),
        'w_branch_a': nrm(ks[13], (DEPTH, V_A, D_MODEL), V_A ** -0.5),
        'rel_bias': nrm(ks[14], (DEPTH, H_B, NUM_REL), 0.5),
        'w_branch_b': nrm(ks[15], (DEPTH, W_B, D_MODEL), W_B ** -0.5),
        'w_out': nrm(ks[16], (DEPTH, D_MODEL, D_MODEL), D_MODEL ** -0.5),
        'norm_ffn': 1.0 + nrm(ks[17], (DEPTH, D_MODEL), 0.05),
        'w_ffn_in': nrm(ks[18], (DEPTH, D_MODEL, 2 * D_FF), D_MODEL ** -0.5),
        'w_ffn_out': nrm(ks[19], (DEPTH, D_FF, D_MODEL), D_FF ** -0.5),
        'norm_final': 1.0 + nrm(ks[20], (D_MODEL,), 0.05),
    }


def reference(x_prompt, x_sample, c_prompt, c_sample, state_hgrn, cache_k, cache_v, w_ada, b_ada,
              norm_mix, w_in, hgrn_lb_logits, hgrn_out_norm, w_branch_a, rel_bias, w_branch_b,
              w_out, norm_ffn, w_ffn_in, w_ffn_out, norm_final):
    y_prompt, s_p, k_p, v_p = trunk(x_prompt, c_prompt, None, None, None, w_ada, b_ada, norm_mix,
                                    w_in, hgrn_lb_logits, hgrn_out_norm, w_branch_a, rel_bias,
                                    w_branch_b, w_out, norm_ffn, w_ffn_in, w_ffn_out, norm_final)
    y_sample, s_s, k_s, v_s = trunk(x_sample, c_sample, state_hgrn, cache_k, cache_v, w_ada, b_ada,
                                    norm_mix, w_in, hgrn_lb_logits, hgrn_out_norm, w_branch_a,
                                    rel_bias, w_branch_b, w_out, norm_ffn, w_ffn_in, w_ffn_out,
                                    norm_final)
    return (y_prompt, y_sample, s_p, k_p, v_p, s_s, k_s, v_s)
```

```python
import numpy as np
from contextlib import ExitStack
import concourse.bass as bass
import concourse.mybir as mybir
from concourse.bass_utils import run_bass_kernel_spmd

F32 = mybir.dt.float32
BF16 = mybir.dt.bfloat16
AF = mybir.ActivationFunctionType
ALU = mybir.AluOpType

D = 1024
NCORES = 8
EPS = 1e-6
NEGM = -30000.0

ENGS = ("pe", "act", "dve", "pool", "sp")
NDMA_SEM = 16
SAME_ENGINE_SYNC = True
PS_EXCL = True
DBG_SETS = False
KSTOP = 1e9


class Op:
    __slots__ = ("eng", "fn", "dma", "idx", "deps", "waited", "sem", "val", "dq")

    def __init__(self, eng, fn, dma):
        self.eng = eng
        self.fn = fn
        self.dma = dma
        self.deps = []
        self.waited = False
        self.sem = None
        self.val = 0
        self.dq = -1


class Sched:
    def __init__(self, nc):
        self.nc = nc
        self.ops = []
        self.by_eng = {e: [] for e in ENGS}
        self.state = {}

    def _ents(self, name, sub):
        d = self.state.setdefault(name, {})
        if sub is None:
            if None not in d:
                d[None] = [None, []]
            return list(d.values())
        if sub not in d:
            w = d[None][0] if None in d else None
            d[sub] = [w, []]
        ents = [d[sub]]
        if None in d:
            ents.append(d[None])
        return ents

    @staticmethod
    def _k(key):
        return key if isinstance(key, tuple) else (key, None)

    def add(self, eng, fn, reads=(), writes=(), dma=False):
        op = Op(eng, fn, dma)
        op.idx = len(self.ops)
        deps = {}
        if PS_EXCL and eng != "pe":
            writes = list(writes) + [k_ for k_ in reads if isinstance(k_, tuple) and k_[0] == "ps"]
        for key in reads:
            name, sub = self._k(key)
            for ent in self._ents(name, sub):
                if ent[0] is not None:
                    deps[ent[0].idx] = ent[0]
        for key in writes:
            name, sub = self._k(key)
            for ent in self._ents(name, sub):
                if ent[0] is not None:
                    deps[ent[0].idx] = ent[0]
                for r in ent[1]:
                    deps[r.idx] = r
        for key in reads:
            name, sub = self._k(key)
            d = self.state[name]
            if sub is None:
                for ent in d.values():
                    ent[1].append(op)
            else:
                d[sub][1].append(op)
        for key in writes:
            name, sub = self._k(key)
            d = self.state[name]
            if sub is None:
                for ent in d.values():
                    ent[0] = op
                    ent[1] = []
            else:
                d[sub][0] = op
                d[sub][1] = []
        deps.pop(op.idx, None)
        op.deps = [deps[i] for i in sorted(deps)]
        self.by_eng[eng].append(op)
        self.ops.append(op)
        return op

    @staticmethod
    def _need_wait(op, d):
        if d.dma:
            return True
        if d.eng == op.eng:
            if op.eng == "pe":
                return False
            if op.dma:
                return True
            return SAME_ENGINE_SYNC
        return True

    def emit(self, sems, dma_sems):
        nc = self.nc
        for op in self.ops:
            for d in op.deps:
                if self._need_wait(op, d):
                    d.waited = True
        final_waits = []
        for e in ENGS:
            cnt = 0
            dcnt = 0
            for op in self.by_eng[e]:
                if op.dma:
                    op.dq = dcnt
                    op.sem = dma_sems[e][dcnt % NDMA_SEM]
                    op.val = 16 * (dcnt // NDMA_SEM + 1)
                    dcnt += 1
                elif op.waited:
                    cnt += 1
                    op.sem = sems[e]
                    op.val = cnt
            for i in range(min(dcnt, NDMA_SEM)):
                uses = (dcnt - 1 - i) // NDMA_SEM + 1
                final_waits.append((dma_sems[e][i], 16 * uses))
        stats = {}

        def run(e):
            def body(eng):
                seen = {}
                nw = 0
                for op in self.by_eng[e]:
                    waits = []
                    if op.dma and op.dq >= NDMA_SEM:
                        waits.append((op.sem, op.val - 16))
                    for d in op.deps:
                        if self._need_wait(op, d):
                            waits.append((d.sem, d.val))
                    best = {}
                    for s, v in waits:
                        kk = id(s)
                        if seen.get(kk, 0) >= v:
                            continue
                        if kk not in best or best[kk][1] < v:
                            best[kk] = (s, v)
                    for kk, (s, v) in best.items():
                        eng.wait_ge(s, v)
                        seen[kk] = v
                        nw += 1
                    ins = op.fn(eng)
                    if op.dma:
                        ins.then_inc(op.sem, 16)
                    elif op.waited:
                        ins.then_inc(op.sem, 1)
                if e == "sp":
                    for s, v in final_waits:
                        eng.wait_ge(s, v)
                stats[e] = (len(self.by_eng[e]), nw)
            return body

        with nc.Block() as block:
            block.tensor(run("pe"))
            block.scalar(run("act"))
            block.vector(run("dve"))
            block.gpsimd(run("pool"))
            block.sync(run("sp"))
        return stats


class _Stop(Exception):
    pass


class Builder:
    def chk(self, n):
        if n > KSTOP:
            raise _Stop()

    def __init__(self, NS):
        self.NS = NS
        self.NP = NS * 512
        self.NTOK = 256 + 512 + self.NP
        self.NOUT = 256 + self.NP
        self.nc = bass.Bass("TRN2", target_bir_lowering=False)
        self.es = ExitStack()

    def sb(self, name, shape, dt):
        return self.es.enter_context(self.nc.sbuf_tensor(name, shape, dt))

    def din(self, name, shape):
        return self.nc.dram_tensor(name, shape, F32, kind="ExternalInput").ap()

    def dout(self, name, shape):
        return self.nc.dram_tensor(name, shape, F32, kind="ExternalOutput").ap()

    def op(self, eng, fn, r=(), w=(), dma=False):
        return self.S.add(eng, fn, reads=r, writes=w, dma=dma)

    def mm(self, out, lhsT, rhs, start, stop, r, w):
        self.op("pe", lambda e: e.matmul(out, lhsT=lhsT, rhs=rhs, start=start, stop=stop), r, w)

    def act(self, out, in_, func, r, w, scale=1.0, bias=None):
        if bias is None:
            self.op("act", lambda e: e.activation(out=out, in_=in_, func=func, scale=scale), r, w)
        else:
            self.op("act", lambda e: e.activation(out=out, in_=in_, func=func, scale=scale, bias=bias), r, w)

    def tt(self, out, in0, in1, op, r, w, eng="dve"):
        self.op(eng, lambda e: e.tensor_tensor(out=out, in0=in0, in1=in1, op=op), r, w)

    def ts(self, out, in0, s1, s2, op0, op1, r, w, eng="dve"):
        if s2 is None:
            self.op(eng, lambda e: e.tensor_scalar(out=out, in0=in0, scalar1=s1, scalar2=None, op0=op0), r, w)
        else:
            self.op(eng, lambda e: e.tensor_scalar(out=out, in0=in0, scalar1=s1, scalar2=s2, op0=op0, op1=op1), r, w)

    def stt(self, out, in0, scalar, in1, op0, op1, r, w):
        self.op("dve", lambda e: e.scalar_tensor_tensor(out=out, in0=in0, scalar=scalar, in1=in1, op0=op0, op1=op1), r, w)

    def cp(self, out, in_, r, w, eng="dve"):
        if eng == "act":
            self.op("act", lambda e: e.activation(out=out, in_=in_, func=AF.Copy), r, w)
        else:
            self.op(eng, lambda e: e.tensor_copy(out=out, in_=in_), r, w)

    def memset(self, ap, val, w, eng="pool"):
        self.op(eng, lambda e: e.memset(ap, val), (), w)

    def dma(self, eng, out, in_, r, w):
        self.op(eng, lambda e: e.dma_start(out=out, in_=in_, max_dma_last_dim=2048), r, w, dma=True)

    def pa(self):
        assert self.pfree, "out of PSUM banks"
        return self.pfree.pop(0)

    def pf(self, b):
        self.pfree.append(b)

    def PB(self, b):
        return self.ps[b]

    def wld(self, name, blk, k0=0, kn=8, width=512):
        slot = self.wslot
        self.wslot = (self.wslot + 1) % len(self.wb)
        t = self.wb[slot]
        src = self.dr[name]
        key = ("wb", slot)
        self.dma("pool", t[:, 0:kn, 0:width], src[blk, :, k0:k0 + kn, 0:width], (), [key])
        return t, key

    def wld_arena(self, name, blk, aslot):
        t = self.sT[:, 8 * aslot:8 * aslot + 8, :]
        keys = [("arena", u) for u in range(8 * aslot, 8 * aslot + 8)]
        self.dma("pool", t, self.dr[name][blk, :, :, :], (), keys)
        return t, keys

    def build(self):
        nc = self.nc
        with self.es:
            self._declare()
            self.S = Sched(nc)
            self._setup()
            sts = []
            sts.append(dict(kind="sample", col0=0, ncol=256, ocol=0,
                            segs=[(i * 64, 64, 1 + i) for i in range(4)], cur=0))
            sts.append(dict(kind="halo", col0=256, ncol=512, segs=[(0, 512, 0)], cur=1))
            for s in range(self.NS):
                sts.append(dict(kind="prompt", col0=768 + 512 * s, ncol=512, ocol=256 + 512 * s,
                                segs=[(0, 512, 0)], cur=s % 2, first=(s == 0), last=(s == self.NS - 1)))
            try:
                self.chk(1)
                for st in sts:
                    self._supertile(st)
                self._finish()
            except _Stop:
                pass
            sems = {e: self.es.enter_context(nc.semaphore("s_" + e)) for e in ENGS}
            dsems = {e: [self.es.enter_context(nc.semaphore("d_%s%d" % (e, i))) for i in range(NDMA_SEM)]
                     for e in ENGS}
            self.stats = self.S.emit(sems, dsems)
        return nc

    def _declare(self):
        nc = self.nc
        dr = {}
        dr["xT"] = self.din("xT", [D, self.NTOK])
        dr["cT"] = self.din("cT", [128, 8, 8])
        dr["flags"] = self.din("flags", [128, 4])
        dr["sin"] = self.din("sin", [128, 4, 4, 128])
        dr["ckT"] = self.din("ckT", [4, 128, 4, 512])
        dr["cv"] = self.din("cv", [4, 128, 4, 512])
        dr["w_ada"] = self.din("w_ada", [12, 128, 8, 512])
        dr["w_in"] = self.din("w_in", [11, 128, 8, 512])
        dr["w_br"] = self.din("w_br", [2, 128, 8, 512])
        dr["w_out"] = self.din("w_out", [2, 128, 8, 512])
        dr["w_fi"] = self.din("w_fi", [12, 128, 8, 512])
        dr["w_fo"] = self.din("w_fo", [2, 128, 22, 512])
        dr["b_ada"] = self.din("b_ada", [128, 48])
        dr["nmix"] = self.din("nmix", [128, 8])
        dr["nffn"] = self.din("nffn", [128, 8])
        dr["nfin"] = self.din("nfin", [128, 8])
        dr["onorm"] = self.din("onorm", [128, 4])
        dr["lbl"] = self.din("lbl", [128, 4, 2])
        dr["rb"] = self.din("rb", [8, 192])
        dr["yT"] = self.dout("yT", [D, self.NOUT])
        dr["s_out"] = self.dout("s_out", [128, 5, 4, 128])
        dr["kTs"] = self.dout("kTs", [128, 4, 256])
        dr["vs"] = self.dout("vs", [128, 2, 512])
        dr["kTp"] = self.dout("kTp", [128, 4, 512])
        dr["vp"] = self.dout("vp", [128, 4, 512])
        self.ext_d = nc.dram_tensor("ext_d", [8, 383], F32)
        self.dr = dr
        sb = self.sb
        self.onesf = sb("onesf", [128, 128], F32)
        self.identf = sb("identf", [128, 128], F32)
        self.identb = sb("identb", [128, 128], BF16)
        self.onesb = sb("onesb", [128, 128], BF16)
        self.Jf = sb("Jf", [128, 128], F32)
        self.resetm = sb("resetm", [128, 512], F32)
        self.trim = sb("trim", [128, 4, 128], F32)
        self.epsc = sb("epsc", [128, 1], F32)
        self.cT = sb("cT_sb", [128, 8, 8], F32)
        self.scT = sb("scT", [128, 8, 8], BF16)
        self.flags = sb("flags_sb", [128, 4], F32)
        self.b_ada = sb("b_ada_sb", [128, 48], F32)
        self.nmix = sb("nmix_sb", [128, 8], F32)
        self.nffn = sb("nffn_sb", [128, 8], F32)
        self.nfin = sb("nfin_sb", [128, 8], F32)
        self.onorm = sb("onorm_sb", [128, 4], F32)
        self.lbl = sb("lbl_sb", [128, 4, 2], F32)
        self.lbd = sb("lbd", [128, 4], F32)
        self.lb = sb("lb", [128, 4], F32)
        self.oml = sb("oml", [128, 4], F32)
        self.modT = sb("modT", [128, 48, 8], F32)
        self.gm1 = sb("gm1", [128, 8, 8], F32)
        self.gm2 = sb("gm2", [128, 8, 8], F32)
        self.rbt = sb("rbt", [8, 192], F32)
        self.ext = sb("ext", [8, 383], F32)
        self.BT3 = sb("BT3", [128, 8, 128], F32)
        self.BT4 = sb("BT4", [128, 8, 128], F32)
        self.BT5 = sb("BT5", [128, 8, 64], F32)
        self.xT = sb("xT_sb", [128, 8, 512], F32)
        self.hT = sb("hT", [128, 8, 512], BF16)
        self.mT = sb("mT", [128, 8, 512], BF16)
        self.rstd = sb("rstd", [128, 512], F32)
        self.wb = [sb("wb%d" % i, [128, 8, 512], BF16) for i in range(4)]
        self.wslot = 0
        self.arena = sb("arena", [128, 22 * 512], BF16)
        ar32 = self.arena.bitcast(F32)
        self.sT = self.arena[:, :].rearrange("p (j t) -> p j t", t=512)
        self.sqa4 = ar32[:, 0:2048].rearrange("p (h t) -> p h t", t=512)
        self.fT4 = ar32[:, 2048:4096].rearrange("p (h t) -> p h t", t=512)
        self.kTf = ar32[:, 0:2048].rearrange("p (h t) -> p h t", t=512)
        self.vf = ar32[:, 2048:4096].rearrange("p (h t) -> p h t", t=512)
        self.lgf = sb("lgf", [128, 512], F32)
        self.cum = sb("cum", [128, 512], F32)
        self.dd = self.lgf
        self.e1 = sb("e1", [128, 512], F32)
        self.e2 = self.lgf
        self.qpT = sb("qpT", [128, 4, 512], BF16)
        self.kpT = sb("kpT", [128, 4, 512], BF16)
        self.vtok = sb("vtok", [128, 4, 512], BF16)
        self.sga = sb("sga", [128, 4, 512], BF16)
        self.AT = sb("AT", [128, 4, 512], BF16)
        self.ktokA = sb("ktokA", [128, 4, 4, 128], BF16)
        self.ktokB = sb("ktokB", [128, 4, 4, 128], BF16)
        self.Sst = sb("Sst", [128, 4, 128], F32)
        self.sin = sb("sin_sb", [128, 4, 4, 128], F32)
        self.Ssc = sb("Ssc", [128, 8, 128], BF16)
        self.tmpS = sb("tmpS", [128, 8, 128], F32)
        self.eref = sb("eref", [128, 4, 8], F32)
        self.etot = sb("etot", [128, 4, 8], F32)
        self.etr = sb("etr", [128, 4, 8], F32)
        self.oaT = sb("oaT", [128, 4, 512], BF16)
        self.sqo = sb("sqo", [128, 4, 512], BF16)
        self.rso = self.e1
        self.t1 = sb("t1", [128, 512], F32)
        self.t2 = sb("t2", [128, 512], F32)
        self.sa = sb("sa", [128, 512], F32)
        self.sbg = sb("sbg", [128, 512], F32)
        m32 = self.mT.bitcast(F32)[:, :, :].rearrange("p a b -> p (a b)")
        self.hset = [(self.lgf, self.cum, self.e1),
                     (ar32[:, 4096:4608], ar32[:, 4608:5120], ar32[:, 5120:5632]),
                     (m32[:, 0:512], m32[:, 512:1024], m32[:, 1024:1536]),
                     (self.sa, self.sbg, self.rstd)]
        self.hkeys = [(["lgf"], ["cum"], ["e1"]),
                      ([("arena", 16), ("arena", 17)], [("arena", 18), ("arena", 19)], [("arena", 20), ("arena", 21)]),
                      ([("mT", 0), ("mT", 1)], [("mT", 2), ("mT", 3)], [("mT", 4), ("mT", 5)]),
                      (["sa"], ["sbg"], ["rstd"])]
        if DBG_SETS:
            self.hset[2], self.hkeys[2] = self.hset[0], self.hkeys[0]
            self.hset[3], self.hkeys[3] = self.hset[1], self.hkeys[1]
        self.qTA = sb("qTA", [128, 4, 512], BF16)
        self.qTB = sb("qTB", [128, 4, 512], BF16)
        self.kT = [sb("kT%d" % i, [128, 4, 512], BF16) for i in range(2)]
        self.Va = [sb("Va%d" % i, [128, 4, 8, 65], BF16) for i in range(2)]
        self.sbias = sb("sbias", [128, 2, 256], F32)
        self.PTf = [sb("PTf%d" % i, [128, 5, 128], BF16) for i in range(2)]
        self.PTh = [sb("PTh%d" % i, [128, 5, 128], BF16) for i in range(2)]
        self.rden = sb("rden", [128, 8], F32)
        self.ob = self.AT
        self.obT = self.vtok
        self.ps = [self.es.enter_context(nc.psum_tensor("ps%d" % i, [128, 512], F32)) for i in range(8)]
        self.pfree = list(range(8))

    def _setup(self):
        dr = self.dr
        for name, t in (("cT", self.cT), ("flags", self.flags), ("b_ada", self.b_ada), ("nmix", self.nmix),
                        ("nffn", self.nffn), ("nfin", self.nfin), ("onorm", self.onorm), ("lbl", self.lbl),
                        ("rb", self.rbt), ("sin", self.sin)):
            self.dma("sp", t[:], dr[name], (), [t.name])
        self.memset(self.onesf[:], 1.0, ["onesf"])
        self.memset(self.onesb[:], 1.0, ["onesb"])
        self.memset(self.epsc[:], EPS, ["epsc"])
        self.memset(self.identf[:], 0.0, ["identf"])
        self.op("pool", lambda e: e.affine_select(out=self.identf[:], in_=self.onesf[:], pattern=[[-1, 128]],
                                                  compare_op=ALU.is_equal, fill=0.0, base=0, channel_multiplier=1),
                ["onesf"], ["identf"])
        self.cp(self.identb[:], self.identf[:], ["identf"], ["identb"])
        self.memset(self.Jf[:], 0.0, ["Jf"])
        self.op("pool", lambda e: e.affine_select(out=self.Jf[:], in_=self.onesf[:], pattern=[[1, 128]],
                                                  compare_op=ALU.is_equal, fill=0.0, base=-127, channel_multiplier=1),
                ["onesf"], ["Jf"])
        self.memset(self.resetm[:], 1.0, ["resetm"])
        self.memset(self.resetm[:].rearrange("p (c t) -> p c t", t=64)[:, :, 0:1], 0.0, ["resetm"])
        self.memset(self.trim[:], 1.0, ["trim"])
        self.op("pool", lambda e: e.affine_select(out=self.trim[:], in_=self.trim[:], pattern=[[0, 4], [1, 128]],
                                                  compare_op=ALU.is_ge, fill=0.0, base=0, channel_multiplier=-1),
                ["trim"], ["trim"])
        self.memset(self.trim[0:64, :, 64:128], 0.0, ["trim"])
        for t in (self.ktokA, self.ktokB, self.qTA, self.qTB, self.PTh[0], self.PTh[1]):
            self.memset(t[:], 0.0, [t.name])
        for t in self.Va:
            self.memset(t[:], 1.0, [t.name])
        self.memset(self.Sst[:], 0.0, ["Sst"])
        self.tt(self.lbd[:], self.lbl[:, :, 0], self.lbl[:, :, 1], ALU.subtract, ["lbl_sb"], ["lbd"])
        self.act(self.lb[:], self.lbd[:], AF.Sigmoid, ["lbd"], ["lb"])
        self.act(self.oml[:], self.lbd[:], AF.Sigmoid, ["lbd"], ["oml"], scale=-1.0)
        self.act(self.scT[:], self.cT[:], AF.Silu, ["cT_sb"], ["scT"])
        bm = self.pa()
        for blk in range(12):
            w, wk = self.wld("w_ada", blk)
            for t4 in range(4):
                t = blk * 4 + t4
                for k in range(8):
                    self.mm(self.PB(bm)[:, t * 8:(t + 1) * 8], w[:, k, t4 * 128:(t4 + 1) * 128], self.scT[:, k, :],
                            k == 0, k == 7, [wk, "scT"], [("ps", bm)])
        self.tt(self.modT[:], self.PB(bm)[:, 0:384].rearrange("p (t s) -> p t s", s=8),
                self.b_ada[:].unsqueeze(2).to_broadcast([128, 48, 8]), ALU.add, [("ps", bm), "b_ada_sb"], ["modT"])
        self.pf(bm)
        self.stt(self.gm1[:], self.modT[:, 8:16, :], 1.0, self.nmix[:].unsqueeze(2).to_broadcast([128, 8, 8]),
                 ALU.add, ALU.mult, ["modT", "nmix_sb"], ["gm1"])
        self.stt(self.gm2[:], self.modT[:, 32:40, :], 1.0, self.nffn[:].unsqueeze(2).to_broadcast([128, 8, 8]),
                 ALU.add, ALU.mult, ["modT", "nffn_sb"], ["gm2"])
        self.memset(self.ext[:], 0.0, ["ext"])
        self.ts(self.ext[:, 64:256], self.rbt[:], self.rbt[:, 191:192], None, ALU.subtract, None, ["rbt"], ["ext"])
        self.ts(self.ext[:, 0:64], self.ext[:, 256:320], self.rbt[:, 0:1], self.rbt[:, 191:192], ALU.add, ALU.subtract,
                ["rbt", "ext"], ["ext"])
        self.dma("sp", self.ext_d.ap(), self.ext[:], ["ext"], ["ext_d"])
        hank = self.sqa4
        for which, off, nq, BT in ((3, 128, 128, self.BT3), (4, 0, 128, self.BT4), (5, 64, 64, self.BT5)):
            hv = self.arena.bitcast(F32)[:, 0:8 * nq].rearrange("p (h q) -> p h q", q=nq)
            src = bass.AP(self.ext_d, off, [[1, 128], [383, 8], [1, nq]])
            self.dma("sp", hv, src, ["ext_d"], [("arena", u) for u in range(8)])
            nb = (8 * nq) // 512
            for b2 in range(nb):
                bk = self.pa()
                hpb = 512 // nq
                for hh in range(hpb):
                    h = b2 * hpb + hh
                    self.mm(self.PB(bk)[:, hh * nq:(hh + 1) * nq], self.Jf[:], hv[:, h, :], True, True,
                            ["Jf"] + [("arena", u) for u in range(8)], [("ps", bk)])
                self.cp(BT[:, b2 * hpb:(b2 + 1) * hpb, :], self.PB(bk)[:, :].rearrange("p (h q) -> p h q", q=nq),
                        [("ps", bk)], [BT.name])
                self.pf(bk)
        self.memset(self.BT4[64:128, :, 0:64], NEGM, ["BT4"])
        self.memset(self.BT5[0:64, :, :], NEGM, ["BT5"])

    def _norm_stats(self, ncol):
        sq = self.mT
        b = self.pa()
        for k in range(8):
            self.act(sq[:, k, 0:ncol], self.xT[:, k, 0:ncol], AF.Square, [("xT_sb", k)], [("mT", k)])
            self.mm(self.PB(b)[:, 0:ncol], self.onesb[:], sq[:, k, 0:ncol], k == 0, k == 7, ["onesb", ("mT", k)], [("ps", b)])
        self.act(self.rstd[:, 0:ncol], self.PB(b)[:, 0:ncol], AF.Ln, [("ps", b), "epsc"], ["rstd"], scale=1.0 / D,
                 bias=self.epsc[:, 0:1])
        self.act(self.PB(b)[:, 0:ncol], self.rstd[:, 0:ncol], AF.Exp, ["rstd"], [("ps", b)], scale=-0.5)
        return b

    def _norm_apply(self, st, gm, sh_off, rb_):
        i = 0
        for k in range(8):
            for (c0, n, seq) in st["segs"]:
                sl = i % 2
                i += 1
                tb = self.t1 if sl == 0 else self.t2
                self.stt(tb[:, 0:n], self.xT[:, k, c0:c0 + n], gm[:, k, seq:seq + 1], self.PB(rb_)[:, c0:c0 + n],
                         ALU.mult, ALU.mult, [("xT_sb", k), ("ps", rb_), gm.name], [tb.name])
                self.act(self.hT[:, k, c0:c0 + n], tb[:, 0:n], AF.Identity, [tb.name, "modT"],
                         [("hT", k)], bias=self.modT[:, sh_off + k, seq:seq + 1])
        self.pf(rb_)

    def _supertile(self, st):
        dr = self.dr
        ncol = st["ncol"]
        kind = st["kind"]
        cur = st["cur"]
        ntt = ncol // 128
        nch = ncol // 64
        segs = st["segs"]
        xsrc = dr["xT"].rearrange("(k p) n -> p k n", p=128)[:, :, st["col0"]:st["col0"] + ncol]
        for k in range(8):
            self.dma("sp", self.xT[:, k, 0:ncol], xsrc[:, k, :], (), [("xT_sb", k)])
        self.chk(2)
        rb_ = self._norm_stats(ncol)
        self.chk(3)
        self._norm_apply(st, self.gm1, 0, rb_)
        self.chk(4)
        want_kv = kind == "sample" or (kind == "prompt" and st["last"])

        self._hgrn(st, kind == "halo")

        self.chk(6)
        if kind != "halo":
            w, wk = self.wld("w_in", 4)
            for hp in range(4):
                b = self.pa()
                for k in range(8):
                    self.mm(self.PB(b)[:, 0:ncol], w[:, k, hp * 128:(hp + 1) * 128], self.hT[:, k, 0:ncol], k == 0, k == 7,
                            [wk, ("hT", k)], [("ps", b)])
                self.act(self.qTA[0:64, hp, 0:ncol], self.PB(b)[0:64, 0:ncol], AF.Identity, [("ps", b)], [("qTA", hp)], scale=0.125)
                self.ts(self.qTB[64:128, hp, 0:ncol], self.PB(b)[64:128, 0:ncol], 0.125, None, ALU.mult, None,
                        [("ps", b)], [("qTB", hp)])
                self.pf(b)
        self.chk(6.1)
        w, wk = self.wld("w_in", 5)
        for hp in range(4):
            b = self.pa()
            for k in range(8):
                self.mm(self.PB(b)[:, 0:ncol], w[:, k, hp * 128:(hp + 1) * 128], self.hT[:, k, 0:ncol], k == 0, k == 7,
                        [wk, ("hT", k)], [("ps", b)])
            self.cp(self.kT[cur][:, hp, 0:ncol], self.PB(b)[:, 0:ncol], [("ps", b)], [(self.kT[cur].name, hp)], eng="act")
            if want_kv:
                self.cp(self.kTf[:, hp, 0:ncol], self.PB(b)[:, 0:ncol], [("ps", b)], [("arena", 2 * hp), ("arena", 2 * hp + 1)])
            self.pf(b)
        self.chk(6.2)
        w, wk = self.wld("w_in", 6)
        if kind != "halo":
            self.memset(self.Va[cur][:, :, :, 64:65], 1.0, [self.Va[cur].name])
        for tt_ in range(ntt):
            b = self.pa()
            for k in range(8):
                self.mm(self.PB(b)[:, 0:512], self.hT[:, k, tt_ * 128:(tt_ + 1) * 128], w[:, k, :], k == 0, k == 7,
                        [wk, ("hT", k)], [("ps", b)])
            self.cp(self.Va[cur][:, tt_, :, 0:64], self.PB(b)[:, :].rearrange("p (h d) -> p h d", d=64), [("ps", b)],
                    [(self.Va[cur].name, tt_)], eng="act")
            if want_kv:
                self.cp(self.vf[:, tt_, :], self.PB(b)[:, :], [("ps", b)], [("arena", 8 + 2 * tt_), ("arena", 9 + 2 * tt_)])
            self.pf(b)
        self.chk(6.4)
        if want_kv:
            ko, vo = ("kTs", "vs") if kind == "sample" else ("kTp", "vp")
            self.dma("sp", dr[ko], self.kTf[:, :, 0:ncol], [("arena", u) for u in range(8)], [ko])
            self.dma("sp", dr[vo], self.vf[:, 0:ntt, :], [("arena", u) for u in range(8, 16)], [vo])
        if kind == "halo":
            va = self.Va[cur]
            self.ts(va[:], va[:], self.flags[:, 0:1], None, ALU.mult, None, [va.name, "flags_sb"], [va.name])
            return

        self.chk(7)
        wout_pre = [self.wld_arena("w_out", cb, cb) for cb in range(2)]
        self._attention(st)
        self.chk(8)

        for cb in range(2):
            wbr, kbr = self.wld("w_br", cb)
            wga, kga = self.wld("w_in", 7 + cb)
            wgb, kgb = self.wld("w_in", 9 + cb)
            for m4 in range(4):
                m = cb * 4 + m4
                cs = slice(m4 * 128, (m4 + 1) * 128)
                bya, byb, bga, bgb = self.pa(), self.pa(), self.pa(), self.pa()
                for k in range(4):
                    self.mm(self.PB(bya)[:, 0:ncol], wbr[:, k, cs], self.oaT[:, k, 0:ncol], k == 0, k == 3,
                            [kbr, ("oaT", k)], [("ps", bya)])
                for k in range(4):
                    self.mm(self.PB(byb)[:, 0:ncol], wbr[:, 4 + k, cs], self.obT[:, k, 0:ncol], k == 0, k == 3,
                            [kbr, ("vtok", k)], [("ps", byb)])
                for k in range(8):
                    self.mm(self.PB(bga)[:, 0:ncol], wga[:, k, cs], self.hT[:, k, 0:ncol], k == 0, k == 7,
                            [kga, ("hT", k)], [("ps", bga)])
                for k in range(8):
                    self.mm(self.PB(bgb)[:, 0:ncol], wgb[:, k, cs], self.hT[:, k, 0:ncol], k == 0, k == 7,
                            [kgb, ("hT", k)], [("ps", bgb)])
                self.act(self.sa[:, 0:ncol], self.PB(bga)[:, 0:ncol], AF.Sigmoid, [("ps", bga)], ["sa"])
                self.act(self.sbg[:, 0:ncol], self.PB(bgb)[:, 0:ncol], AF.Sigmoid, [("ps", bgb)], ["sbg"])
                self.pf(bga)
                self.pf(bgb)
                self.tt(self.t1[:, 0:ncol], self.PB(bya)[:, 0:ncol], self.sa[:, 0:ncol], ALU.mult, [("ps", bya), "sa"], ["t1"])
                self.tt(self.t2[:, 0:ncol], self.PB(byb)[:, 0:ncol], self.sbg[:, 0:ncol], ALU.mult, [("ps", byb), "sbg"], ["t2"])
                self.pf(bya)
                self.pf(byb)
                self.tt(self.mT[:, m, 0:ncol], self.t1[:, 0:ncol], self.t2[:, 0:ncol], ALU.add, ["t1", "t2"], [("mT", m)])
        self.chk(9)
        for cb in range(2):
            w, wks = wout_pre[cb]
            for m4 in range(4):
                m = cb * 4 + m4
                b = self.pa()
                for k in range(8):
                    self.mm(self.PB(b)[:, 0:ncol], w[:, k, m4 * 128:(m4 + 1) * 128], self.mT[:, k, 0:ncol], k == 0, k == 7,
                            wks + [("mT", k)], [("ps", b)])
                for (c0, n, seq) in segs:
                    self.stt(self.xT[:, m, c0:c0 + n], self.PB(b)[:, c0:c0 + n], self.modT[:, 16 + m, seq:seq + 1],
                             self.xT[:, m, c0:c0 + n], ALU.mult, ALU.add, [("ps", b), "modT", ("xT_sb", m)], [("xT_sb", m)])
                self.pf(b)
        self.chk(10)
        rb_ = self._norm_stats(ncol)
        self._norm_apply(st, self.gm2, 24, rb_)
        for bi in range(6):
            width = 512 if bi < 5 else 256
            wa, ka = self.wld("w_fi", bi, width=width)
            wu, ku = self.wld("w_fi", 6 + bi, width=width)
            for j4 in range(width // 128):
                j = bi * 4 + j4
                cs = slice(j4 * 128, (j4 + 1) * 128)
                ba, bu = self.pa(), self.pa()
                for k in range(8):
                    self.mm(self.PB(ba)[:, 0:ncol], wa[:, k, cs], self.hT[:, k, 0:ncol], k == 0, k == 7,
                            [ka, ("hT", k)], [("ps", ba)])
                for k in range(8):
                    self.mm(self.PB(bu)[:, 0:ncol], wu[:, k, cs], self.hT[:, k, 0:ncol], k == 0, k == 7,
                            [ku, ("hT", k)], [("ps", bu)])
                sl = j % 2
                tsl = self.t1 if sl == 0 else self.t2
                self.act(tsl[:, 0:ncol], self.PB(ba)[:, 0:ncol], AF.Silu, [("ps", ba)], [tsl.name])
                self.pf(ba)
                self.tt(self.sT[:, j, 0:ncol], self.PB(bu)[:, 0:ncol], tsl[:, 0:ncol], ALU.mult, [("ps", bu), tsl.name],
                        [("arena", j)])
                self.pf(bu)
        for cb in range(2):
            banks = [self.pa() for _ in range(4)]
            for (k0, kn) in ((0, 8), (8, 8), (16, 6)):
                w, wk = self.wld("w_fo", cb, k0=k0, kn=kn)
                for m4 in range(4):
                    for kk in range(kn):
                        kg = k0 + kk
                        self.mm(self.PB(banks[m4])[:, 0:ncol], w[:, kk, m4 * 128:(m4 + 1) * 128], self.sT[:, kg, 0:ncol],
                                kg == 0, kg == 21, [wk, ("arena", kg)], [("ps", banks[m4])])
            for m4 in range(4):
                m = cb * 4 + m4
                b = banks[m4]
                for (c0, n, seq) in segs:
                    self.stt(self.xT[:, m, c0:c0 + n], self.PB(b)[:, c0:c0 + n], self.modT[:, 40 + m, seq:seq + 1],
                             self.xT[:, m, c0:c0 + n], ALU.mult, ALU.add, [("ps", b), "modT", ("xT_sb", m)], [("xT_sb", m)])
                self.pf(b)
        self.chk(11)
        rb_ = self._norm_stats(ncol)
        ydst = dr["yT"].rearrange("(k p) n -> p k n", p=128)[:, :, st["ocol"]:st["ocol"] + ncol]
        for k in range(8):
            self.stt(self.xT[:, k, 0:ncol], self.xT[:, k, 0:ncol], self.nfin[:, k:k + 1], self.PB(rb_)[:, 0:ncol],
                     ALU.mult, ALU.mult, [("xT_sb", k), ("ps", rb_), "nfin_sb"], [("xT_sb", k)])
            self.dma("sp", ydst[:, k, :], self.xT[:, k, 0:ncol], [("xT_sb", k)], [("yT", k)])
        self.pf(rb_)

    def _hgrn(self, st, so=False):
        ncol = st["ncol"]
        kind = st["kind"]
        ntt = ncol // 128
        nch = ncol // 64
        npair = ncol // 128
        wblk = {}
        for blk in range(3):
            wblk[blk] = self.wld("w_in", blk)
        for h in range(4):
            for which in ((1,) if so else (0, 1, 2)):
                idx = 3 * h + which
                w, wk = wblk[idx // 4]
                cs = slice((idx % 4) * 128, (idx % 4 + 1) * 128)
                b = self.pa()
                for k in range(8):
                    self.mm(self.PB(b)[:, 0:ncol], w[:, k, cs], self.hT[:, k, 0:ncol], k == 0, k == 7,
                            [wk, ("hT", k)], [("ps", b)])
                if which == 0:
                    self.act(self.sqa4[:, h, 0:ncol], self.PB(b)[:, 0:ncol], AF.Silu, [("ps", b)],
                             [("arena", 2 * h), ("arena", 2 * h + 1)])
                elif which == 1:
                    self.act(self.fT4[:, h, 0:ncol], self.PB(b)[:, 0:ncol], AF.Sigmoid, [("ps", b)],
                             [("arena", 8 + 2 * h), ("arena", 9 + 2 * h)])
                else:
                    self.act(self.sga[:, h, 0:ncol], self.PB(b)[:, 0:ncol], AF.Silu, [("ps", b)], [("sga", h)])
                self.pf(b)
        w, wk = self.wld("w_in", 3)
        for tt_ in range(ntt):
            b = self.pa()
            for k in range(8):
                self.mm(self.PB(b)[:, 0:512], self.hT[:, k, tt_ * 128:(tt_ + 1) * 128], w[:, k, :], k == 0, k == 7,
                        [wk, ("hT", k)], [("ps", b)])
            self.cp(self.vtok[:, tt_, :], self.PB(b)[:, :], [("ps", b)], [("vtok", tt_)])
            self.pf(b)
        gens = [self._hg_pre(st, so, h) for h in range(4)]
        while gens:
            for g in list(gens):
                try:
                    next(g)
                except StopIteration:
                    gens.remove(g)
        bO = [None] * 4 if so else [self.pa() for _ in range(4)]
        for c in range(nch):
            p, half = c // 2, c % 2
            bKV = self.pa()
            kt = self.ktokA if half == 0 else self.ktokB
            for h in range(4):
                hs = slice(h * 128, (h + 1) * 128)
                self.mm(self.PB(bKV)[:, hs], kt[:, h, p, :], self.vtok[:, p, hs], True, True,
                        [(kt.name, h), ("vtok", p)], [("ps", bKV)])
            for h in range(4):
                hs = slice(h * 128, (h + 1) * 128)
                if kind == "sample":
                    Sap = self.sin[:, c, h, :]
                    skey = ("sin_sb", c * 4 + h)
                else:
                    Sap = self.Sst[:, h, :]
                    skey = ("Sst", h)
                sl = h * 2 + c % 2
                if not so:
                    self.act(self.Ssc[:, sl, :], Sap, AF.Identity, [skey, ("eref", h)], [("Ssc", sl)], scale=self.eref[:, h, c:c + 1])
                    if half == 0:
                        ps_ = slice(p * 128, (p + 1) * 128)
                        self.mm(self.PB(bO[h])[:, ps_], self.vtok[:, p, hs], self.AT[:, h, ps_], True, False,
                                [("vtok", p), ("AT", h)], [("ps", bO[h])])
                    cs = slice(c * 64, (c + 1) * 64)
                    self.mm(self.PB(bO[h])[:, cs], self.Ssc[:, sl, :], self.qpT[:, h, cs], False, half == 1,
                            [("Ssc", sl), ("qpT", h)], [("ps", bO[h])])
                self.act(self.tmpS[:, sl, :], Sap, AF.Identity, [skey, ("etot", h)], [("tmpS", sl)], scale=self.etot[:, h, c:c + 1])
                self.stt(Sap, self.PB(bKV)[:, hs], self.etr[:, h, c:c + 1], self.tmpS[:, sl, :], ALU.mult, ALU.add,
                         [("ps", bKV), ("etr", h), ("tmpS", sl)], [skey])
            self.pf(bKV)
        if not so:
            gens = [self._hg_post(st, h, bO[h]) for h in range(4)]
            while gens:
                for g in list(gens):
                    try:
                        next(g)
                    except StopIteration:
                        gens.remove(g)
        if so:
            self.ts(self.Sst[:], self.Sst[:], self.flags[:, 0:1], None, ALU.mult, None, ["Sst", "flags_sb"], ["Sst"])
        if kind == "sample":
            self.dma("sp", self.dr["s_out"][:, 0:4, :, :], self.sin[:], ["sin_sb"], ["s_out"])

    def _hg_pre(self, st, so, h):
        ncol = st["ncol"]
        kind = st["kind"]
        nch = ncol // 64
        npair = ncol // 128
        lgf, cum, e1 = self.hset[h]
        lk, ck, ek = self.hkeys[h]
        dd = lgf
        e2 = lgf
        fk = [("arena", 8 + 2 * h), ("arena", 9 + 2 * h)]
        qk = [("arena", 2 * h), ("arena", 2 * h + 1)]
        fT = self.fT4[:, h, 0:ncol]
        self.ts(fT, fT, self.oml[:, h:h + 1], self.lb[:, h:h + 1], ALU.mult, ALU.add, fk + ["oml", "lb"], fk)
        self.act(lgf[:, 0:ncol], fT, AF.Ln, fk, lk)
        yield
        self.op("dve", lambda e: e.tensor_tensor_scan(out=cum[:, 0:ncol], data0=self.resetm[:, 0:ncol],
                                                      data1=lgf[:, 0:ncol], initial=0.0, op0=ALU.mult, op1=ALU.add),
                ["resetm"] + lk, ck)
        self.act(fT, fT, AF.Identity, fk + ["onesf"], fk, scale=-1.0, bias=self.onesf[:, 0:1])
        yield
        cum3 = cum[:, 0:ncol].rearrange("p (c t) -> p c t", t=64)
        self.tt(dd[:, 0:ncol].rearrange("p (c t) -> p c t", t=64), cum3, cum3[:, :, 32:33].to_broadcast([128, nch, 64]),
                ALU.subtract, ck, lk)
        if not so:
            self.act(e1[:, 0:ncol], dd[:, 0:ncol], AF.Exp, lk, ek)
        self.act(e2[:, 0:ncol], dd[:, 0:ncol], AF.Exp, lk, lk, scale=-1.0)
        yield
        if not so:
            self.tt(self.qpT[:, h, 0:ncol], self.sqa4[:, h, 0:ncol], e1[:, 0:ncol], ALU.mult, qk + ek, [("qpT", h)])
        self.tt(self.kpT[:, h, 0:ncol], fT, e2[:, 0:ncol], ALU.mult, fk + lk, [("kpT", h)])
        if not so:
            self.act(self.eref[:, h, 0:nch], cum3[:, :, 32], AF.Exp, ck, [("eref", h)])
        self.act(self.etot[:, h, 0:nch], cum3[:, :, 63], AF.Exp, ck, [("etot", h)])
        self.tt(self.etr[:, h, 0:nch], cum3[:, :, 63], cum3[:, :, 32], ALU.subtract, ck, [("etr", h)])
        self.act(self.etr[:, h, 0:nch], self.etr[:, h, 0:nch], AF.Exp, [("etr", h)], [("etr", h)])
        yield
        if not so:
            bA = self.pa()
            for p in range(npair):
                ps_ = slice(p * 128, (p + 1) * 128)
                self.mm(self.PB(bA)[:, ps_], self.kpT[:, h, ps_], self.qpT[:, h, ps_], True, True,
                        [("kpT", h), ("qpT", h)], [("ps", bA)])
            self.tt(self.AT[:, h, 0:ncol], self.PB(bA)[:, 0:ncol], self.trim[:, 0:npair, :].rearrange("p a b -> p (a b)"), ALU.mult,
                    [("ps", bA), "trim"], [("AT", h)])
            self.pf(bA)
        bT = self.pa()
        for p in range(npair):
            ps_ = slice(p * 128, (p + 1) * 128)
            self.mm(self.PB(bT)[:, ps_], self.kpT[:, h, ps_], self.identb[:], True, True,
                    [("kpT", h), "identb"], [("ps", bT)])
        bT3 = self.PB(bT)[:, 0:ncol].rearrange("p (a b) -> p a b", b=128)
        self.cp(self.ktokA[0:64, h, 0:npair, :], bT3[0:64, :, :], [("ps", bT)], [("ktokA", h)], eng="act")
        self.cp(self.ktokB[64:128, h, 0:npair, :], bT3[64:128, :, :], [("ps", bT)], [("ktokB", h)])
        self.pf(bT)
        yield

    def _hg_post(self, st, h, bO):
        ncol = st["ncol"]
        e1 = self.hset[h][2]
        ek = self.hkeys[h][2]
        sqo = self.sqo[:, h, :]
        sk = ("sqo", h)
        tb = self.hset[h][0]
        tk = self.hkeys[h][0]
        self.act(sqo[:, 0:ncol], self.PB(bO)[:, 0:ncol], AF.Square, [("ps", bO)], [sk])
        bS = self.pa()
        self.mm(self.PB(bS)[:, 0:ncol], self.onesb[:], sqo[:, 0:ncol], True, True, ["onesb", sk], [("ps", bS)])
        self.act(e1[:, 0:ncol], self.PB(bS)[:, 0:ncol], AF.Ln, [("ps", bS), "epsc"], ek, scale=1.0 / 128,
                 bias=self.epsc[:, 0:1])
        self.pf(bS)
        yield
        self.act(e1[:, 0:ncol], e1[:, 0:ncol], AF.Exp, ek, ek, scale=-0.5)
        self.tt(tb[:, 0:ncol], self.PB(bO)[:, 0:ncol], e1[:, 0:ncol], ALU.mult, [("ps", bO)] + ek, tk)
        self.pf(bO)
        yield
        self.stt(self.oaT[:, h, 0:ncol], tb[:, 0:ncol], self.onorm[:, h:h + 1], self.sga[:, h, 0:ncol], ALU.mult, ALU.mult,
                 tk + ["onorm_sb", ("sga", h)], [("oaT", h)])

    def _attn_unit(self, q0, nq, qcol, tiles, PTs, ob_tt):
        nt = len(tiles)
        plain = [i for i, t in enumerate(tiles) if t["bias"] is None]
        biased = [i for i, t in enumerate(tiles) if t["bias"] is not None]
        assert len(plain) <= 3 and len(biased) == 2
        npl = len(plain)
        assert plain == list(range(plain[0], plain[0] + npl)) and biased == [biased[0], biased[0] + 1]
        bO = [self.pa(), self.pa()]
        pipelined = isinstance(PTs, list)

        def stageA(h):
            hp = h // 2
            qsel = self.qTA if h % 2 == 0 else self.qTB
            qk = (qsel.name, hp)
            PT = PTs[h % 2] if pipelined else PTs
            b1 = self.pa()
            for n_, i in enumerate(plain):
                kb, kt = tiles[i]["k"]
                self.mm(self.PB(b1)[:, n_ * nq:(n_ + 1) * nq], self.kT[kb][:, hp, kt * 128:(kt + 1) * 128],
                        qsel[:, hp, qcol:qcol + nq], True, True, [(self.kT[kb].name, hp), qk], [("ps", b1)])
            b2 = self.pa()
            for n_, i in enumerate(biased):
                kb, kt = tiles[i]["k"]
                self.mm(self.PB(b2)[:, n_ * nq:(n_ + 1) * nq], self.kT[kb][:, hp, kt * 128:(kt + 1) * 128],
                        qsel[:, hp, qcol:qcol + nq], True, True, [(self.kT[kb].name, hp), qk], [("ps", b2)])
            self.act(PT[:, plain[0]:plain[0] + npl, q0:q0 + nq], self.PB(b1)[:, 0:npl * nq].rearrange("p (a b) -> p a b", b=nq),
                     AF.Exp, [("ps", b1)], [(PT.name, 0)])
            for i in plain:
                if tiles[i]["zero"]:
                    self.memset(PT[0:64, i, 64:128], 0.0, [(PT.name, 0)])
            self.pf(b1)
            sl = h % 2
            for n_, i in enumerate(biased):
                self.tt(self.sbias[:, sl, n_ * nq:(n_ + 1) * nq], self.PB(b2)[:, n_ * nq:(n_ + 1) * nq], tiles[i]["bias"](h), ALU.add,
                        [("ps", b2), "BT3", "BT4", "BT5"], [("sbias", sl)])
            self.pf(b2)
            self.act(PT[:, biased[0]:biased[0] + 2, q0:q0 + nq], self.sbias[:, sl, 0:2 * nq].rearrange("p (a b) -> p a b", b=nq),
                     AF.Exp, [("sbias", sl)], [(PT.name, 0)])

        def stageB(h):
            PT = PTs[h % 2] if pipelined else PTs
            ob_ = bO[h // 4]
            oc = (h % 4) * 65
            for i, t in enumerate(tiles):
                vb, vt = t["k"]
                self.mm(self.PB(ob_)[:, oc:oc + 65], PT[:, i, :], self.Va[vb][:, vt, h, :], i == 0, i == nt - 1,
                        [(PT.name, 0), (self.Va[vb].name, vt)], [("ps", ob_)])

        if pipelined:
            stageA(0)
            for h in range(8):
                if h + 1 < 8:
                    stageA(h + 1)
                stageB(h)
        else:
            for h in range(8):
                stageA(h)
                stageB(h)
        rows = slice(q0, q0 + nq)
        for g in range(2):
            o3 = self.PB(bO[g])[rows, 0:260].rearrange("p (h d) -> p h d", d=65)
            self.op("dve", lambda e, o3=o3, g=g: e.reciprocal(out=self.rden[rows, 4 * g:4 * g + 4], in_=o3[:, :, 64]),
                    [("ps", bO[g])], [("rden", g)])
            self.tt(self.ob[rows, ob_tt, 256 * g:256 * (g + 1)].rearrange("p (h d) -> p h d", d=64), o3[:, :, 0:64],
                    self.rden[rows, 4 * g:4 * g + 4].unsqueeze(2).to_broadcast([nq, 4, 64]), ALU.mult,
                    [("ps", bO[g]), ("rden", g)], [("AT", ob_tt)])
            self.pf(bO[g])

    def _attention(self, st):
        ncol = st["ncol"]
        kind = st["kind"]
        cur = st["cur"]
        prev = 1 - cur
        npair = ncol // 128
        if kind == "prompt":
            for p in range(npair):
                tiles = []
                for j in range(5):
                    kk = (prev, p + j) if p + j <= 3 else (cur, p + j - 4)
                    if j == 3:
                        bias = (lambda h: self.BT3[:, h, :])
                    elif j == 4:
                        bias = (lambda h: self.BT4[:, h, :])
                    else:
                        bias = None
                    tiles.append(dict(k=kk, bias=bias, zero=(j == 0)))
                self._attn_unit(0, 128, p * 128, tiles, self.PTf, p)
        else:
            for i in range(4):
                pr, half = i // 2, i % 2
                kdst, vdst = self.kT[prev], self.Va[prev]
                self.dma("pool", kdst[:], self.dr["ckT"][i], (), [kdst.name])
                self.dma("pool", vdst[:, :, :, 0:64], self.dr["cv"][i].rearrange("p t (h d) -> p t h d", d=64), (), [vdst.name])
                tiles = []
                for j in range(4):
                    bias = (lambda h: self.BT3[:, h, 0:64]) if j == 3 else None
                    tiles.append(dict(k=(prev, j), bias=bias, zero=False))
                own_bias = (lambda h: self.BT4[:, h, 0:64]) if half == 0 else (lambda h: self.BT5[:, h, 0:64])
                tiles.append(dict(k=(cur, pr), bias=own_bias, zero=False))
                self._attn_unit(half * 64, 64, i * 64, tiles, self.PTh[half], pr)
        for p in range(npair):
            b = self.pa()
            for vt in range(4):
                self.mm(self.PB(b)[:, vt * 128:(vt + 1) * 128], self.ob[:, p, vt * 128:(vt + 1) * 128], self.identb[:], True, True,
                        [("AT", p), "identb"], [("ps", b)])
            self.cp(self.obT[:, :, p * 128:(p + 1) * 128], self.PB(b)[:, :].rearrange("p (a b) -> p a b", b=128), [("ps", b)],
                    [("vtok", k) for k in range(4)], eng="act" if p % 2 else "dve")
            self.pf(b)

    def _finish(self):
        self.dma("sp", self.dr["s_out"][:, 4, :, :], self.Sst[:], ["Sst"], ["s_out4"])


def _blk(w, ncols_list=None):
    K, N = w.shape
    nb = N // 512
    return np.ascontiguousarray(w.reshape(K // 128, 128, nb, 512).transpose(2, 1, 0, 3))


_WIN_TILES = None


def _win_perm():
    tiles = []
    for h in range(4):
        tiles += [0 + h, 4 + h, 12 + h]
    tiles += [8, 9, 10, 11]
    tiles += list(range(16, 44))
    cols = np.concatenate([np.arange(t * 128, (t + 1) * 128) for t in tiles])
    return cols


_PROG = {}


def _get_prog(NS):
    if NS not in _PROG:
        b = Builder(NS)
        nc = b.build()
        _PROG[NS] = (nc, b)
    return _PROG[NS]


def make_in_maps(x_prompt, x_sample, c_prompt, c_sample, state_hgrn, cache_k, cache_v, w_ada, b_ada, norm_mix, w_in,
                 hgrn_lb_logits, hgrn_out_norm, w_branch_a, rel_bias, w_branch_b, w_out, norm_ffn, w_ffn_in, w_ffn_out,
                 norm_final):
    f32 = np.float32
    x_prompt = np.asarray(x_prompt, f32)
    x_sample = np.asarray(x_sample, f32)
    B, L, _ = x_prompt.shape
    SEG = L // 4
    NS = SEG // 512
    assert B == 2 and SEG % 512 == 0 and x_sample.shape[0] == 32 and x_sample.shape[1] == 64
    w_in_p = np.asarray(w_in[0], f32)[:, _win_perm()]
    wfi = np.asarray(w_ffn_in[0], f32)
    DFF = wfi.shape[1] // 2
    wfi_p = np.zeros((1024, 12 * 512), f32)
    wfi_p[:, 0:DFF] = wfi[:, 0:DFF]
    wfi_p[:, 6 * 512:6 * 512 + DFF] = wfi[:, DFF:]
    w_br = np.concatenate([np.asarray(w_branch_a[0], f32), np.asarray(w_branch_b[0], f32)], axis=0)
    shared = {
        "w_ada": _blk(np.asarray(w_ada[0], f32)),
        "w_in": _blk(w_in_p),
        "w_br": _blk(w_br),
        "w_out": _blk(np.asarray(w_out[0], f32)),
        "w_fi": _blk(wfi_p),
        "w_fo": _blk(np.asarray(w_ffn_out[0], f32)),
        "b_ada": np.ascontiguousarray(np.asarray(b_ada[0], f32).reshape(48, 128).T),
        "nmix": np.ascontiguousarray(np.asarray(norm_mix[0], f32).reshape(8, 128).T),
        "nffn": np.ascontiguousarray(np.asarray(norm_ffn[0], f32).reshape(8, 128).T),
        "nfin": np.ascontiguousarray(np.asarray(norm_final, f32).reshape(8, 128).T),
        "onorm": np.ascontiguousarray(np.asarray(hgrn_out_norm[0], f32).reshape(4, 128).T),
        "lbl": np.ascontiguousarray(np.asarray(hgrn_lb_logits, f32)[0:2].reshape(2, 4, 128).transpose(2, 1, 0)),
        "rb": np.ascontiguousarray(np.asarray(rel_bias[0], f32)),
    }
    c_prompt = np.asarray(c_prompt, f32)
    c_sample = np.asarray(c_sample, f32)
    state_hgrn = np.asarray(state_hgrn, f32)
    cache_k = np.asarray(cache_k, f32)
    cache_v = np.asarray(cache_v, f32)
    in_maps = []
    for r in range(NCORES):
        b, kseg = r // 4, r % 4
        xs = x_sample[4 * r:4 * r + 4].reshape(256, D)
        t0 = kseg * SEG
        if kseg == 0:
            xh = np.zeros((512, D), f32)
        else:
            xh = x_prompt[b, t0 - 512:t0]
        xp = x_prompt[b, t0:t0 + SEG]
        xT = np.ascontiguousarray(np.concatenate([xs, xh, xp], axis=0).T)
        cs = np.zeros((8, D), f32)
        cs[0] = c_prompt[b]
        cs[1:5] = c_sample[4 * r:4 * r + 4]
        cT = np.ascontiguousarray(cs.reshape(8, 8, 128).transpose(2, 1, 0))
        flags = np.zeros((128, 4), f32)
        flags[:, 0] = 0.0 if kseg == 0 else 1.0
        sin = np.ascontiguousarray(state_hgrn[0, 4 * r:4 * r + 4].transpose(2, 0, 1, 3))
        ck = cache_k[0, 4 * r:4 * r + 4]
        ckT = np.ascontiguousarray(ck.reshape(4, 512, 4, 128).transpose(0, 3, 2, 1))
        cv = np.ascontiguousarray(cache_v[0, 4 * r:4 * r + 4].reshape(4, 4, 128, 512).transpose(0, 2, 1, 3))
        m = dict(shared)
        m.update({"xT": xT, "cT": cT, "flags": flags, "sin": sin, "ckT": ckT, "cv": cv})
        in_maps.append(m)
    return NS, in_maps


def kernel(**inputs):
    f32 = np.float32
    NS, in_maps = make_in_maps(**inputs)
    L = NS * 512 * 4
    SEG = L // 4
    nc, bld = _get_prog(NS)
    res = run_bass_kernel_spmd(nc, in_maps, core_ids=list(range(NCORES)))
    return assemble(res.results, L)


def assemble(R, L):
    f32 = np.float32
    SEG = L // 4
    y_prompt = np.zeros((2, L, D), f32)
    y_sample = np.zeros((32, 64, D), f32)
    s_p = np.zeros((1, 2, 4, 128, 128), f32)
    k_p = np.zeros((1, 2, 512, 8, 64), f32)
    v_p = np.zeros((1, 2, 512, 8, 64), f32)
    s_s = np.zeros((1, 32, 4, 128, 128), f32)
    k_s = np.zeros((1, 32, 64, 8, 64), f32)
    v_s = np.zeros((1, 32, 64, 8, 64), f32)
    for r in range(NCORES):
        b, kseg = r // 4, r % 4
        o = R[r]
        yT = np.asarray(o["yT"])
        y_sample[4 * r:4 * r + 4] = yT[:, 0:256].T.reshape(4, 64, D)
        y_prompt[b, kseg * SEG:(kseg + 1) * SEG] = yT[:, 256:].T
        so = np.asarray(o["s_out"])
        s_s[0, 4 * r:4 * r + 4] = so[:, 0:4].transpose(1, 2, 0, 3)
        kTs = np.asarray(o["kTs"])
        k_s[0, 4 * r:4 * r + 4] = kTs.transpose(2, 1, 0).reshape(4, 64, 8, 64)
        vs = np.asarray(o["vs"])
        v_s[0, 4 * r:4 * r + 4] = vs.transpose(1, 0, 2).reshape(4, 64, 8, 64)
        if kseg == 3:
            s_p[0, b] = so[:, 4].transpose(1, 0, 2)
            k_p[0, b] = np.asarray(o["kTp"]).transpose(2, 1, 0).reshape(512, 8, 64)
            v_p[0, b] = np.asarray(o["vp"]).transpose(1, 0, 2).reshape(512, 8, 64)
    return (y_prompt, y_sample, s_p, k_p, v_p, s_s, k_s, v_s)
```

```python
import numpy as np
from contextlib import ExitStack
import concourse.bass as bass
import concourse.mybir as mybir
from concourse.bass_utils import run_bass_kernel_spmd

F32 = mybir.dt.float32
BF16 = mybir.dt.bfloat16
AF = mybir.ActivationFunctionType
ALU = mybir.AluOpType

D = 1024
NCORES = 8
EPS = 1e-6
NEGM = -30000.0

ENGS = ("pe", "act", "dve", "pool", "sp")
NDMA_SEM = 16
SAME_ENGINE_SYNC = True
PS_EXCL = True
DBG_SETS = False
KSTOP = 1e9


class Op:
    __slots__ = ("eng", "fn", "dma", "idx", "deps", "waited", "sem", "val", "dq")

    def __init__(self, eng, fn, dma):
        self.eng = eng
        self.fn = fn
        self.dma = dma
        self.deps = []
        self.waited = False
        self.sem = None
        self.val = 0
        self.dq = -1


class Sched:
    def __init__(self, nc):
        self.nc = nc
        self.ops = []
        self.by_eng = {e: [] for e in ENGS}
        self.state = {}

    def _ents(self, name, sub):
        d = self.state.setdefault(name, {})
        if sub is None:
            if None not in d:
                d[None] = [None, []]
            return list(d.values())
        if sub not in d:
            w = d[None][0] if None in d else None
            d[sub] = [w, []]
        ents = [d[sub]]
        if None in d:
            ents.append(d[None])
        return ents

    @staticmethod
    def _k(key):
        return key if isinstance(key, tuple) else (key, None)

    def add(self, eng, fn, reads=(), writes=(), dma=False):
        op = Op(eng, fn, dma)
        op.idx = len(self.ops)
        deps = {}
        if PS_EXCL and eng != "pe":
            writes = list(writes) + [k_ for k_ in reads if isinstance(k_, tuple) and k_[0] == "ps"]
        for key in reads:
            name, sub = self._k(key)
            for ent in self._ents(name, sub):
                if ent[0] is not None:
                    deps[ent[0].idx] = ent[0]
        for key in writes:
            name, sub = self._k(key)
            for ent in self._ents(name, sub):
                if ent[0] is not None:
                    deps[ent[0].idx] = ent[0]
                for r in ent[1]:
                    deps[r.idx] = r
        for key in reads:
            name, sub = self._k(key)
            d = self.state[name]
            if sub is None:
                for ent in d.values():
                    ent[1].append(op)
            else:
                d[sub][1].append(op)
        for key in writes:
            name, sub = self._k(key)
            d = self.state[name]
            if sub is None:
                for ent in d.values():
                    ent[0] = op
                    ent[1] = []
            else:
                d[sub][0] = op
                d[sub][1] = []
        deps.pop(op.idx, None)
        op.deps = [deps[i] for i in sorted(deps)]
        self.by_eng[eng].append(op)
        self.ops.append(op)
        return op

    @staticmethod
    def _need_wait(op, d):
        if d.dma:
            return True
        if d.eng == op.eng:
            if op.eng == "pe":
                return False
            if op.dma:
                return True
            return SAME_ENGINE_SYNC
        return True

    def emit(self, sems, dma_sems):
        nc = self.nc
        for op in self.ops:
            for d in op.deps:
                if self._need_wait(op, d):
                    d.waited = True
        final_waits = []
        for e in ENGS:
            cnt = 0
            dcnt = 0
            for op in self.by_eng[e]:
                if op.dma:
                    op.dq = dcnt
                    op.sem = dma_sems[e][dcnt % NDMA_SEM]
                    op.val = 16 * (dcnt // NDMA_SEM + 1)
                    dcnt += 1
                elif op.waited:
                    cnt += 1
                    op.sem = sems[e]
                    op.val = cnt
            for i in range(min(dcnt, NDMA_SEM)):
                uses = (dcnt - 1 - i) // NDMA_SEM + 1
                final_waits.append((dma_sems[e][i], 16 * uses))
        stats = {}

        def run(e):
            def body(eng):
                seen = {}
                nw = 0
                for op in self.by_eng[e]:
                    waits = []
                    if op.dma and op.dq >= NDMA_SEM:
                        waits.append((op.sem, op.val - 16))
                    for d in op.deps:
                        if self._need_wait(op, d):
                            waits.append((d.sem, d.val))
                    best = {}
                    for s, v in waits:
                        kk = id(s)
                        if seen.get(kk, 0) >= v:
                            continue
                        if kk not in best or best[kk][1] < v:
                            best[kk] = (s, v)
                    for kk, (s, v) in best.items():
                        eng.wait_ge(s, v)
                        seen[kk] = v
                        nw += 1
                    ins = op.fn(eng)
                    if op.dma:
                        ins.then_inc(op.sem, 16)
                    elif op.waited:
                        ins.then_inc(op.sem, 1)
                if e == "sp":
                    for s, v in final_waits:
                        eng.wait_ge(s, v)
                stats[e] = (len(self.by_eng[e]), nw)
            return body

        with nc.Block() as block:
            block.tensor(run("pe"))
            block.scalar(run("act"))
            block.vector(run("dve"))
            block.gpsimd(run("pool"))
            block.sync(run("sp"))
        return stats


class _Stop(Exception):
    pass


class Builder:
    def chk(self, n):
        if n > KSTOP:
            raise _Stop()

    def __init__(self, NS):
        self.NS = NS
        self.NP = NS * 512
        self.NTOK = 256 + 512 + self.NP
        self.NOUT = 256 + self.NP
        self.nc = bass.Bass("TRN2", target_bir_lowering=False)
        self.es = ExitStack()

    def sb(self, name, shape, dt):
        return self.es.enter_context(self.nc.sbuf_tensor(name, shape, dt))

    def din(self, name, shape):
        return self.nc.dram_tensor(name, shape, F32, kind="ExternalInput").ap()

    def dout(self, name, shape):
        return self.nc.dram_tensor(name, shape, F32, kind="ExternalOutput").ap()

    def op(self, eng, fn, r=(), w=(), dma=False):
        return self.S.add(eng, fn, reads=r, writes=w, dma=dma)

    def mm(self, out, lhsT, rhs, start, stop, r, w):
        self.op("pe", lambda e: e.matmul(out, lhsT=lhsT, rhs=rhs, start=start, stop=stop), r, w)

    def act(self, out, in_, func, r, w, scale=1.0, bias=None):
        if bias is None:
            self.op("act", lambda e: e.activation(out=out, in_=in_, func=func, scale=scale), r, w)
        else:
            self.op("act", lambda e: e.activation(out=out, in_=in_, func=func, scale=scale, bias=bias), r, w)

    def tt(self, out, in0, in1, op, r, w, eng="dve"):
        self.op(eng, lambda e: e.tensor_tensor(out=out, in0=in0, in1=in1, op=op), r, w)

    def ts(self, out, in0, s1, s2, op0, op1, r, w, eng="dve"):
        if s2 is None:
            self.op(eng, lambda e: e.tensor_scalar(out=out, in0=in0, scalar1=s1, scalar2=None, op0=op0), r, w)
        else:
            self.op(eng, lambda e: e.tensor_scalar(out=out, in0=in0, scalar1=s1, scalar2=s2, op0=op0, op1=op1), r, w)

    def stt(self, out, in0, scalar, in1, op0, op1, r, w):
        self.op("dve", lambda e: e.scalar_tensor_tensor(out=out, in0=in0, scalar=scalar, in1=in1, op0=op0, op1=op1), r, w)

    def cp(self, out, in_, r, w, eng="dve"):
        if eng == "act":
            self.op("act", lambda e: e.activation(out=out, in_=in_, func=AF.Copy), r, w)
        else:
            self.op(eng, lambda e: e.tensor_copy(out=out, in_=in_), r, w)

    def memset(self, ap, val, w, eng="pool"):
        self.op(eng, lambda e: e.memset(ap, val), (), w)

    def dma(self, eng, out, in_, r, w):
        self.op(eng, lambda e: e.dma_start(out=out, in_=in_, max_dma_last_dim=2048), r, w, dma=True)

    def pa(self):
        assert self.pfree, "out of PSUM banks"
        return self.pfree.pop(0)

    def pf(self, b):
        self.pfree.append(b)

    def PB(self, b):
        return self.ps[b]

    def wld(self, name, blk, k0=0, kn=8, width=512):
        slot = self.wslot
        self.wslot = (self.wslot + 1) % len(self.wb)
        t = self.wb[slot]
        src = self.dr[name]
        key = ("wb", slot)
        self.dma("pool", t[:, 0:kn, 0:width], src[blk, :, k0:k0 + kn, 0:width], (), [key])
        return t, key

    def build(self):
        nc = self.nc
        with self.es:
            self._declare()
            self.S = Sched(nc)
            self._setup()
            sts = []
            sts.append(dict(kind="sample", col0=0, ncol=256, ocol=0,
                            segs=[(i * 64, 64, 1 + i) for i in range(4)], cur=0))
            sts.append(dict(kind="halo", col0=256, ncol=512, segs=[(0, 512, 0)], cur=1))
            for s in range(self.NS):
                sts.append(dict(kind="prompt", col0=768 + 512 * s, ncol=512, ocol=256 + 512 * s,
                                segs=[(0, 512, 0)], cur=s % 2, first=(s == 0), last=(s == self.NS - 1)))
            try:
                self.chk(1)
                for st in sts:
                    self._supertile(st)
                self._finish()
            except _Stop:
                pass
            sems = {e: self.es.enter_context(nc.semaphore("s_" + e)) for e in ENGS}
            dsems = {e: [self.es.enter_context(nc.semaphore("d_%s%d" % (e, i))) for i in range(NDMA_SEM)]
                     for e in ENGS}
            self.stats = self.S.emit(sems, dsems)
        return nc

    def _declare(self):
        nc = self.nc
        dr = {}
        dr["xT"] = self.din("xT", [D, self.NTOK])
        dr["cT"] = self.din("cT", [128, 8, 8])
        dr["flags"] = self.din("flags", [128, 4])
        dr["sin"] = self.din("sin", [128, 4, 4, 128])
        dr["ckT"] = self.din("ckT", [4, 128, 4, 512])
        dr["cv"] = self.din("cv", [4, 128, 4, 512])
        dr["w_ada"] = self.din("w_ada", [12, 128, 8, 512])
        dr["w_in"] = self.din("w_in", [11, 128, 8, 512])
        dr["w_br"] = self.din("w_br", [2, 128, 8, 512])
        dr["w_out"] = self.din("w_out", [2, 128, 8, 512])
        dr["w_fi"] = self.din("w_fi", [12, 128, 8, 512])
        dr["w_fo"] = self.din("w_fo", [2, 128, 22, 512])
        dr["b_ada"] = self.din("b_ada", [128, 48])
        dr["nmix"] = self.din("nmix", [128, 8])
        dr["nffn"] = self.din("nffn", [128, 8])
        dr["nfin"] = self.din("nfin", [128, 8])
        dr["onorm"] = self.din("onorm", [128, 4])
        dr["lbl"] = self.din("lbl", [128, 4, 2])
        dr["rb"] = self.din("rb", [8, 192])
        dr["yT"] = self.dout("yT", [D, self.NOUT])
        dr["s_out"] = self.dout("s_out", [128, 5, 4, 128])
        dr["kTs"] = self.dout("kTs", [128, 4, 256])
        dr["vs"] = self.dout("vs", [128, 2, 512])
        dr["kTp"] = self.dout("kTp", [128, 4, 512])
        dr["vp"] = self.dout("vp", [128, 4, 512])
        self.ext_d = nc.dram_tensor("ext_d", [8, 383], F32)
        self.dr = dr
        sb = self.sb
        self.onesf = sb("onesf", [128, 128], F32)
        self.identf = sb("identf", [128, 128], F32)
        self.identb = sb("identb", [128, 128], BF16)
        self.onesb = sb("onesb", [128, 128], BF16)
        self.Jf = sb("Jf", [128, 128], F32)
        self.resetm = sb("resetm", [128, 512], F32)
        self.trim = sb("trim", [128, 4, 128], F32)
        self.epsc = sb("epsc", [128, 1], F32)
        self.cT = sb("cT_sb", [128, 8, 8], F32)
        self.scT = sb("scT", [128, 8, 8], BF16)
        self.flags = sb("flags_sb", [128, 4], F32)
        self.b_ada = sb("b_ada_sb", [128, 48], F32)
        self.nmix = sb("nmix_sb", [128, 8], F32)
        self.nffn = sb("nffn_sb", [128, 8], F32)
        self.nfin = sb("nfin_sb", [128, 8], F32)
        self.onorm = sb("onorm_sb", [128, 4], F32)
        self.lbl = sb("lbl_sb", [128, 4, 2], F32)
        self.lbd = sb("lbd", [128, 4], F32)
        self.lb = sb("lb", [128, 4], F32)
        self.oml = sb("oml", [128, 4], F32)
        self.modT = sb("modT", [128, 48, 8], F32)
        self.gm1 = sb("gm1", [128, 8, 8], F32)
        self.gm2 = sb("gm2", [128, 8, 8], F32)
        self.BT3 = sb("BT3", [128, 8, 128], F32)
        self.BT4 = sb("BT4", [128, 8, 128], F32)
        self.BT5 = sb("BT5", [128, 8, 64], F32)
        self.BT0 = sb("BT0", [128, 128], F32)
        self.xT = sb("xT_sb", [128, 8, 512], F32)
        self.hT = sb("hT", [128, 8, 512], BF16)
        self.mT = sb("mT", [128, 8, 512], BF16)
        self.rstd = sb("rstd", [128, 512], F32)
        self.wb = [sb("wb%d" % i, [128, 8, 512], BF16) for i in range(4)]
        self.wslot = 0
        self.arena = sb("arena", [128, 22 * 512], BF16)
        ar32 = self.arena.bitcast(F32)
        self.sT = self.arena[:, :].rearrange("p (j t) -> p j t", t=512)
        self.sqa4 = ar32[:, 0:2048].rearrange("p (h t) -> p h t", t=512)
        self.fT4 = ar32[:, 2048:4096].rearrange("p (h t) -> p h t", t=512)
        self.kTf = ar32[:, 0:2048].rearrange("p (h t) -> p h t", t=512)
        self.vf = ar32[:, 2048:4096].rearrange("p (h t) -> p h t", t=512)
        self.lgf = sb("lgf", [128, 512], F32)
        self.cum = sb("cum", [128, 512], F32)
        self.dd = self.lgf
        self.e1 = sb("e1", [128, 512], F32)
        self.e2 = self.lgf
        self.qpT = sb("qpT", [128, 4, 512], BF16)
        self.kpT = sb("kpT", [128, 4, 512], BF16)
        self.vtok = sb("vtok", [128, 4, 512], BF16)
        self.sga = sb("sga", [128, 4, 512], BF16)
        self.AT = sb("AT", [128, 4, 512], BF16)
        self.ktokA = sb("ktokA", [128, 4, 4, 128], BF16)
        self.ktokB = sb("ktokB", [128, 4, 4, 128], BF16)
        self.Sst = sb("Sst", [128, 4, 128], F32)
        self.sin = sb("sin_sb", [128, 4, 4, 128], F32)
        self.Ssc = sb("Ssc", [128, 8, 128], BF16)
        self.tmpS = sb("tmpS", [128, 8, 128], F32)
        self.eref = sb("eref", [128, 4, 8], F32)
        self.etot = sb("etot", [128, 4, 8], F32)
        self.etr = sb("etr", [128, 4, 8], F32)
        self.oaT = sb("oaT", [128, 4, 512], BF16)
        self.sqo = sb("sqo", [128, 4, 512], BF16)
        self.rso = self.e1
        self.t1 = sb("t1", [128, 512], F32)
        self.t2 = sb("t2", [128, 512], F32)
        self.rbt = self.t2[0:8, 0:192]
        self.ext = self.t1[0:8, 0:383]
        self.sa = sb("sa", [128, 512], F32)
        self.sbg = sb("sbg", [128, 512], F32)
        m32 = self.mT.bitcast(F32)[:, :, :].rearrange("p a b -> p (a b)")
        self.hset = [(self.lgf, self.cum, self.e1),
                     (ar32[:, 4096:4608], ar32[:, 4608:5120], ar32[:, 5120:5632]),
                     (m32[:, 0:512], m32[:, 512:1024], m32[:, 1024:1536]),
                     (self.sa, self.sbg, self.rstd)]
        self.hkeys = [(["lgf"], ["cum"], ["e1"]),
                      ([("arena", 16), ("arena", 17)], [("arena", 18), ("arena", 19)], [("arena", 20), ("arena", 21)]),
                      ([("mT", 0), ("mT", 1)], [("mT", 2), ("mT", 3)], [("mT", 4), ("mT", 5)]),
                      (["sa"], ["sbg"], ["rstd"])]
        if DBG_SETS:
            self.hset[2], self.hkeys[2] = self.hset[0], self.hkeys[0]
            self.hset[3], self.hkeys[3] = self.hset[1], self.hkeys[1]
        self.qTA = sb("qTA", [128, 4, 512], BF16)
        self.qTB = sb("qTB", [128, 4, 512], BF16)
        self.kT = [sb("kT%d" % i, [128, 4, 512], BF16) for i in range(2)]
        self.Va = [sb("Va%d" % i, [128, 4, 8, 65], BF16) for i in range(2)]
        self.sbias = sb("sbias", [128, 2, 384], F32)
        self.PTf = [sb("PTf%d" % i, [128, 5, 128], BF16) for i in range(2)]
        self.PTh = [sb("PTh%d" % i, [128, 5, 128], BF16) for i in range(2)]
        self.rden = sb("rden", [128, 8], F32)
        self.ob = self.AT
        self.obT = self.vtok
        self.ps = [self.es.enter_context(nc.psum_tensor("ps%d" % i, [128, 512], F32)) for i in range(8)]
        self.pfree = list(range(8))

    def _setup(self):
        dr = self.dr
        for name, t in (("cT", self.cT), ("flags", self.flags), ("b_ada", self.b_ada), ("nmix", self.nmix),
                        ("nffn", self.nffn), ("nfin", self.nfin), ("onorm", self.onorm), ("lbl", self.lbl),
                        ("rb", self.rbt), ("sin", self.sin)):
            self.dma("sp", t[:], dr[name], (), [t.name])
        self.memset(self.onesf[:], 1.0, ["onesf"])
        self.memset(self.onesb[:], 1.0, ["onesb"])
        self.memset(self.epsc[:], EPS, ["epsc"])
        self.memset(self.identf[:], 0.0, ["identf"])
        self.op("pool", lambda e: e.affine_select(out=self.identf[:], in_=self.onesf[:], pattern=[[-1, 128]],
                                                  compare_op=ALU.is_equal, fill=0.0, base=0, channel_multiplier=1),
                ["onesf"], ["identf"])
        self.cp(self.identb[:], self.identf[:], ["identf"], ["identb"])
        self.memset(self.Jf[:], 0.0, ["Jf"])
        self.op("pool", lambda e: e.affine_select(out=self.Jf[:], in_=self.onesf[:], pattern=[[1, 128]],
                                                  compare_op=ALU.is_equal, fill=0.0, base=-127, channel_multiplier=1),
                ["onesf"], ["Jf"])
        self.memset(self.resetm[:], 1.0, ["resetm"])
        self.memset(self.resetm[:].rearrange("p (c t) -> p c t", t=64)[:, :, 0:1], 0.0, ["resetm"])
        self.memset(self.trim[:], 1.0, ["trim"])
        self.op("pool", lambda e: e.affine_select(out=self.trim[:], in_=self.trim[:], pattern=[[0, 4], [1, 128]],
                                                  compare_op=ALU.is_ge, fill=0.0, base=0, channel_multiplier=-1),
                ["trim"], ["trim"])
        self.memset(self.trim[0:64, :, 64:128], 0.0, ["trim"])
        for t in (self.ktokA, self.ktokB, self.qTA, self.qTB, self.PTh[0], self.PTh[1]):
            self.memset(t[:], 0.0, [t.name])
        for t in self.Va:
            self.memset(t[:], 1.0, [t.name])
        self.memset(self.Sst[:], 0.0, ["Sst"])
        self.tt(self.lbd[:], self.lbl[:, :, 0], self.lbl[:, :, 1], ALU.subtract, ["lbl_sb"], ["lbd"])
        self.act(self.lb[:], self.lbd[:], AF.Sigmoid, ["lbd"], ["lb"])
        self.act(self.oml[:], self.lbd[:], AF.Sigmoid, ["lbd"], ["oml"], scale=-1.0)
        self.act(self.scT[:], self.cT[:], AF.Silu, ["cT_sb"], ["scT"])
        bm = self.pa()
        for blk in range(12):
            w, wk = self.wld("w_ada", blk)
            for t4 in range(4):
                t = blk * 4 + t4
                for k in range(8):
                    self.mm(self.PB(bm)[:, t * 8:(t + 1) * 8], w[:, k, t4 * 128:(t4 + 1) * 128], self.scT[:, k, :],
                            k == 0, k == 7, [wk, "scT"], [("ps", bm)])
        self.tt(self.modT[:], self.PB(bm)[:, 0:384].rearrange("p (t s) -> p t s", s=8),
                self.b_ada[:].unsqueeze(2).to_broadcast([128, 48, 8]), ALU.add, [("ps", bm), "b_ada_sb"], ["modT"])
        self.pf(bm)
        self.stt(self.gm1[:], self.modT[:, 8:16, :], 1.0, self.nmix[:].unsqueeze(2).to_broadcast([128, 8, 8]),
                 ALU.add, ALU.mult, ["modT", "nmix_sb"], ["gm1"])
        self.stt(self.gm2[:], self.modT[:, 32:40, :], 1.0, self.nffn[:].unsqueeze(2).to_broadcast([128, 8, 8]),
                 ALU.add, ALU.mult, ["modT", "nffn_sb"], ["gm2"])
        self.memset(self.ext[:], 0.0, ["t1"])
        self.ts(self.ext[:, 64:256], self.rbt[:], self.rbt[:, 191:192], None, ALU.subtract, None, ["t2"], ["t1"])
        self.ts(self.ext[:, 0:64], self.ext[:, 256:320], self.rbt[:, 0:1], self.rbt[:, 191:192], ALU.add, ALU.subtract,
                ["t2", "t1"], ["t1"])
        self.dma("sp", self.ext_d.ap(), self.ext[:], ["t1"], ["ext_d"])
        hank = self.sqa4
        for which, off, nq, BT in ((3, 128, 128, self.BT3), (4, 0, 128, self.BT4), (5, 64, 64, self.BT5)):
            hv = self.arena.bitcast(F32)[:, 0:8 * nq].rearrange("p (h q) -> p h q", q=nq)
            src = bass.AP(self.ext_d, off, [[1, 128], [383, 8], [1, nq]])
            self.dma("sp", hv, src, ["ext_d"], [("arena", u) for u in range(8)])
            nb = (8 * nq) // 512
            for b2 in range(nb):
                bk = self.pa()
                hpb = 512 // nq
                for hh in range(hpb):
                    h = b2 * hpb + hh
                    self.mm(self.PB(bk)[:, hh * nq:(hh + 1) * nq], self.Jf[:], hv[:, h, :], True, True,
                            ["Jf"] + [("arena", u) for u in range(8)], [("ps", bk)])
                self.cp(BT[:, b2 * hpb:(b2 + 1) * hpb, :], self.PB(bk)[:, :].rearrange("p (h q) -> p h q", q=nq),
                        [("ps", bk)], [BT.name])
                self.pf(bk)
        self.memset(self.BT0[:], 0.0, ["BT0"])
        self.memset(self.BT0[0:64, 64:128], NEGM, ["BT0"])
        self.memset(self.BT4[64:128, :, 0:64], NEGM, ["BT4"])
        self.memset(self.BT5[0:64, :, :], NEGM, ["BT5"])

    def _norm_stats(self, ncol):
        sq = self.mT
        b = self.pa()
        for k in range(8):
            self.act(sq[:, k, 0:ncol], self.xT[:, k, 0:ncol], AF.Square, [("xT_sb", k)], [("mT", k)])
            self.mm(self.PB(b)[:, 0:ncol], self.onesb[:], sq[:, k, 0:ncol], k == 0, k == 7, ["onesb", ("mT", k)], [("ps", b)])
        self.act(self.rstd[:, 0:ncol], self.PB(b)[:, 0:ncol], AF.Ln, [("ps", b), "epsc"], ["rstd"], scale=1.0 / D,
                 bias=self.epsc[:, 0:1])
        self.act(self.PB(b)[:, 0:ncol], self.rstd[:, 0:ncol], AF.Exp, ["rstd"], [("ps", b)], scale=-0.5)
        return b

    def _norm_apply(self, st, gm, sh_off, rb_):
        i = 0
        for k in range(8):
            for (c0, n, seq) in st["segs"]:
                sl = i % 2
                i += 1
                tb = self.t1 if sl == 0 else self.t2
                self.stt(tb[:, 0:n], self.xT[:, k, c0:c0 + n], gm[:, k, seq:seq + 1], self.PB(rb_)[:, c0:c0 + n],
                         ALU.mult, ALU.mult, [("xT_sb", k), ("ps", rb_), gm.name], [tb.name])
                self.act(self.hT[:, k, c0:c0 + n], tb[:, 0:n], AF.Identity, [tb.name, "modT"],
                         [("hT", k)], bias=self.modT[:, sh_off + k, seq:seq + 1])
        self.pf(rb_)

    def _supertile(self, st):
        dr = self.dr
        ncol = st["ncol"]
        kind = st["kind"]
        cur = st["cur"]
        ntt = ncol // 128
        nch = ncol // 64
        segs = st["segs"]
        xsrc = dr["xT"].rearrange("(k p) n -> p k n", p=128)[:, :, st["col0"]:st["col0"] + ncol]
        for k in range(8):
            self.dma("sp", self.xT[:, k, 0:ncol], xsrc[:, k, :], (), [("xT_sb", k)])
        self.chk(2)
        rb_ = self._norm_stats(ncol)
        self.chk(3)
        self._norm_apply(st, self.gm1, 0, rb_)
        self.chk(4)
        want_kv = kind == "sample" or (kind == "prompt" and st["last"])

        self._hgrn(st, kind == "halo")

        self.chk(6)
        if kind != "halo":
            w, wk = self.wld("w_in", 4)
            for hp in range(4):
                b = self.pa()
                for k in range(8):
                    self.mm(self.PB(b)[:, 0:ncol], w[:, k, hp * 128:(hp + 1) * 128], self.hT[:, k, 0:ncol], k == 0, k == 7,
                            [wk, ("hT", k)], [("ps", b)])
                self.act(self.qTA[0:64, hp, 0:ncol], self.PB(b)[0:64, 0:ncol], AF.Identity, [("ps", b)], [("qTA", hp)], scale=0.125)
                self.ts(self.qTB[64:128, hp, 0:ncol], self.PB(b)[64:128, 0:ncol], 0.125, None, ALU.mult, None,
                        [("ps", b)], [("qTB", hp)])
                self.pf(b)
        self.chk(6.1)
        w, wk = self.wld("w_in", 5)
        for hp in range(4):
            b = self.pa()
            for k in range(8):
                self.mm(self.PB(b)[:, 0:ncol], w[:, k, hp * 128:(hp + 1) * 128], self.hT[:, k, 0:ncol], k == 0, k == 7,
                        [wk, ("hT", k)], [("ps", b)])
            self.cp(self.kT[cur][:, hp, 0:ncol], self.PB(b)[:, 0:ncol], [("ps", b)], [(self.kT[cur].name, hp)], eng="act")
            if want_kv:
                self.cp(self.kTf[:, hp, 0:ncol], self.PB(b)[:, 0:ncol], [("ps", b)], [("arena", 2 * hp), ("arena", 2 * hp + 1)])
            self.pf(b)
        self.chk(6.2)
        w, wk = self.wld("w_in", 6)
        if kind != "halo":
            self.memset(self.Va[cur][:, :, :, 64:65], 1.0, [self.Va[cur].name])
        for tt_ in range(ntt):
            b = self.pa()
            for k in range(8):
                self.mm(self.PB(b)[:, 0:512], self.hT[:, k, tt_ * 128:(tt_ + 1) * 128], w[:, k, :], k == 0, k == 7,
                        [wk, ("hT", k)], [("ps", b)])
            self.cp(self.Va[cur][:, tt_, :, 0:64], self.PB(b)[:, :].rearrange("p (h d) -> p h d", d=64), [("ps", b)],
                    [(self.Va[cur].name, tt_)], eng="act")
            if want_kv:
                self.cp(self.vf[:, tt_, :], self.PB(b)[:, :], [("ps", b)], [("arena", 8 + 2 * tt_), ("arena", 9 + 2 * tt_)])
            self.pf(b)
        self.chk(6.4)
        if want_kv:
            ko, vo = ("kTs", "vs") if kind == "sample" else ("kTp", "vp")
            self.dma("sp", dr[ko], self.kTf[:, :, 0:ncol], [("arena", u) for u in range(8)], [ko])
            self.dma("sp", dr[vo], self.vf[:, 0:ntt, :], [("arena", u) for u in range(8, 16)], [vo])
        if kind == "halo":
            va = self.Va[cur]
            self.ts(va[:], va[:], self.flags[:, 0:1], None, ALU.mult, None, [va.name, "flags_sb"], [va.name])
            return

        self.chk(7)
        self._attention(st)
        self.chk(8)

        for cb in range(2):
            wbr, kbr = self.wld("w_br", cb)
            wga, kga = self.wld("w_in", 7 + cb)
            wgb, kgb = self.wld("w_in", 9 + cb)
            for m4 in range(4):
                m = cb * 4 + m4
                cs = slice(m4 * 128, (m4 + 1) * 128)
                bya, byb, bga, bgb = self.pa(), self.pa(), self.pa(), self.pa()
                for k in range(4):
                    self.mm(self.PB(bya)[:, 0:ncol], wbr[:, k, cs], self.oaT[:, k, 0:ncol], k == 0, k == 3,
                            [kbr, ("oaT", k)], [("ps", bya)])
                for k in range(4):
                    self.mm(self.PB(byb)[:, 0:ncol], wbr[:, 4 + k, cs], self.obT[:, k, 0:ncol], k == 0, k == 3,
                            [kbr, ("vtok", k)], [("ps", byb)])
                for k in range(8):
                    self.mm(self.PB(bga)[:, 0:ncol], wga[:, k, cs], self.hT[:, k, 0:ncol], k == 0, k == 7,
                            [kga, ("hT", k)], [("ps", bga)])
                for k in range(8):
                    self.mm(self.PB(bgb)[:, 0:ncol], wgb[:, k, cs], self.hT[:, k, 0:ncol], k == 0, k == 7,
                            [kgb, ("hT", k)], [("ps", bgb)])
                self.act(self.sa[:, 0:ncol], self.PB(bga)[:, 0:ncol], AF.Sigmoid, [("ps", bga)], ["sa"])
                self.act(self.sbg[:, 0:ncol], self.PB(bgb)[:, 0:ncol], AF.Sigmoid, [("ps", bgb)], ["sbg"])
                self.pf(bga)
                self.pf(bgb)
                self.tt(self.t1[:, 0:ncol], self.PB(bya)[:, 0:ncol], self.sa[:, 0:ncol], ALU.mult, [("ps", bya), "sa"], ["t1"])
                self.tt(self.t2[:, 0:ncol], self.PB(byb)[:, 0:ncol], self.sbg[:, 0:ncol], ALU.mult, [("ps", byb), "sbg"], ["t2"])
                self.pf(bya)
                self.pf(byb)
                self.tt(self.mT[:, m, 0:ncol], self.t1[:, 0:ncol], self.t2[:, 0:ncol], ALU.add, ["t1", "t2"], [("mT", m)])
        self.chk(9)
        for cb in range(2):
            w, wk = self.wld("w_out", cb)
            for m4 in range(4):
                m = cb * 4 + m4
                b = self.pa()
                for k in range(8):
                    self.mm(self.PB(b)[:, 0:ncol], w[:, k, m4 * 128:(m4 + 1) * 128], self.mT[:, k, 0:ncol], k == 0, k == 7,
                            [wk, ("mT", k)], [("ps", b)])
                for (c0, n, seq) in segs:
                    self.stt(self.xT[:, m, c0:c0 + n], self.PB(b)[:, c0:c0 + n], self.modT[:, 16 + m, seq:seq + 1],
                             self.xT[:, m, c0:c0 + n], ALU.mult, ALU.add, [("ps", b), "modT", ("xT_sb", m)], [("xT_sb", m)])
                self.pf(b)
        self.chk(10)
        rb_ = self._norm_stats(ncol)
        self._norm_apply(st, self.gm2, 24, rb_)
        for bi in range(6):
            width = 512 if bi < 5 else 256
            wa, ka = self.wld("w_fi", bi, width=width)
            wu, ku = self.wld("w_fi", 6 + bi, width=width)
            for j4 in range(width // 128):
                j = bi * 4 + j4
                cs = slice(j4 * 128, (j4 + 1) * 128)
                ba, bu = self.pa(), self.pa()
                for k in range(8):
                    self.mm(self.PB(ba)[:, 0:ncol], wa[:, k, cs], self.hT[:, k, 0:ncol], k == 0, k == 7,
                            [ka, ("hT", k)], [("ps", ba)])
                for k in range(8):
                    self.mm(self.PB(bu)[:, 0:ncol], wu[:, k, cs], self.hT[:, k, 0:ncol], k == 0, k == 7,
                            [ku, ("hT", k)], [("ps", bu)])
                sl = j % 2
                tsl = self.t1 if sl == 0 else self.t2
                self.act(tsl[:, 0:ncol], self.PB(ba)[:, 0:ncol], AF.Silu, [("ps", ba)], [tsl.name])
                self.pf(ba)
                self.tt(self.sT[:, j, 0:ncol], self.PB(bu)[:, 0:ncol], tsl[:, 0:ncol], ALU.mult, [("ps", bu), tsl.name],
                        [("arena", j)])
                self.pf(bu)
        for cb in range(2):
            banks = [self.pa() for _ in range(4)]
            for (k0, kn) in ((0, 8), (8, 8), (16, 6)):
                w, wk = self.wld("w_fo", cb, k0=k0, kn=kn)
                for m4 in range(4):
                    for kk in range(kn):
                        kg = k0 + kk
                        self.mm(self.PB(banks[m4])[:, 0:ncol], w[:, kk, m4 * 128:(m4 + 1) * 128], self.sT[:, kg, 0:ncol],
                                kg == 0, kg == 21, [wk, ("arena", kg)], [("ps", banks[m4])])
            for m4 in range(4):
                m = cb * 4 + m4
                b = banks[m4]
                for (c0, n, seq) in segs:
                    self.stt(self.xT[:, m, c0:c0 + n], self.PB(b)[:, c0:c0 + n], self.modT[:, 40 + m, seq:seq + 1],
                             self.xT[:, m, c0:c0 + n], ALU.mult, ALU.add, [("ps", b), "modT", ("xT_sb", m)], [("xT_sb", m)])
                self.pf(b)
        self.chk(11)
        rb_ = self._norm_stats(ncol)
        ydst = dr["yT"].rearrange("(k p) n -> p k n", p=128)[:, :, st["ocol"]:st["ocol"] + ncol]
        for k in range(8):
            self.stt(self.xT[:, k, 0:ncol], self.xT[:, k, 0:ncol], self.nfin[:, k:k + 1], self.PB(rb_)[:, 0:ncol],
                     ALU.mult, ALU.mult, [("xT_sb", k), ("ps", rb_), "nfin_sb"], [("xT_sb", k)])
            self.dma("sp", ydst[:, k, :], self.xT[:, k, 0:ncol], [("xT_sb", k)], [("yT", k)])
        self.pf(rb_)

    def _hgrn(self, st, so=False):
        ncol = st["ncol"]
        kind = st["kind"]
        ntt = ncol // 128
        nch = ncol // 64
        npair = ncol // 128
        wblk = {}
        for blk in range(3):
            wblk[blk] = self.wld("w_in", blk)
        for h in range(4):
            for which in ((1,) if so else (0, 1, 2)):
                idx = 3 * h + which
                w, wk = wblk[idx // 4]
                cs = slice((idx % 4) * 128, (idx % 4 + 1) * 128)
                b = self.pa()
                for k in range(8):
                    self.mm(self.PB(b)[:, 0:ncol], w[:, k, cs], self.hT[:, k, 0:ncol], k == 0, k == 7,
                            [wk, ("hT", k)], [("ps", b)])
                if which == 0:
                    self.act(self.sqa4[:, h, 0:ncol], self.PB(b)[:, 0:ncol], AF.Silu, [("ps", b)],
                             [("arena", 2 * h), ("arena", 2 * h + 1)])
                elif which == 1:
                    self.act(self.fT4[:, h, 0:ncol], self.PB(b)[:, 0:ncol], AF.Sigmoid, [("ps", b)],
                             [("arena", 8 + 2 * h), ("arena", 9 + 2 * h)])
                else:
                    self.act(self.sga[:, h, 0:ncol], self.PB(b)[:, 0:ncol], AF.Silu, [("ps", b)], [("sga", h)])
                self.pf(b)
        w, wk = self.wld("w_in", 3)
        for tt_ in range(ntt):
            b = self.pa()
            for k in range(8):
                self.mm(self.PB(b)[:, 0:512], self.hT[:, k, tt_ * 128:(tt_ + 1) * 128], w[:, k, :], k == 0, k == 7,
                        [wk, ("hT", k)], [("ps", b)])
            self.cp(self.vtok[:, tt_, :], self.PB(b)[:, :], [("ps", b)], [("vtok", tt_)])
            self.pf(b)
        gens = [self._hg_pre(st, so, h) for h in range(4)]
        while gens:
            for g in list(gens):
                try:
                    next(g)
                except StopIteration:
                    gens.remove(g)
        bO = [None] * 4 if so else [self.pa() for _ in range(4)]
        for c in range(nch):
            p, half = c // 2, c % 2
            bKV = self.pa()
            kt = self.ktokA if half == 0 else self.ktokB
            for h in range(4):
                hs = slice(h * 128, (h + 1) * 128)
                self.mm(self.PB(bKV)[:, hs], kt[:, h, p, :], self.vtok[:, p, hs], True, True,
                        [(kt.name, h), ("vtok", p)], [("ps", bKV)])
            for h in range(4):
                hs = slice(h * 128, (h + 1) * 128)
                if kind == "sample":
                    Sap = self.sin[:, c, h, :]
                    skey = ("sin_sb", c * 4 + h)
                else:
                    Sap = self.Sst[:, h, :]
                    skey = ("Sst", h)
                sl = h * 2 + c % 2
                if not so:
                    self.act(self.Ssc[:, sl, :], Sap, AF.Identity, [skey, ("eref", h)], [("Ssc", sl)], scale=self.eref[:, h, c:c + 1])
                    if half == 0:
                        ps_ = slice(p * 128, (p + 1) * 128)
                        self.mm(self.PB(bO[h])[:, ps_], self.vtok[:, p, hs], self.AT[:, h, ps_], True, False,
                                [("vtok", p), ("AT", h)], [("ps", bO[h])])
                    cs = slice(c * 64, (c + 1) * 64)
                    self.mm(self.PB(bO[h])[:, cs], self.Ssc[:, sl, :], self.qpT[:, h, cs], False, half == 1,
                            [("Ssc", sl), ("qpT", h)], [("ps", bO[h])])
                self.act(self.tmpS[:, sl, :], Sap, AF.Identity, [skey, ("etot", h)], [("tmpS", sl)], scale=self.etot[:, h, c:c + 1])
                self.stt(Sap, self.PB(bKV)[:, hs], self.etr[:, h, c:c + 1], self.tmpS[:, sl, :], ALU.mult, ALU.add,
                         [("ps", bKV), ("etr", h), ("tmpS", sl)], [skey])
            self.pf(bKV)
        if not so:
            gens = [self._hg_post(st, h, bO[h]) for h in range(4)]
            while gens:
                for g in list(gens):
                    try:
                        next(g)
                    except StopIteration:
                        gens.remove(g)
        if so:
            self.ts(self.Sst[:], self.Sst[:], self.flags[:, 0:1], None, ALU.mult, None, ["Sst", "flags_sb"], ["Sst"])
        if kind == "sample":
            self.dma("sp", self.dr["s_out"][:, 0:4, :, :], self.sin[:], ["sin_sb"], ["s_out"])

    def _hg_pre(self, st, so, h):
        ncol = st["ncol"]
        kind = st["kind"]
        nch = ncol // 64
        npair = ncol // 128
        lgf, cum, e1 = self.hset[h]
        lk, ck, ek = self.hkeys[h]
        dd = lgf
        e2 = lgf
        fk = [("arena", 8 + 2 * h), ("arena", 9 + 2 * h)]
        qk = [("arena", 2 * h), ("arena", 2 * h + 1)]
        fT = self.fT4[:, h, 0:ncol]
        self.ts(fT, fT, self.oml[:, h:h + 1], self.lb[:, h:h + 1], ALU.mult, ALU.add, fk + ["oml", "lb"], fk)
        self.act(lgf[:, 0:ncol], fT, AF.Ln, fk, lk)
        yield
        self.op("dve", lambda e: e.tensor_tensor_scan(out=cum[:, 0:ncol], data0=self.resetm[:, 0:ncol],
                                                      data1=lgf[:, 0:ncol], initial=0.0, op0=ALU.mult, op1=ALU.add),
                ["resetm"] + lk, ck)
        self.act(fT, fT, AF.Identity, fk + ["onesf"], fk, scale=-1.0, bias=self.onesf[:, 0:1])
        yield
        cum3 = cum[:, 0:ncol].rearrange("p (c t) -> p c t", t=64)
        self.tt(dd[:, 0:ncol].rearrange("p (c t) -> p c t", t=64), cum3, cum3[:, :, 32:33].to_broadcast([128, nch, 64]),
                ALU.subtract, ck, lk)
        if not so:
            self.act(e1[:, 0:ncol], dd[:, 0:ncol], AF.Exp, lk, ek)
        self.act(e2[:, 0:ncol], dd[:, 0:ncol], AF.Exp, lk, lk, scale=-1.0)
        yield
        if not so:
            self.tt(self.qpT[:, h, 0:ncol], self.sqa4[:, h, 0:ncol], e1[:, 0:ncol], ALU.mult, qk + ek, [("qpT", h)])
        self.tt(self.kpT[:, h, 0:ncol], fT, e2[:, 0:ncol], ALU.mult, fk + lk, [("kpT", h)])
        if not so:
            self.act(self.eref[:, h, 0:nch], cum3[:, :, 32], AF.Exp, ck, [("eref", h)])
        self.act(self.etot[:, h, 0:nch], cum3[:, :, 63], AF.Exp, ck, [("etot", h)])
        self.tt(self.etr[:, h, 0:nch], cum3[:, :, 63], cum3[:, :, 32], ALU.subtract, ck, [("etr", h)])
        self.act(self.etr[:, h, 0:nch], self.etr[:, h, 0:nch], AF.Exp, [("etr", h)], [("etr", h)])
        yield
        if not so:
            bA = self.pa()
            for p in range(npair):
                ps_ = slice(p * 128, (p + 1) * 128)
                self.mm(self.PB(bA)[:, ps_], self.kpT[:, h, ps_], self.qpT[:, h, ps_], True, True,
                        [("kpT", h), ("qpT", h)], [("ps", bA)])
            self.tt(self.AT[:, h, 0:ncol], self.PB(bA)[:, 0:ncol], self.trim[:, 0:npair, :].rearrange("p a b -> p (a b)"), ALU.mult,
                    [("ps", bA), "trim"], [("AT", h)])
            self.pf(bA)
        bT = self.pa()
        for p in range(npair):
            ps_ = slice(p * 128, (p + 1) * 128)
            self.mm(self.PB(bT)[:, ps_], self.kpT[:, h, ps_], self.identb[:], True, True,
                    [("kpT", h), "identb"], [("ps", bT)])
        bT3 = self.PB(bT)[:, 0:ncol].rearrange("p (a b) -> p a b", b=128)
        self.cp(self.ktokA[0:64, h, 0:npair, :], bT3[0:64, :, :], [("ps", bT)], [("ktokA", h)], eng="act")
        self.cp(self.ktokB[64:128, h, 0:npair, :], bT3[64:128, :, :], [("ps", bT)], [("ktokB", h)])
        self.pf(bT)
        yield

    def _hg_post(self, st, h, bO):
        ncol = st["ncol"]
        e1 = self.hset[h][2]
        ek = self.hkeys[h][2]
        sqo = self.sqo[:, h, :]
        sk = ("sqo", h)
        tb = self.hset[h][0]
        tk = self.hkeys[h][0]
        self.act(sqo[:, 0:ncol], self.PB(bO)[:, 0:ncol], AF.Square, [("ps", bO)], [sk])
        bS = self.pa()
        self.mm(self.PB(bS)[:, 0:ncol], self.onesb[:], sqo[:, 0:ncol], True, True, ["onesb", sk], [("ps", bS)])
        self.act(e1[:, 0:ncol], self.PB(bS)[:, 0:ncol], AF.Ln, [("ps", bS), "epsc"], ek, scale=1.0 / 128,
                 bias=self.epsc[:, 0:1])
        self.pf(bS)
        yield
        self.act(e1[:, 0:ncol], e1[:, 0:ncol], AF.Exp, ek, ek, scale=-0.5)
        self.tt(tb[:, 0:ncol], self.PB(bO)[:, 0:ncol], e1[:, 0:ncol], ALU.mult, [("ps", bO)] + ek, tk)
        self.pf(bO)
        yield
        self.stt(self.oaT[:, h, 0:ncol], tb[:, 0:ncol], self.onorm[:, h:h + 1], self.sga[:, h, 0:ncol], ALU.mult, ALU.mult,
                 tk + ["onorm_sb", ("sga", h)], [("oaT", h)])

    def _attn_unit(self, q0, nq, qcol, tiles, PTs, ob_tt):
        nt = len(tiles)
        plain = [i for i, t in enumerate(tiles) if t["bias"] is None]
        biased = [i for i, t in enumerate(tiles) if t["bias"] is not None]
        nbi = len(biased)
        assert len(plain) <= 3 and nbi in (2, 3)
        npl = len(plain)
        assert plain == list(range(plain[0], plain[0] + npl)) and biased == list(range(biased[0], biased[0] + nbi))
        bO = [self.pa(), self.pa()]
        pipelined = isinstance(PTs, list)

        def stageA(h):
            hp = h // 2
            qsel = self.qTA if h % 2 == 0 else self.qTB
            qk = (qsel.name, hp)
            PT = PTs[h % 2] if pipelined else PTs
            b1 = self.pa()
            for n_, i in enumerate(plain):
                kb, kt = tiles[i]["k"]
                self.mm(self.PB(b1)[:, n_ * nq:(n_ + 1) * nq], self.kT[kb][:, hp, kt * 128:(kt + 1) * 128],
                        qsel[:, hp, qcol:qcol + nq], True, True, [(self.kT[kb].name, hp), qk], [("ps", b1)])
            b2 = self.pa()
            for n_, i in enumerate(biased):
                kb, kt = tiles[i]["k"]
                self.mm(self.PB(b2)[:, n_ * nq:(n_ + 1) * nq], self.kT[kb][:, hp, kt * 128:(kt + 1) * 128],
                        qsel[:, hp, qcol:qcol + nq], True, True, [(self.kT[kb].name, hp), qk], [("ps", b2)])
            self.act(PT[:, plain[0]:plain[0] + npl, q0:q0 + nq], self.PB(b1)[:, 0:npl * nq].rearrange("p (a b) -> p a b", b=nq),
                     AF.Exp, [("ps", b1)], [(PT.name, 0)])
            for i in plain:
                if tiles[i]["zero"]:
                    self.memset(PT[0:64, i, 64:128], 0.0, [(PT.name, 0)])
            self.pf(b1)
            sl = h % 2
            for n_, i in enumerate(biased):
                self.tt(self.sbias[:, sl, n_ * nq:(n_ + 1) * nq], self.PB(b2)[:, n_ * nq:(n_ + 1) * nq], tiles[i]["bias"](h), ALU.add,
                        [("ps", b2), "BT0", "BT3", "BT4", "BT5"], [("sbias", sl)])
            self.pf(b2)
            self.act(PT[:, biased[0]:biased[0] + nbi, q0:q0 + nq], self.sbias[:, sl, 0:nbi * nq].rearrange("p (a b) -> p a b", b=nq),
                     AF.Exp, [("sbias", sl)], [(PT.name, 0)])

        def stageB(h):
            PT = PTs[h % 2] if pipelined else PTs
            ob_ = bO[h // 4]
            oc = (h % 4) * 65
            for i, t in enumerate(tiles):
                vb, vt = t["k"]
                self.mm(self.PB(ob_)[:, oc:oc + 65], PT[:, i, :], self.Va[vb][:, vt, h, :], i == 0, i == nt - 1,
                        [(PT.name, 0), (self.Va[vb].name, vt)], [("ps", ob_)])

        if pipelined:
            stageA(0)
            for h in range(8):
                if h + 1 < 8:
                    stageA(h + 1)
                stageB(h)
        else:
            for h in range(8):
                stageA(h)
                stageB(h)
        rows = slice(q0, q0 + nq)
        for g in range(2):
            o3 = self.PB(bO[g])[rows, 0:260].rearrange("p (h d) -> p h d", d=65)
            self.op("dve", lambda e, o3=o3, g=g: e.reciprocal(out=self.rden[rows, 4 * g:4 * g + 4], in_=o3[:, :, 64]),
                    [("ps", bO[g])], [("rden", g)])
            self.tt(self.ob[rows, ob_tt, 256 * g:256 * (g + 1)].rearrange("p (h d) -> p h d", d=64), o3[:, :, 0:64],
                    self.rden[rows, 4 * g:4 * g + 4].unsqueeze(2).to_broadcast([nq, 4, 64]), ALU.mult,
                    [("ps", bO[g]), ("rden", g)], [("AT", ob_tt)])
            self.pf(bO[g])

    def _attention(self, st):
        ncol = st["ncol"]
        kind = st["kind"]
        cur = st["cur"]
        prev = 1 - cur
        npair = ncol // 128
        if kind == "prompt":
            for p in range(npair):
                tiles = []
                for j in (1, 2, 0, 3, 4):
                    kk = (prev, p + j) if p + j <= 3 else (cur, p + j - 4)
                    if j == 3:
                        bias = (lambda h: self.BT3[:, h, :])
                    elif j == 4:
                        bias = (lambda h: self.BT4[:, h, :])
                    elif j == 0:
                        bias = (lambda h: self.BT0[:, :])
                    else:
                        bias = None
                    tiles.append(dict(k=kk, bias=bias, zero=False))
                self._attn_unit(0, 128, p * 128, tiles, self.PTf, p)
        else:
            for i in range(4):
                pr, half = i // 2, i % 2
                kdst, vdst = self.kT[prev], self.Va[prev]
                self.dma("pool", kdst[:], self.dr["ckT"][i], (), [kdst.name])
                self.dma("pool", vdst[:, :, :, 0:64], self.dr["cv"][i].rearrange("p t (h d) -> p t h d", d=64), (), [vdst.name])
                tiles = []
                for j in range(4):
                    bias = (lambda h: self.BT3[:, h, 0:64]) if j == 3 else None
                    tiles.append(dict(k=(prev, j), bias=bias, zero=False))
                own_bias = (lambda h: self.BT4[:, h, 0:64]) if half == 0 else (lambda h: self.BT5[:, h, 0:64])
                tiles.append(dict(k=(cur, pr), bias=own_bias, zero=False))
                self._attn_unit(half * 64, 64, i * 64, tiles, self.PTh[half], pr)
        for p in range(npair):
            b = self.pa()
            for vt in range(4):
                self.mm(self.PB(b)[:, vt * 128:(vt + 1) * 128], self.ob[:, p, vt * 128:(vt + 1) * 128], self.identb[:], True, True,
                        [("AT", p), "identb"], [("ps", b)])
            self.cp(self.obT[:, :, p * 128:(p + 1) * 128], self.PB(b)[:, :].rearrange("p (a b) -> p a b", b=128), [("ps", b)],
                    [("vtok", k) for k in range(4)], eng="act" if p % 2 else "dve")
            self.pf(b)

    def _finish(self):
        self.dma("sp", self.dr["s_out"][:, 4, :, :], self.Sst[:], ["Sst"], ["s_out4"])


def _blk(w, ncols_list=None):
    K, N = w.shape
    nb = N // 512
    return np.ascontiguousarray(w.reshape(K // 128, 128, nb, 512).transpose(2, 1, 0, 3))


_WIN_TILES = None


def _win_perm():
    tiles = []
    for h in range(4):
        tiles += [0 + h, 4 + h, 12 + h]
    tiles += [8, 9, 10, 11]
    tiles += list(range(16, 44))
    cols = np.concatenate([np.arange(t * 128, (t + 1) * 128) for t in tiles])
    return cols


_PROG = {}


def _get_prog(NS):
    if NS not in _PROG:
        b = Builder(NS)
        nc = b.build()
        _PROG[NS] = (nc, b)
    return _PROG[NS]


def make_in_maps(x_prompt, x_sample, c_prompt, c_sample, state_hgrn, cache_k, cache_v, w_ada, b_ada, norm_mix, w_in,
                 hgrn_lb_logits, hgrn_out_norm, w_branch_a, rel_bias, w_branch_b, w_out, norm_ffn, w_ffn_in, w_ffn_out,
                 norm_final):
    f32 = np.float32
    x_prompt = np.asarray(x_prompt, f32)
    x_sample = np.asarray(x_sample, f32)
    B, L, _ = x_prompt.shape
    SEG = L // 4
    NS = SEG // 512
    assert B == 2 and SEG % 512 == 0 and x_sample.shape[0] == 32 and x_sample.shape[1] == 64
    w_in_p = np.asarray(w_in[0], f32)[:, _win_perm()]
    wfi = np.asarray(w_ffn_in[0], f32)
    DFF = wfi.shape[1] // 2
    wfi_p = np.zeros((1024, 12 * 512), f32)
    wfi_p[:, 0:DFF] = wfi[:, 0:DFF]
    wfi_p[:, 6 * 512:6 * 512 + DFF] = wfi[:, DFF:]
    w_br = np.concatenate([np.asarray(w_branch_a[0], f32), np.asarray(w_branch_b[0], f32)], axis=0)
    shared = {
        "w_ada": _blk(np.asarray(w_ada[0], f32)),
        "w_in": _blk(w_in_p),
        "w_br": _blk(w_br),
        "w_out": _blk(np.asarray(w_out[0], f32)),
        "w_fi": _blk(wfi_p),
        "w_fo": _blk(np.asarray(w_ffn_out[0], f32)),
        "b_ada": np.ascontiguousarray(np.asarray(b_ada[0], f32).reshape(48, 128).T),
        "nmix": np.ascontiguousarray(np.asarray(norm_mix[0], f32).reshape(8, 128).T),
        "nffn": np.ascontiguousarray(np.asarray(norm_ffn[0], f32).reshape(8, 128).T),
        "nfin": np.ascontiguousarray(np.asarray(norm_final, f32).reshape(8, 128).T),
        "onorm": np.ascontiguousarray(np.asarray(hgrn_out_norm[0], f32).reshape(4, 128).T),
        "lbl": np.ascontiguousarray(np.asarray(hgrn_lb_logits, f32)[0:2].reshape(2, 4, 128).transpose(2, 1, 0)),
        "rb": np.ascontiguousarray(np.asarray(rel_bias[0], f32)),
    }
    c_prompt = np.asarray(c_prompt, f32)
    c_sample = np.asarray(c_sample, f32)
    state_hgrn = np.asarray(state_hgrn, f32)
    cache_k = np.asarray(cache_k, f32)
    cache_v = np.asarray(cache_v, f32)
    in_maps = []
    for r in range(NCORES):
        b, kseg = r // 4, r % 4
        xs = x_sample[4 * r:4 * r + 4].reshape(256, D)
        t0 = kseg * SEG
        if kseg == 0:
            xh = np.zeros((512, D), f32)
        else:
            xh = x_prompt[b, t0 - 512:t0]
        xp = x_prompt[b, t0:t0 + SEG]
        xT = np.ascontiguousarray(np.concatenate([xs, xh, xp], axis=0).T)
        cs = np.zeros((8, D), f32)
        cs[0] = c_prompt[b]
        cs[1:5] = c_sample[4 * r:4 * r + 4]
        cT = np.ascontiguousarray(cs.reshape(8, 8, 128).transpose(2, 1, 0))
        flags = np.zeros((128, 4), f32)
        flags[:, 0] = 0.0 if kseg == 0 else 1.0
        sin = np.ascontiguousarray(state_hgrn[0, 4 * r:4 * r + 4].transpose(2, 0, 1, 3))
        ck = cache_k[0, 4 * r:4 * r + 4]
        ckT = np.ascontiguousarray(ck.reshape(4, 512, 4, 128).transpose(0, 3, 2, 1))
        cv = np.ascontiguousarray(cache_v[0, 4 * r:4 * r + 4].reshape(4, 4, 128, 512).transpose(0, 2, 1, 3))
        m = dict(shared)
        m.update({"xT": xT, "cT": cT, "flags": flags, "sin": sin, "ckT": ckT, "cv": cv})
        in_maps.append(m)
    return NS, in_maps


def kernel(**inputs):
    f32 = np.float32
    NS, in_maps = make_in_maps(**inputs)
    L = NS * 512 * 4
    SEG = L // 4
    nc, bld = _get_prog(NS)
    res = run_bass_kernel_spmd(nc, in_maps, core_ids=list(range(NCORES)))
    return assemble(res.results, L)


def assemble(R, L):
    f32 = np.float32
    SEG = L // 4
    y_prompt = np.zeros((2, L, D), f32)
    y_sample = np.zeros((32, 64, D), f32)
    s_p = np.zeros((1, 2, 4, 128, 128), f32)
    k_p = np.zeros((1, 2, 512, 8, 64), f32)
    v_p = np.zeros((1, 2, 512, 8, 64), f32)
    s_s = np.zeros((1, 32, 4, 128, 128), f32)
    k_s = np.zeros((1, 32, 64, 8, 64), f32)
    v_s = np.zeros((1, 32, 64, 8, 64), f32)
    for r in range(NCORES):
        b, kseg = r // 4, r % 4
        o = R[r]
        yT = np.asarray(o["yT"])
        y_sample[4 * r:4 * r + 4] = yT[:, 0:256].T.reshape(4, 64, D)
        y_prompt[b, kseg * SEG:(kseg + 1) * SEG] = yT[:, 256:].T
        so = np.asarray(o["s_out"])
        s_s[0, 4 * r:4 * r + 4] = so[:, 0:4].transpose(1, 2, 0, 3)
        kTs = np.asarray(o["kTs"])
        k_s[0, 4 * r:4 * r + 4] = kTs.transpose(2, 1, 0).reshape(4, 64, 8, 64)
        vs = np.asarray(o["vs"])
        v_s[0, 4 * r:4 * r + 4] = vs.transpose(1, 0, 2).reshape(4, 64, 8, 64)
        if kseg == 3:
            s_p[0, b] = so[:, 4].transpose(1, 0, 2)
            k_p[0, b] = np.asarray(o["kTp"]).transpose(2, 1, 0).reshape(512, 8, 64)
            v_p[0, b] = np.asarray(o["vp"]).transpose(1, 0, 2).reshape(512, 8, 64)
    return (y_prompt, y_sample, s_p, k_p, v_p, s_s, k_s, v_s)
```

```python
import numpy as np
from contextlib import ExitStack
import concourse.bass as bass
import concourse.mybir as mybir
from concourse.bass_utils import run_bass_kernel_spmd

F32 = mybir.dt.float32
BF16 = mybir.dt.bfloat16
AF = mybir.ActivationFunctionType
ALU = mybir.AluOpType

D = 1024
NCORES = 8
EPS = 1e-6
NEGM = -30000.0

ENGS = ("pe", "act", "dve", "pool", "sp")
NDMA_SEM = 16
SAME_ENGINE_SYNC = True
PS_EXCL = True
DBG_SETS = False
KSTOP = 1e9


class Op:
    __slots__ = ("eng", "fn", "dma", "idx", "deps", "waited", "sem", "val", "dq")

    def __init__(self, eng, fn, dma):
        self.eng = eng
        self.fn = fn
        self.dma = dma
        self.deps = []
        self.waited = False
        self.sem = None
        self.val = 0
        self.dq = -1


class Sched:
    def __init__(self, nc):
        self.nc = nc
        self.ops = []
        self.by_eng = {e: [] for e in ENGS}
        self.state = {}

    def _ents(self, name, sub):
        d = self.state.setdefault(name, {})
        if sub is None:
            if None not in d:
                d[None] = [None, []]
            return list(d.values())
        if sub not in d:
            w = d[None][0] if None in d else None
            d[sub] = [w, []]
        ents = [d[sub]]
        if None in d:
            ents.append(d[None])
        return ents

    @staticmethod
    def _k(key):
        return key if isinstance(key, tuple) else (key, None)

    def add(self, eng, fn, reads=(), writes=(), dma=False):
        op = Op(eng, fn, dma)
        op.idx = len(self.ops)
        deps = {}
        if PS_EXCL and eng != "pe":
            writes = list(writes) + [k_ for k_ in reads if isinstance(k_, tuple) and k_[0] == "ps"]
        for key in reads:
            name, sub = self._k(key)
            for ent in self._ents(name, sub):
                if ent[0] is not None:
                    deps[ent[0].idx] = ent[0]
        for key in writes:
            name, sub = self._k(key)
            for ent in self._ents(name, sub):
                if ent[0] is not None:
                    deps[ent[0].idx] = ent[0]
                for r in ent[1]:
                    deps[r.idx] = r
        for key in reads:
            name, sub = self._k(key)
            d = self.state[name]
            if sub is None:
                for ent in d.values():
                    ent[1].append(op)
            else:
                d[sub][1].append(op)
        for key in writes:
            name, sub = self._k(key)
            d = self.state[name]
            if sub is None:
                for ent in d.values():
                    ent[0] = op
                    ent[1] = []
            else:
                d[sub][0] = op
                d[sub][1] = []
        deps.pop(op.idx, None)
        op.deps = [deps[i] for i in sorted(deps)]
        self.by_eng[eng].append(op)
        self.ops.append(op)
        return op

    @staticmethod
    def _need_wait(op, d):
        if d.dma:
            return True
        if d.eng == op.eng:
            if op.eng == "pe":
                return False
            if op.dma:
                return True
            return SAME_ENGINE_SYNC
        return True

    def emit(self, sems, dma_sems):
        nc = self.nc
        for op in self.ops:
            for d in op.deps:
                if self._need_wait(op, d):
                    d.waited = True
        final_waits = []
        for e in ENGS:
            cnt = 0
            dcnt = 0
            for op in self.by_eng[e]:
                if op.dma:
                    op.dq = dcnt
                    op.sem = dma_sems[e][dcnt % NDMA_SEM]
                    op.val = 16 * (dcnt // NDMA_SEM + 1)
                    dcnt += 1
                elif op.waited:
                    cnt += 1
                    op.sem = sems[e]
                    op.val = cnt
            for i in range(min(dcnt, NDMA_SEM)):
                uses = (dcnt - 1 - i) // NDMA_SEM + 1
                final_waits.append((dma_sems[e][i], 16 * uses))
        stats = {}

        def run(e):
            def body(eng):
                seen = {}
                nw = 0
                for op in self.by_eng[e]:
                    waits = []
                    if op.dma and op.dq >= NDMA_SEM:
                        waits.append((op.sem, op.val - 16))
                    for d in op.deps:
                        if self._need_wait(op, d):
                            waits.append((d.sem, d.val))
                    best = {}
                    for s, v in waits:
                        kk = id(s)
                        if seen.get(kk, 0) >= v:
                            continue
                        if kk not in best or best[kk][1] < v:
                            best[kk] = (s, v)
                    for kk, (s, v) in best.items():
                        eng.wait_ge(s, v)
                        seen[kk] = v
                        nw += 1
                    ins = op.fn(eng)
                    if op.dma:
                        ins.then_inc(op.sem, 16)
                    elif op.waited:
                        ins.then_inc(op.sem, 1)
                if e == "sp":
                    for s, v in final_waits:
                        eng.wait_ge(s, v)
                stats[e] = (len(self.by_eng[e]), nw)
            return body

        with nc.Block() as block:
            block.tensor(run("pe"))
            block.scalar(run("act"))
            block.vector(run("dve"))
            block.gpsimd(run("pool"))
            block.sync(run("sp"))
        return stats


class _Stop(Exception):
    pass


class Builder:
    def chk(self, n):
        if n > KSTOP:
            raise _Stop()

    def __init__(self, NS):
        self.NS = NS
        self.NP = NS * 512
        self.NTOK = 256 + 512 + self.NP
        self.NOUT = 256 + self.NP
        self.nc = bass.Bass("TRN2", target_bir_lowering=False)
        self.es = ExitStack()

    def sb(self, name, shape, dt):
        return self.es.enter_context(self.nc.sbuf_tensor(name, shape, dt))

    def din(self, name, shape):
        return self.nc.dram_tensor(name, shape, F32, kind="ExternalInput").ap()

    def dout(self, name, shape):
        return self.nc.dram_tensor(name, shape, F32, kind="ExternalOutput").ap()

    def op(self, eng, fn, r=(), w=(), dma=False):
        return self.S.add(eng, fn, reads=r, writes=w, dma=dma)

    def mm(self, out, lhsT, rhs, start, stop, r, w):
        self.op("pe", lambda e: e.matmul(out, lhsT=lhsT, rhs=rhs, start=start, stop=stop), r, w)

    def act(self, out, in_, func, r, w, scale=1.0, bias=None):
        if bias is None:
            self.op("act", lambda e: e.activation(out=out, in_=in_, func=func, scale=scale), r, w)
        else:
            self.op("act", lambda e: e.activation(out=out, in_=in_, func=func, scale=scale, bias=bias), r, w)

    def tt(self, out, in0, in1, op, r, w, eng="dve"):
        self.op(eng, lambda e: e.tensor_tensor(out=out, in0=in0, in1=in1, op=op), r, w)

    def ts(self, out, in0, s1, s2, op0, op1, r, w, eng="dve"):
        if s2 is None:
            self.op(eng, lambda e: e.tensor_scalar(out=out, in0=in0, scalar1=s1, scalar2=None, op0=op0), r, w)
        else:
            self.op(eng, lambda e: e.tensor_scalar(out=out, in0=in0, scalar1=s1, scalar2=s2, op0=op0, op1=op1), r, w)

    def stt(self, out, in0, scalar, in1, op0, op1, r, w):
        self.op("dve", lambda e: e.scalar_tensor_tensor(out=out, in0=in0, scalar=scalar, in1=in1, op0=op0, op1=op1), r, w)

    def cp(self, out, in_, r, w, eng="dve"):
        if eng == "act":
            self.op("act", lambda e: e.activation(out=out, in_=in_, func=AF.Copy), r, w)
        else:
            self.op(eng, lambda e: e.tensor_copy(out=out, in_=in_), r, w)

    def memset(self, ap, val, w, eng="pool"):
        self.op(eng, lambda e: e.memset(ap, val), (), w)

    def dma(self, eng, out, in_, r, w):
        self.op(eng, lambda e: e.dma_start(out=out, in_=in_, max_dma_last_dim=2048), r, w, dma=True)

    def pa(self):
        assert self.pfree, "out of PSUM banks"
        return self.pfree.pop(0)

    def pf(self, b):
        self.pfree.append(b)

    def PB(self, b):
        return self.ps[b]

    def wld(self, name, blk, k0=0, kn=8, width=512):
        slot = self.wslot
        self.wslot = (self.wslot + 1) % len(self.wb)
        t = self.wb[slot]
        src = self.dr[name]
        key = ("wb", slot)
        self.dma("pool", t[:, 0:kn, 0:width], src[blk, :, k0:k0 + kn, 0:width], (), [key])
        return t, key

    def build(self):
        nc = self.nc
        with self.es:
            self._declare()
            self.S = Sched(nc)
            self._setup()
            sts = []
            sts.append(dict(kind="sample", col0=0, ncol=256, ocol=0,
                            segs=[(i * 64, 64, 1 + i) for i in range(4)], cur=0))
            sts.append(dict(kind="halo", col0=256, ncol=512, segs=[(0, 512, 0)], cur=1))
            for s in range(self.NS):
                sts.append(dict(kind="prompt", col0=768 + 512 * s, ncol=512, ocol=256 + 512 * s,
                                segs=[(0, 512, 0)], cur=s % 2, first=(s == 0), last=(s == self.NS - 1)))
            try:
                self.chk(1)
                for st in sts:
                    self._supertile(st)
                self._finish()
            except _Stop:
                pass
            sems = {e: self.es.enter_context(nc.semaphore("s_" + e)) for e in ENGS}
            dsems = {e: [self.es.enter_context(nc.semaphore("d_%s%d" % (e, i))) for i in range(NDMA_SEM)]
                     for e in ENGS}
            self.stats = self.S.emit(sems, dsems)
        return nc

    def _declare(self):
        nc = self.nc
        dr = {}
        dr["xT"] = self.din("xT", [D, self.NTOK])
        dr["cT"] = self.din("cT", [128, 8, 8])
        dr["flags"] = self.din("flags", [128, 4])
        dr["sin"] = self.din("sin", [128, 4, 4, 128])
        dr["ckT"] = self.din("ckT", [4, 128, 4, 512])
        dr["cv"] = self.din("cv", [4, 128, 4, 512])
        dr["w_ada"] = self.din("w_ada", [12, 128, 8, 512])
        dr["w_in"] = self.din("w_in", [11, 128, 8, 512])
        dr["w_br"] = self.din("w_br", [2, 128, 8, 512])
        dr["w_out"] = self.din("w_out", [2, 128, 8, 512])
        dr["w_fi"] = self.din("w_fi", [12, 128, 8, 512])
        dr["w_fo"] = self.din("w_fo", [2, 128, 22, 512])
        dr["b_ada"] = self.din("b_ada", [128, 48])
        dr["nmix"] = self.din("nmix", [128, 8])
        dr["nffn"] = self.din("nffn", [128, 8])
        dr["nfin"] = self.din("nfin", [128, 8])
        dr["onorm"] = self.din("onorm", [128, 4])
        dr["lbl"] = self.din("lbl", [128, 4, 2])
        dr["rb"] = self.din("rb", [8, 192])
        dr["yT"] = self.dout("yT", [D, self.NOUT])
        dr["s_out"] = self.dout("s_out", [128, 5, 4, 128])
        dr["kTs"] = self.dout("kTs", [128, 4, 256])
        dr["vs"] = self.dout("vs", [128, 2, 512])
        dr["kTp"] = self.dout("kTp", [128, 4, 512])
        dr["vp"] = self.dout("vp", [128, 4, 512])
        self.ext_d = nc.dram_tensor("ext_d", [8, 383], F32)
        self.dr = dr
        sb = self.sb
        self.onesf = sb("onesf", [128, 128], F32)
        self.identb = sb("identb", [128, 128], BF16)
        self.onesb = sb("onesb", [128, 128], BF16)
        self.resetm = sb("resetm", [128, 512], F32)
        self.trim = sb("trim", [128, 4, 128], F32)
        self.epsc = sb("epsc", [128, 1], F32)
        self.cT = sb("cT_sb", [128, 8, 8], F32)
        self.scT = sb("scT", [128, 8, 8], BF16)
        self.flags = sb("flags_sb", [128, 4], F32)
        self.b_ada = sb("b_ada_sb", [128, 48], F32)
        self.nmix = sb("nmix_sb", [128, 8], F32)
        self.nffn = sb("nffn_sb", [128, 8], F32)
        self.nfin = sb("nfin_sb", [128, 8], F32)
        self.onorm = sb("onorm_sb", [128, 4], F32)
        self.lbl = sb("lbl_sb", [128, 4, 2], F32)
        self.lbd = sb("lbd", [128, 4], F32)
        self.lb = sb("lb", [128, 4], F32)
        self.oml = sb("oml", [128, 4], F32)
        self.modT = sb("modT", [128, 48, 8], F32)
        self.gm1 = sb("gm1", [128, 8, 8], F32)
        self.gm2 = sb("gm2", [128, 8, 8], F32)
        self.BT3 = sb("BT3", [128, 8, 128], F32)
        self.BT4 = sb("BT4", [128, 8, 128], F32)
        self.BT5 = sb("BT5", [128, 8, 64], F32)
        self.BT0 = sb("BT0", [128, 128], F32)
        self.xT = sb("xT_sb", [128, 8, 512], F32)
        self.hT = sb("hT", [128, 8, 512], BF16)
        self.mT = sb("mT", [128, 8, 512], BF16)
        self.rstd = sb("rstd", [128, 512], F32)
        self.wb = [sb("wb%d" % i, [128, 8, 512], BF16) for i in range(4)]
        self.wslot = 0
        self.arena = sb("arena", [128, 22 * 512], BF16)
        ar32 = self.arena.bitcast(F32)
        self.sT = self.arena[:, :].rearrange("p (j t) -> p j t", t=512)
        self.sqa4 = ar32[:, 0:2048].rearrange("p (h t) -> p h t", t=512)
        self.fT4 = ar32[:, 2048:4096].rearrange("p (h t) -> p h t", t=512)
        self.kTf = ar32[:, 0:2048].rearrange("p (h t) -> p h t", t=512)
        self.vf = ar32[:, 2048:4096].rearrange("p (h t) -> p h t", t=512)
        self.lgf = sb("lgf", [128, 512], F32)
        self.cum = sb("cum", [128, 512], F32)
        self.dd = self.lgf
        self.e1 = sb("e1", [128, 512], F32)
        self.e2 = self.lgf
        self.qpT = sb("qpT", [128, 4, 512], BF16)
        self.kpT = sb("kpT", [128, 4, 512], BF16)
        self.vtok = sb("vtok", [128, 4, 512], BF16)
        self.sga = sb("sga", [128, 4, 512], BF16)
        self.AT = sb("AT", [128, 4, 512], BF16)
        self.ktokA = sb("ktokA", [128, 4, 4, 128], BF16)
        self.ktokB = sb("ktokB", [128, 4, 4, 128], BF16)
        self.Sst = sb("Sst", [128, 4, 128], F32)
        self.sin = sb("sin_sb", [128, 4, 4, 128], F32)
        self.Ssc = sb("Ssc", [128, 8, 128], BF16)
        self.tmpS = sb("tmpS", [128, 8, 128], F32)
        self.eref = sb("eref", [128, 4, 8], F32)
        self.etot = sb("etot", [128, 4, 8], F32)
        self.etr = sb("etr", [128, 4, 8], F32)
        self.oaT = sb("oaT", [128, 4, 512], BF16)
        self.sqo = sb("sqo", [128, 4, 512], BF16)
        self.rso = self.e1
        self.t1 = sb("t1", [128, 512], F32)
        self.t2 = sb("t2", [128, 512], F32)
        self.rbt = self.t2[0:8, 0:192]
        self.ext = self.t1[0:8, 0:383]
        self.sa = sb("sa", [128, 512], F32)
        self.sbg = sb("sbg", [128, 512], F32)
        self.identf = self.sa[:, 0:128]
        self.Jf = self.sbg[:, 0:128]
        m32 = self.mT.bitcast(F32)[:, :, :].rearrange("p a b -> p (a b)")
        self.hset = [(self.lgf, self.cum, self.e1),
                     (ar32[:, 4096:4608], ar32[:, 4608:5120], ar32[:, 5120:5632]),
                     (m32[:, 0:512], m32[:, 512:1024], m32[:, 1024:1536]),
                     (self.sa, self.sbg, self.rstd)]
        self.hkeys = [(["lgf"], ["cum"], ["e1"]),
                      ([("arena", 16), ("arena", 17)], [("arena", 18), ("arena", 19)], [("arena", 20), ("arena", 21)]),
                      ([("mT", 0), ("mT", 1)], [("mT", 2), ("mT", 3)], [("mT", 4), ("mT", 5)]),
                      (["sa"], ["sbg"], ["rstd"])]
        if DBG_SETS:
            self.hset[2], self.hkeys[2] = self.hset[0], self.hkeys[0]
            self.hset[3], self.hkeys[3] = self.hset[1], self.hkeys[1]
        self.qTA = sb("qTA", [128, 4, 512], BF16)
        self.qTB = sb("qTB", [128, 4, 512], BF16)
        self.kT = [sb("kT%d" % i, [128, 4, 512], BF16) for i in range(2)]
        self.Va = [sb("Va%d" % i, [128, 4, 8, 65], BF16) for i in range(2)]
        self.sbias = sb("sbias", [128, 3, 384], F32)
        self.PTf = [sb("PTf%d" % i, [128, 5, 128], BF16) for i in range(3)]
        self.PTh = [sb("PTh%d" % i, [128, 5, 128], BF16) for i in range(2)]
        self.rden = sb("rden", [128, 8], F32)
        self.ob = self.AT
        self.obT = self.vtok
        self.ps = [self.es.enter_context(nc.psum_tensor("ps%d" % i, [128, 512], F32)) for i in range(8)]
        self.pfree = list(range(8))

    def _setup(self):
        dr = self.dr
        for name, t in (("cT", self.cT), ("flags", self.flags), ("b_ada", self.b_ada), ("nmix", self.nmix),
                        ("nffn", self.nffn), ("nfin", self.nfin), ("onorm", self.onorm), ("lbl", self.lbl),
                        ("rb", self.rbt), ("sin", self.sin)):
            self.dma("sp", t[:], dr[name], (), [t.name])
        self.memset(self.onesf[:], 1.0, ["onesf"])
        self.memset(self.onesb[:], 1.0, ["onesb"])
        self.memset(self.epsc[:], EPS, ["epsc"])
        self.memset(self.identf[:], 0.0, ["sa"])
        self.op("pool", lambda e: e.affine_select(out=self.identf[:], in_=self.onesf[:], pattern=[[-1, 128]],
                                                  compare_op=ALU.is_equal, fill=0.0, base=0, channel_multiplier=1),
                ["onesf"], ["sa"])
        self.cp(self.identb[:], self.identf[:], ["sa"], ["identb"])
        self.memset(self.Jf[:], 0.0, ["sbg"])
        self.op("pool", lambda e: e.affine_select(out=self.Jf[:], in_=self.onesf[:], pattern=[[1, 128]],
                                                  compare_op=ALU.is_equal, fill=0.0, base=-127, channel_multiplier=1),
                ["onesf"], ["sbg"])
        self.memset(self.resetm[:], 1.0, ["resetm"])
        self.memset(self.resetm[:].rearrange("p (c t) -> p c t", t=64)[:, :, 0:1], 0.0, ["resetm"])
        self.memset(self.trim[:], 1.0, ["trim"])
        self.op("pool", lambda e: e.affine_select(out=self.trim[:], in_=self.trim[:], pattern=[[0, 4], [1, 128]],
                                                  compare_op=ALU.is_ge, fill=0.0, base=0, channel_multiplier=-1),
                ["trim"], ["trim"])
        self.memset(self.trim[0:64, :, 64:128], 0.0, ["trim"])
        for t in (self.ktokA, self.ktokB, self.qTA, self.qTB, self.PTh[0], self.PTh[1]):
            self.memset(t[:], 0.0, [t.name])
        for t in self.Va:
            self.memset(t[:], 1.0, [t.name])
        self.memset(self.Sst[:], 0.0, ["Sst"])
        self.tt(self.lbd[:], self.lbl[:, :, 0], self.lbl[:, :, 1], ALU.subtract, ["lbl_sb"], ["lbd"])
        self.act(self.lb[:], self.lbd[:], AF.Sigmoid, ["lbd"], ["lb"])
        self.act(self.oml[:], self.lbd[:], AF.Sigmoid, ["lbd"], ["oml"], scale=-1.0)
        self.act(self.scT[:], self.cT[:], AF.Silu, ["cT_sb"], ["scT"])
        bm = self.pa()
        for blk in range(12):
            w, wk = self.wld("w_ada", blk)
            for t4 in range(4):
                t = blk * 4 + t4
                for k in range(8):
                    self.mm(self.PB(bm)[:, t * 8:(t + 1) * 8], w[:, k, t4 * 128:(t4 + 1) * 128], self.scT[:, k, :],
                            k == 0, k == 7, [wk, "scT"], [("ps", bm)])
        self.tt(self.modT[:], self.PB(bm)[:, 0:384].rearrange("p (t s) -> p t s", s=8),
                self.b_ada[:].unsqueeze(2).to_broadcast([128, 48, 8]), ALU.add, [("ps", bm), "b_ada_sb"], ["modT"])
        self.pf(bm)
        self.stt(self.gm1[:], self.modT[:, 8:16, :], 1.0, self.nmix[:].unsqueeze(2).to_broadcast([128, 8, 8]),
                 ALU.add, ALU.mult, ["modT", "nmix_sb"], ["gm1"])
        self.stt(self.gm2[:], self.modT[:, 32:40, :], 1.0, self.nffn[:].unsqueeze(2).to_broadcast([128, 8, 8]),
                 ALU.add, ALU.mult, ["modT", "nffn_sb"], ["gm2"])
        self.memset(self.ext[:], 0.0, ["t1"])
        self.ts(self.ext[:, 64:256], self.rbt[:], self.rbt[:, 191:192], None, ALU.subtract, None, ["t2"], ["t1"])
        self.ts(self.ext[:, 0:64], self.ext[:, 256:320], self.rbt[:, 0:1], self.rbt[:, 191:192], ALU.add, ALU.subtract,
                ["t2", "t1"], ["t1"])
        self.dma("sp", self.ext_d.ap(), self.ext[:], ["t1"], ["ext_d"])
        hank = self.sqa4
        for which, off, nq, BT in ((3, 128, 128, self.BT3), (4, 0, 128, self.BT4), (5, 64, 64, self.BT5)):
            hv = self.arena.bitcast(F32)[:, 0:8 * nq].rearrange("p (h q) -> p h q", q=nq)
            src = bass.AP(self.ext_d, off, [[1, 128], [383, 8], [1, nq]])
            self.dma("sp", hv, src, ["ext_d"], [("arena", u) for u in range(8)])
            nb = (8 * nq) // 512
            for b2 in range(nb):
                bk = self.pa()
                hpb = 512 // nq
                for hh in range(hpb):
                    h = b2 * hpb + hh
                    self.mm(self.PB(bk)[:, hh * nq:(hh + 1) * nq], self.Jf[:], hv[:, h, :], True, True,
                            ["sbg"] + [("arena", u) for u in range(8)], [("ps", bk)])
                self.cp(BT[:, b2 * hpb:(b2 + 1) * hpb, :], self.PB(bk)[:, :].rearrange("p (h q) -> p h q", q=nq),
                        [("ps", bk)], [BT.name])
                self.pf(bk)
        self.memset(self.BT0[:], 0.0, ["BT0"])
        self.memset(self.BT0[0:64, 64:128], NEGM, ["BT0"])
        self.memset(self.BT4[64:128, :, 0:64], NEGM, ["BT4"])
        self.memset(self.BT5[0:64, :, :], NEGM, ["BT5"])

    def _norm_stats(self, ncol):
        sq = self.mT
        b = self.pa()
        for k in range(8):
            self.act(sq[:, k, 0:ncol], self.xT[:, k, 0:ncol], AF.Square, [("xT_sb", k)], [("mT", k)])
            self.mm(self.PB(b)[:, 0:ncol], self.onesb[:], sq[:, k, 0:ncol], k == 0, k == 7, ["onesb", ("mT", k)], [("ps", b)])
        self.act(self.rstd[:, 0:ncol], self.PB(b)[:, 0:ncol], AF.Ln, [("ps", b), "epsc"], ["rstd"], scale=1.0 / D,
                 bias=self.epsc[:, 0:1])
        self.act(self.PB(b)[:, 0:ncol], self.rstd[:, 0:ncol], AF.Exp, ["rstd"], [("ps", b)], scale=-0.5)
        return b

    def _norm_apply(self, st, gm, sh_off, rb_):
        i = 0
        for k in range(8):
            for (c0, n, seq) in st["segs"]:
                sl = i % 2
                i += 1
                tb = self.t1 if sl == 0 else self.t2
                self.stt(tb[:, 0:n], self.xT[:, k, c0:c0 + n], gm[:, k, seq:seq + 1], self.PB(rb_)[:, c0:c0 + n],
                         ALU.mult, ALU.mult, [("xT_sb", k), ("ps", rb_), gm.name], [tb.name])
                self.act(self.hT[:, k, c0:c0 + n], tb[:, 0:n], AF.Identity, [tb.name, "modT"],
                         [("hT", k)], bias=self.modT[:, sh_off + k, seq:seq + 1])
        self.pf(rb_)

    def _supertile(self, st):
        dr = self.dr
        ncol = st["ncol"]
        kind = st["kind"]
        cur = st["cur"]
        ntt = ncol // 128
        nch = ncol // 64
        segs = st["segs"]
        xsrc = dr["xT"].rearrange("(k p) n -> p k n", p=128)[:, :, st["col0"]:st["col0"] + ncol]
        for k in range(8):
            self.dma("sp", self.xT[:, k, 0:ncol], xsrc[:, k, :], (), [("xT_sb", k)])
        self.chk(2)
        rb_ = self._norm_stats(ncol)
        self.chk(3)
        self._norm_apply(st, self.gm1, 0, rb_)
        self.chk(4)
        want_kv = kind == "sample" or (kind == "prompt" and st["last"])

        self._hgrn(st, kind == "halo")

        self.chk(6)
        if kind != "halo":
            w, wk = self.wld("w_in", 4)
            for hp in range(4):
                b = self.pa()
                for k in range(8):
                    self.mm(self.PB(b)[:, 0:ncol], w[:, k, hp * 128:(hp + 1) * 128], self.hT[:, k, 0:ncol], k == 0, k == 7,
                            [wk, ("hT", k)], [("ps", b)])
                self.act(self.qTA[0:64, hp, 0:ncol], self.PB(b)[0:64, 0:ncol], AF.Identity, [("ps", b)], [("qTA", hp)], scale=0.125)
                self.ts(self.qTB[64:128, hp, 0:ncol], self.PB(b)[64:128, 0:ncol], 0.125, None, ALU.mult, None,
                        [("ps", b)], [("qTB", hp)])
                self.pf(b)
        self.chk(6.1)
        w, wk = self.wld("w_in", 5)
        for hp in range(4):
            b = self.pa()
            for k in range(8):
                self.mm(self.PB(b)[:, 0:ncol], w[:, k, hp * 128:(hp + 1) * 128], self.hT[:, k, 0:ncol], k == 0, k == 7,
                        [wk, ("hT", k)], [("ps", b)])
            self.cp(self.kT[cur][:, hp, 0:ncol], self.PB(b)[:, 0:ncol], [("ps", b)], [(self.kT[cur].name, hp)], eng="act")
            if want_kv:
                self.cp(self.kTf[:, hp, 0:ncol], self.PB(b)[:, 0:ncol], [("ps", b)], [("arena", 2 * hp), ("arena", 2 * hp + 1)])
            self.pf(b)
        self.chk(6.2)
        w, wk = self.wld("w_in", 6)
        if kind != "halo":
            self.memset(self.Va[cur][:, :, :, 64:65], 1.0, [self.Va[cur].name])
        for tt_ in range(ntt):
            b = self.pa()
            for k in range(8):
                self.mm(self.PB(b)[:, 0:512], self.hT[:, k, tt_ * 128:(tt_ + 1) * 128], w[:, k, :], k == 0, k == 7,
                        [wk, ("hT", k)], [("ps", b)])
            self.cp(self.Va[cur][:, tt_, :, 0:64], self.PB(b)[:, :].rearrange("p (h d) -> p h d", d=64), [("ps", b)],
                    [(self.Va[cur].name, tt_)], eng="act")
            if want_kv:
                self.cp(self.vf[:, tt_, :], self.PB(b)[:, :], [("ps", b)], [("arena", 8 + 2 * tt_), ("arena", 9 + 2 * tt_)])
            self.pf(b)
        self.chk(6.4)
        if want_kv:
            ko, vo = ("kTs", "vs") if kind == "sample" else ("kTp", "vp")
            self.dma("sp", dr[ko], self.kTf[:, :, 0:ncol], [("arena", u) for u in range(8)], [ko])
            self.dma("sp", dr[vo], self.vf[:, 0:ntt, :], [("arena", u) for u in range(8, 16)], [vo])
        if kind == "halo":
            va = self.Va[cur]
            self.ts(va[:], va[:], self.flags[:, 0:1], None, ALU.mult, None, [va.name, "flags_sb"], [va.name])
            return

        self.chk(7)
        self._attention(st)
        self.chk(8)

        for cb in range(2):
            wbr, kbr = self.wld("w_br", cb)
            wga, kga = self.wld("w_in", 7 + cb)
            wgb, kgb = self.wld("w_in", 9 + cb)
            for m4 in range(4):
                m = cb * 4 + m4
                cs = slice(m4 * 128, (m4 + 1) * 128)
                bya, byb, bga, bgb = self.pa(), self.pa(), self.pa(), self.pa()
                for k in range(4):
                    self.mm(self.PB(bya)[:, 0:ncol], wbr[:, k, cs], self.oaT[:, k, 0:ncol], k == 0, k == 3,
                            [kbr, ("oaT", k)], [("ps", bya)])
                for k in range(4):
                    self.mm(self.PB(byb)[:, 0:ncol], wbr[:, 4 + k, cs], self.obT[:, k, 0:ncol], k == 0, k == 3,
                            [kbr, ("vtok", k)], [("ps", byb)])
                for k in range(8):
                    self.mm(self.PB(bga)[:, 0:ncol], wga[:, k, cs], self.hT[:, k, 0:ncol], k == 0, k == 7,
                            [kga, ("hT", k)], [("ps", bga)])
                for k in range(8):
                    self.mm(self.PB(bgb)[:, 0:ncol], wgb[:, k, cs], self.hT[:, k, 0:ncol], k == 0, k == 7,
                            [kgb, ("hT", k)], [("ps", bgb)])
                self.act(self.sa[:, 0:ncol], self.PB(bga)[:, 0:ncol], AF.Sigmoid, [("ps", bga)], ["sa"])
                self.act(self.sbg[:, 0:ncol], self.PB(bgb)[:, 0:ncol], AF.Sigmoid, [("ps", bgb)], ["sbg"])
                self.pf(bga)
                self.pf(bgb)
                self.tt(self.t1[:, 0:ncol], self.PB(bya)[:, 0:ncol], self.sa[:, 0:ncol], ALU.mult, [("ps", bya), "sa"], ["t1"])
                self.tt(self.t2[:, 0:ncol], self.PB(byb)[:, 0:ncol], self.sbg[:, 0:ncol], ALU.mult, [("ps", byb), "sbg"], ["t2"])
                self.pf(bya)
                self.pf(byb)
                self.tt(self.mT[:, m, 0:ncol], self.t1[:, 0:ncol], self.t2[:, 0:ncol], ALU.add, ["t1", "t2"], [("mT", m)])
        self.chk(9)
        for cb in range(2):
            w, wk = self.wld("w_out", cb)
            for m4 in range(4):
                m = cb * 4 + m4
                b = self.pa()
                for k in range(8):
                    self.mm(self.PB(b)[:, 0:ncol], w[:, k, m4 * 128:(m4 + 1) * 128], self.mT[:, k, 0:ncol], k == 0, k == 7,
                            [wk, ("mT", k)], [("ps", b)])
                for (c0, n, seq) in segs:
                    self.stt(self.xT[:, m, c0:c0 + n], self.PB(b)[:, c0:c0 + n], self.modT[:, 16 + m, seq:seq + 1],
                             self.xT[:, m, c0:c0 + n], ALU.mult, ALU.add, [("ps", b), "modT", ("xT_sb", m)], [("xT_sb", m)])
                self.pf(b)
        self.chk(10)
        rb_ = self._norm_stats(ncol)
        self._norm_apply(st, self.gm2, 24, rb_)
        for bi in range(6):
            width = 512 if bi < 5 else 256
            wa, ka = self.wld("w_fi", bi, width=width)
            wu, ku = self.wld("w_fi", 6 + bi, width=width)
            for j4 in range(width // 128):
                j = bi * 4 + j4
                cs = slice(j4 * 128, (j4 + 1) * 128)
                ba, bu = self.pa(), self.pa()
                for k in range(8):
                    self.mm(self.PB(ba)[:, 0:ncol], wa[:, k, cs], self.hT[:, k, 0:ncol], k == 0, k == 7,
                            [ka, ("hT", k)], [("ps", ba)])
                for k in range(8):
                    self.mm(self.PB(bu)[:, 0:ncol], wu[:, k, cs], self.hT[:, k, 0:ncol], k == 0, k == 7,
                            [ku, ("hT", k)], [("ps", bu)])
                sl = j % 2
                tsl = self.t1 if sl == 0 else self.t2
                self.act(tsl[:, 0:ncol], self.PB(ba)[:, 0:ncol], AF.Silu, [("ps", ba)], [tsl.name])
                self.pf(ba)
                self.tt(self.sT[:, j, 0:ncol], self.PB(bu)[:, 0:ncol], tsl[:, 0:ncol], ALU.mult, [("ps", bu), tsl.name],
                        [("arena", j)])
                self.pf(bu)
        for cb in range(2):
            banks = [self.pa() for _ in range(4)]
            for (k0, kn) in ((0, 8), (8, 8), (16, 6)):
                w, wk = self.wld("w_fo", cb, k0=k0, kn=kn)
                for m4 in range(4):
                    for kk in range(kn):
                        kg = k0 + kk
                        self.mm(self.PB(banks[m4])[:, 0:ncol], w[:, kk, m4 * 128:(m4 + 1) * 128], self.sT[:, kg, 0:ncol],
                                kg == 0, kg == 21, [wk, ("arena", kg)], [("ps", banks[m4])])
            for m4 in range(4):
                m = cb * 4 + m4
                b = banks[m4]
                for (c0, n, seq) in segs:
                    self.stt(self.xT[:, m, c0:c0 + n], self.PB(b)[:, c0:c0 + n], self.modT[:, 40 + m, seq:seq + 1],
                             self.xT[:, m, c0:c0 + n], ALU.mult, ALU.add, [("ps", b), "modT", ("xT_sb", m)], [("xT_sb", m)])
                self.pf(b)
        self.chk(11)
        rb_ = self._norm_stats(ncol)
        ydst = dr["yT"].rearrange("(k p) n -> p k n", p=128)[:, :, st["ocol"]:st["ocol"] + ncol]
        for k in range(8):
            self.stt(self.xT[:, k, 0:ncol], self.xT[:, k, 0:ncol], self.nfin[:, k:k + 1], self.PB(rb_)[:, 0:ncol],
                     ALU.mult, ALU.mult, [("xT_sb", k), ("ps", rb_), "nfin_sb"], [("xT_sb", k)])
            self.dma("sp", ydst[:, k, :], self.xT[:, k, 0:ncol], [("xT_sb", k)], [("yT", k)])
        self.pf(rb_)

    def _hgrn(self, st, so=False):
        ncol = st["ncol"]
        kind = st["kind"]
        ntt = ncol // 128
        nch = ncol // 64
        npair = ncol // 128
        wblk = {}
        for blk in range(3):
            wblk[blk] = self.wld("w_in", blk)
        for h in range(4):
            for which in ((1,) if so else (0, 1, 2)):
                idx = 3 * h + which
                w, wk = wblk[idx // 4]
                cs = slice((idx % 4) * 128, (idx % 4 + 1) * 128)
                b = self.pa()
                for k in range(8):
                    self.mm(self.PB(b)[:, 0:ncol], w[:, k, cs], self.hT[:, k, 0:ncol], k == 0, k == 7,
                            [wk, ("hT", k)], [("ps", b)])
                if which == 0:
                    self.act(self.sqa4[:, h, 0:ncol], self.PB(b)[:, 0:ncol], AF.Silu, [("ps", b)],
                             [("arena", 2 * h), ("arena", 2 * h + 1)])
                elif which == 1:
                    self.act(self.fT4[:, h, 0:ncol], self.PB(b)[:, 0:ncol], AF.Sigmoid, [("ps", b)],
                             [("arena", 8 + 2 * h), ("arena", 9 + 2 * h)])
                else:
                    self.act(self.sga[:, h, 0:ncol], self.PB(b)[:, 0:ncol], AF.Silu, [("ps", b)], [("sga", h)])
                self.pf(b)
        w, wk = self.wld("w_in", 3)
        for tt_ in range(ntt):
            b = self.pa()
            for k in range(8):
                self.mm(self.PB(b)[:, 0:512], self.hT[:, k, tt_ * 128:(tt_ + 1) * 128], w[:, k, :], k == 0, k == 7,
                        [wk, ("hT", k)], [("ps", b)])
            self.cp(self.vtok[:, tt_, :], self.PB(b)[:, :], [("ps", b)], [("vtok", tt_)])
            self.pf(b)
        gens = [self._hg_pre(st, so, h) for h in range(4)]
        while gens:
            for g in list(gens):
                try:
                    next(g)
                except StopIteration:
                    gens.remove(g)
        bO = [None] * 4 if so else [self.pa() for _ in range(4)]
        for c in range(nch):
            p, half = c // 2, c % 2
            bKV = self.pa()
            kt = self.ktokA if half == 0 else self.ktokB
            for h in range(4):
                hs = slice(h * 128, (h + 1) * 128)
                self.mm(self.PB(bKV)[:, hs], kt[:, h, p, :], self.vtok[:, p, hs], True, True,
                        [(kt.name, h), ("vtok", p)], [("ps", bKV)])
            for h in range(4):
                hs = slice(h * 128, (h + 1) * 128)
                if kind == "sample":
                    Sap = self.sin[:, c, h, :]
                    skey = ("sin_sb", c * 4 + h)
                else:
                    Sap = self.Sst[:, h, :]
                    skey = ("Sst", h)
                sl = h * 2 + c % 2
                if not so:
                    self.act(self.Ssc[:, sl, :], Sap, AF.Identity, [skey, ("eref", h)], [("Ssc", sl)], scale=self.eref[:, h, c:c + 1])
                    if half == 0:
                        ps_ = slice(p * 128, (p + 1) * 128)
                        self.mm(self.PB(bO[h])[:, ps_], self.vtok[:, p, hs], self.AT[:, h, ps_], True, False,
                                [("vtok", p), ("AT", h)], [("ps", bO[h])])
                    cs = slice(c * 64, (c + 1) * 64)
                    self.mm(self.PB(bO[h])[:, cs], self.Ssc[:, sl, :], self.qpT[:, h, cs], False, half == 1,
                            [("Ssc", sl), ("qpT", h)], [("ps", bO[h])])
                self.act(self.tmpS[:, sl, :], Sap, AF.Identity, [skey, ("etot", h)], [("tmpS", sl)], scale=self.etot[:, h, c:c + 1])
                self.stt(Sap, self.PB(bKV)[:, hs], self.etr[:, h, c:c + 1], self.tmpS[:, sl, :], ALU.mult, ALU.add,
                         [("ps", bKV), ("etr", h), ("tmpS", sl)], [skey])
            self.pf(bKV)
        if not so:
            gens = [self._hg_post(st, h, bO[h]) for h in range(4)]
            while gens:
                for g in list(gens):
                    try:
                        next(g)
                    except StopIteration:
                        gens.remove(g)
        if so:
            self.ts(self.Sst[:], self.Sst[:], self.flags[:, 0:1], None, ALU.mult, None, ["Sst", "flags_sb"], ["Sst"])
        if kind == "sample":
            self.dma("sp", self.dr["s_out"][:, 0:4, :, :], self.sin[:], ["sin_sb"], ["s_out"])

    def _hg_pre(self, st, so, h):
        ncol = st["ncol"]
        kind = st["kind"]
        nch = ncol // 64
        npair = ncol // 128
        lgf, cum, e1 = self.hset[h]
        lk, ck, ek = self.hkeys[h]
        dd = lgf
        e2 = lgf
        fk = [("arena", 8 + 2 * h), ("arena", 9 + 2 * h)]
        qk = [("arena", 2 * h), ("arena", 2 * h + 1)]
        fT = self.fT4[:, h, 0:ncol]
        self.ts(fT, fT, self.oml[:, h:h + 1], self.lb[:, h:h + 1], ALU.mult, ALU.add, fk + ["oml", "lb"], fk)
        self.act(lgf[:, 0:ncol], fT, AF.Ln, fk, lk)
        yield
        self.op("dve", lambda e: e.tensor_tensor_scan(out=cum[:, 0:ncol], data0=self.resetm[:, 0:ncol],
                                                      data1=lgf[:, 0:ncol], initial=0.0, op0=ALU.mult, op1=ALU.add),
                ["resetm"] + lk, ck)
        self.act(fT, fT, AF.Identity, fk + ["onesf"], fk, scale=-1.0, bias=self.onesf[:, 0:1])
        yield
        cum3 = cum[:, 0:ncol].rearrange("p (c t) -> p c t", t=64)
        self.tt(dd[:, 0:ncol].rearrange("p (c t) -> p c t", t=64), cum3, cum3[:, :, 32:33].to_broadcast([128, nch, 64]),
                ALU.subtract, ck, lk)
        if not so:
            self.act(e1[:, 0:ncol], dd[:, 0:ncol], AF.Exp, lk, ek)
        self.act(e2[:, 0:ncol], dd[:, 0:ncol], AF.Exp, lk, lk, scale=-1.0)
        yield
        if not so:
            self.tt(self.qpT[:, h, 0:ncol], self.sqa4[:, h, 0:ncol], e1[:, 0:ncol], ALU.mult, qk + ek, [("qpT", h)])
        self.tt(self.kpT[:, h, 0:ncol], fT, e2[:, 0:ncol], ALU.mult, fk + lk, [("kpT", h)])
        if not so:
            self.act(self.eref[:, h, 0:nch], cum3[:, :, 32], AF.Exp, ck, [("eref", h)])
        self.act(self.etot[:, h, 0:nch], cum3[:, :, 63], AF.Exp, ck, [("etot", h)])
        self.tt(self.etr[:, h, 0:nch], cum3[:, :, 63], cum3[:, :, 32], ALU.subtract, ck, [("etr", h)])
        self.act(self.etr[:, h, 0:nch], self.etr[:, h, 0:nch], AF.Exp, [("etr", h)], [("etr", h)])
        yield
        if not so:
            bA = self.pa()
            for p in range(npair):
                ps_ = slice(p * 128, (p + 1) * 128)
                self.mm(self.PB(bA)[:, ps_], self.kpT[:, h, ps_], self.qpT[:, h, ps_], True, True,
                        [("kpT", h), ("qpT", h)], [("ps", bA)])
            self.tt(self.AT[:, h, 0:ncol], self.PB(bA)[:, 0:ncol], self.trim[:, 0:npair, :].rearrange("p a b -> p (a b)"), ALU.mult,
                    [("ps", bA), "trim"], [("AT", h)])
            self.pf(bA)
        bT = self.pa()
        for p in range(npair):
            ps_ = slice(p * 128, (p + 1) * 128)
            self.mm(self.PB(bT)[:, ps_], self.kpT[:, h, ps_], self.identb[:], True, True,
                    [("kpT", h), "identb"], [("ps", bT)])
        bT3 = self.PB(bT)[:, 0:ncol].rearrange("p (a b) -> p a b", b=128)
        self.cp(self.ktokA[0:64, h, 0:npair, :], bT3[0:64, :, :], [("ps", bT)], [("ktokA", h)], eng="act")
        self.cp(self.ktokB[64:128, h, 0:npair, :], bT3[64:128, :, :], [("ps", bT)], [("ktokB", h)])
        self.pf(bT)
        yield

    def _hg_post(self, st, h, bO):
        ncol = st["ncol"]
        e1 = self.hset[h][2]
        ek = self.hkeys[h][2]
        sqo = self.sqo[:, h, :]
        sk = ("sqo", h)
        tb = self.hset[h][0]
        tk = self.hkeys[h][0]
        self.act(sqo[:, 0:ncol], self.PB(bO)[:, 0:ncol], AF.Square, [("ps", bO)], [sk])
        bS = self.pa()
        self.mm(self.PB(bS)[:, 0:ncol], self.onesb[:], sqo[:, 0:ncol], True, True, ["onesb", sk], [("ps", bS)])
        self.act(e1[:, 0:ncol], self.PB(bS)[:, 0:ncol], AF.Ln, [("ps", bS), "epsc"], ek, scale=1.0 / 128,
                 bias=self.epsc[:, 0:1])
        self.pf(bS)
        yield
        self.act(e1[:, 0:ncol], e1[:, 0:ncol], AF.Exp, ek, ek, scale=-0.5)
        self.tt(tb[:, 0:ncol], self.PB(bO)[:, 0:ncol], e1[:, 0:ncol], ALU.mult, [("ps", bO)] + ek, tk)
        self.pf(bO)
        yield
        self.stt(self.oaT[:, h, 0:ncol], tb[:, 0:ncol], self.onorm[:, h:h + 1], self.sga[:, h, 0:ncol], ALU.mult, ALU.mult,
                 tk + ["onorm_sb", ("sga", h)], [("oaT", h)])

    def _attn_unit(self, q0, nq, qcol, tiles, PTs, ob_tt):
        nt = len(tiles)
        plain = [i for i, t in enumerate(tiles) if t["bias"] is None]
        biased = [i for i, t in enumerate(tiles) if t["bias"] is not None]
        nbi = len(biased)
        assert len(plain) <= 3 and nbi in (2, 3)
        npl = len(plain)
        assert plain == list(range(plain[0], plain[0] + npl)) and biased == list(range(biased[0], biased[0] + nbi))
        bO = [self.pa(), self.pa()]
        pipelined = isinstance(PTs, list)

        def stageA(h):
            hp = h // 2
            qsel = self.qTA if h % 2 == 0 else self.qTB
            qk = (qsel.name, hp)
            PT = PTs[h % 3] if pipelined else PTs
            b1 = self.pa()
            for n_, i in enumerate(plain):
                kb, kt = tiles[i]["k"]
                self.mm(self.PB(b1)[:, n_ * nq:(n_ + 1) * nq], self.kT[kb][:, hp, kt * 128:(kt + 1) * 128],
                        qsel[:, hp, qcol:qcol + nq], True, True, [(self.kT[kb].name, hp), qk], [("ps", b1)])
            b2 = self.pa()
            for n_, i in enumerate(biased):
                kb, kt = tiles[i]["k"]
                self.mm(self.PB(b2)[:, n_ * nq:(n_ + 1) * nq], self.kT[kb][:, hp, kt * 128:(kt + 1) * 128],
                        qsel[:, hp, qcol:qcol + nq], True, True, [(self.kT[kb].name, hp), qk], [("ps", b2)])
            self.act(PT[:, plain[0]:plain[0] + npl, q0:q0 + nq], self.PB(b1)[:, 0:npl * nq].rearrange("p (a b) -> p a b", b=nq),
                     AF.Exp, [("ps", b1)], [(PT.name, 0)])
            for i in plain:
                if tiles[i]["zero"]:
                    self.memset(PT[0:64, i, 64:128], 0.0, [(PT.name, 0)])
            self.pf(b1)
            sl = h % 3
            for n_, i in enumerate(biased):
                self.tt(self.sbias[:, sl, n_ * nq:(n_ + 1) * nq], self.PB(b2)[:, n_ * nq:(n_ + 1) * nq], tiles[i]["bias"](h), ALU.add,
                        [("ps", b2), "BT0", "BT3", "BT4", "BT5"], [("sbias", sl)])
            self.pf(b2)
            self.act(PT[:, biased[0]:biased[0] + nbi, q0:q0 + nq], self.sbias[:, sl, 0:nbi * nq].rearrange("p (a b) -> p a b", b=nq),
                     AF.Exp, [("sbias", sl)], [(PT.name, 0)])

        def stageB(h):
            PT = PTs[h % 3] if pipelined else PTs
            ob_ = bO[h // 4]
            oc = (h % 4) * 65
            for i, t in enumerate(tiles):
                vb, vt = t["k"]
                self.mm(self.PB(ob_)[:, oc:oc + 65], PT[:, i, :], self.Va[vb][:, vt, h, :], i == 0, i == nt - 1,
                        [(PT.name, 0), (self.Va[vb].name, vt)], [("ps", ob_)])

        if pipelined:
            stageA(0)
            stageA(1)
            for h in range(8):
                if h + 2 < 8:
                    stageA(h + 2)
                stageB(h)
        else:
            for h in range(8):
                stageA(h)
                stageB(h)
        rows = slice(q0, q0 + nq)
        for g in range(2):
            o3 = self.PB(bO[g])[rows, 0:260].rearrange("p (h d) -> p h d", d=65)
            self.op("dve", lambda e, o3=o3, g=g: e.reciprocal(out=self.rden[rows, 4 * g:4 * g + 4], in_=o3[:, :, 64]),
                    [("ps", bO[g])], [("rden", g)])
            self.tt(self.ob[rows, ob_tt, 256 * g:256 * (g + 1)].rearrange("p (h d) -> p h d", d=64), o3[:, :, 0:64],
                    self.rden[rows, 4 * g:4 * g + 4].unsqueeze(2).to_broadcast([nq, 4, 64]), ALU.mult,
                    [("ps", bO[g]), ("rden", g)], [("AT", ob_tt)])
            self.pf(bO[g])

    def _attention(self, st):
        ncol = st["ncol"]
        kind = st["kind"]
        cur = st["cur"]
        prev = 1 - cur
        npair = ncol // 128
        if kind == "prompt":
            for p in range(npair):
                tiles = []
                for j in (1, 2, 0, 3, 4):
                    kk = (prev, p + j) if p + j <= 3 else (cur, p + j - 4)
                    if j == 3:
                        bias = (lambda h: self.BT3[:, h, :])
                    elif j == 4:
                        bias = (lambda h: self.BT4[:, h, :])
                    elif j == 0:
                        bias = (lambda h: self.BT0[:, :])
                    else:
                        bias = None
                    tiles.append(dict(k=kk, bias=bias, zero=False))
                self._attn_unit(0, 128, p * 128, tiles, self.PTf, p)
        else:
            for i in range(4):
                pr, half = i // 2, i % 2
                kdst, vdst = self.kT[prev], self.Va[prev]
                self.dma("pool", kdst[:], self.dr["ckT"][i], (), [kdst.name])
                self.dma("pool", vdst[:, :, :, 0:64], self.dr["cv"][i].rearrange("p t (h d) -> p t h d", d=64), (), [vdst.name])
                tiles = []
                for j in range(4):
                    bias = (lambda h: self.BT3[:, h, 0:64]) if j == 3 else None
                    tiles.append(dict(k=(prev, j), bias=bias, zero=False))
                own_bias = (lambda h: self.BT4[:, h, 0:64]) if half == 0 else (lambda h: self.BT5[:, h, 0:64])
                tiles.append(dict(k=(cur, pr), bias=own_bias, zero=False))
                self._attn_unit(half * 64, 64, i * 64, tiles, self.PTh[half], pr)
        for p in range(npair):
            b = self.pa()
            for vt in range(4):
                self.mm(self.PB(b)[:, vt * 128:(vt + 1) * 128], self.ob[:, p, vt * 128:(vt + 1) * 128], self.identb[:], True, True,
                        [("AT", p), "identb"], [("ps", b)])
            self.cp(self.obT[:, :, p * 128:(p + 1) * 128], self.PB(b)[:, :].rearrange("p (a b) -> p a b", b=128), [("ps", b)],
                    [("vtok", k) for k in range(4)], eng="act" if p % 2 else "dve")
            self.pf(b)

    def _finish(self):
        self.dma("sp", self.dr["s_out"][:, 4, :, :], self.Sst[:], ["Sst"], ["s_out4"])


def _blk(w, ncols_list=None):
    K, N = w.shape
    nb = N // 512
    return np.ascontiguousarray(w.reshape(K // 128, 128, nb, 512).transpose(2, 1, 0, 3))


_WIN_TILES = None


def _win_perm():
    tiles = []
    for h in range(4):
        tiles += [0 + h, 4 + h, 12 + h]
    tiles += [8, 9, 10, 11]
    tiles += list(range(16, 44))
    cols = np.concatenate([np.arange(t * 128, (t + 1) * 128) for t in tiles])
    return cols


_PROG = {}


def _get_prog(NS):
    if NS not in _PROG:
        b = Builder(NS)
        nc = b.build()
        _PROG[NS] = (nc, b)
    return _PROG[NS]


def make_in_maps(x_prompt, x_sample, c_prompt, c_sample, state_hgrn, cache_k, cache_v, w_ada, b_ada, norm_mix, w_in,
                 hgrn_lb_logits, hgrn_out_norm, w_branch_a, rel_bias, w_branch_b, w_out, norm_ffn, w_ffn_in, w_ffn_out,
                 norm_final):
    f32 = np.float32
    x_prompt = np.asarray(x_prompt, f32)
    x_sample = np.asarray(x_sample, f32)
    B, L, _ = x_prompt.shape
    SEG = L // 4
    NS = SEG // 512
    assert B == 2 and SEG % 512 == 0 and x_sample.shape[0] == 32 and x_sample.shape[1] == 64
    w_in_p = np.asarray(w_in[0], f32)[:, _win_perm()]
    wfi = np.asarray(w_ffn_in[0], f32)
    DFF = wfi.shape[1] // 2
    wfi_p = np.zeros((1024, 12 * 512), f32)
    wfi_p[:, 0:DFF] = wfi[:, 0:DFF]
    wfi_p[:, 6 * 512:6 * 512 + DFF] = wfi[:, DFF:]
    w_br = np.concatenate([np.asarray(w_branch_a[0], f32), np.asarray(w_branch_b[0], f32)], axis=0)
    shared = {
        "w_ada": _blk(np.asarray(w_ada[0], f32)),
        "w_in": _blk(w_in_p),
        "w_br": _blk(w_br),
        "w_out": _blk(np.asarray(w_out[0], f32)),
        "w_fi": _blk(wfi_p),
        "w_fo": _blk(np.asarray(w_ffn_out[0], f32)),
        "b_ada": np.ascontiguousarray(np.asarray(b_ada[0], f32).reshape(48, 128).T),
        "nmix": np.ascontiguousarray(np.asarray(norm_mix[0], f32).reshape(8, 128).T),
        "nffn": np.ascontiguousarray(np.asarray(norm_ffn[0], f32).reshape(8, 128).T),
        "nfin": np.ascontiguousarray(np.asarray(norm_final, f32).reshape(8, 128).T),
        "onorm": np.ascontiguousarray(np.asarray(hgrn_out_norm[0], f32).reshape(4, 128).T),
        "lbl": np.ascontiguousarray(np.asarray(hgrn_lb_logits, f32)[0:2].reshape(2, 4, 128).transpose(2, 1, 0)),
        "rb": np.ascontiguousarray(np.asarray(rel_bias[0], f32)),
    }
    c_prompt = np.asarray(c_prompt, f32)
    c_sample = np.asarray(c_sample, f32)
    state_hgrn = np.asarray(state_hgrn, f32)
    cache_k = np.asarray(cache_k, f32)
    cache_v = np.asarray(cache_v, f32)
    in_maps = []
    for r in range(NCORES):
        b, kseg = r // 4, r % 4
        xs = x_sample[4 * r:4 * r + 4].reshape(256, D)
        t0 = kseg * SEG
        if kseg == 0:
            xh = np.zeros((512, D), f32)
        else:
            xh = x_prompt[b, t0 - 512:t0]
        xp = x_prompt[b, t0:t0 + SEG]
        xT = np.ascontiguousarray(np.concatenate([xs, xh, xp], axis=0).T)
        cs = np.zeros((8, D), f32)
        cs[0] = c_prompt[b]
        cs[1:5] = c_sample[4 * r:4 * r + 4]
        cT = np.ascontiguousarray(cs.reshape(8, 8, 128).transpose(2, 1, 0))
        flags = np.zeros((128, 4), f32)
        flags[:, 0] = 0.0 if kseg == 0 else 1.0
        sin = np.ascontiguousarray(state_hgrn[0, 4 * r:4 * r + 4].transpose(2, 0, 1, 3))
        ck = cache_k[0, 4 * r:4 * r + 4]
        ckT = np.ascontiguousarray(ck.reshape(4, 512, 4, 128).transpose(0, 3, 2, 1))
        cv = np.ascontiguousarray(cache_v[0, 4 * r:4 * r + 4].reshape(4, 4, 128, 512).transpose(0, 2, 1, 3))
        m = dict(shared)
        m.update({"xT": xT, "cT": cT, "flags": flags, "sin": sin, "ckT": ckT, "cv": cv})
        in_maps.append(m)
    return NS, in_maps


def kernel(**inputs):
    f32 = np.float32
    NS, in_maps = make_in_maps(**inputs)
    L = NS * 512 * 4
    SEG = L // 4
    nc, bld = _get_prog(NS)
    res = run_bass_kernel_spmd(nc, in_maps, core_ids=list(range(NCORES)))
    return assemble(res.results, L)


def assemble(R, L):
    f32 = np.float32
    SEG = L // 4
    y_prompt = np.zeros((2, L, D), f32)
    y_sample = np.zeros((32, 64, D), f32)
    s_p = np.zeros((1, 2, 4, 128, 128), f32)
    k_p = np.zeros((1, 2, 512, 8, 64), f32)
    v_p = np.zeros((1, 2, 512, 8, 64), f32)
    s_s = np.zeros((1, 32, 4, 128, 128), f32)
    k_s = np.zeros((1, 32, 64, 8, 64), f32)
    v_s = np.zeros((1, 32, 64, 8, 64), f32)
    for r in range(NCORES):
        b, kseg = r // 4, r % 4
        o = R[r]
        yT = np.asarray(o["yT"])
        y_sample[4 * r:4 * r + 4] = yT[:, 0:256].T.reshape(4, 64, D)
        y_prompt[b, kseg * SEG:(kseg + 1) * SEG] = yT[:, 256:].T
        so = np.asarray(o["s_out"])
        s_s[0, 4 * r:4 * r + 4] = so[:, 0:4].transpose(1, 2, 0, 3)
        kTs = np.asarray(o["kTs"])
        k_s[0, 4 * r:4 * r + 4] = kTs.transpose(2, 1, 0).reshape(4, 64, 8, 64)
        vs = np.asarray(o["vs"])
        v_s[0, 4 * r:4 * r + 4] = vs.transpose(1, 0, 2).reshape(4, 64, 8, 64)
        if kseg == 3:
            s_p[0, b] = so[:, 4].transpose(1, 0, 2)
            k_p[0, b] = np.asarray(o["kTp"]).transpose(2, 1, 0).reshape(512, 8, 64)
            v_p[0, b] = np.asarray(o["vp"]).transpose(1, 0, 2).reshape(512, 8, 64)
    return (y_prompt, y_sample, s_p, k_p, v_p, s_s, k_s, v_s)
```

```python
import numpy as np
from contextlib import ExitStack
import concourse.bass as bass
import concourse.mybir as mybir
from concourse.bass_utils import run_bass_kernel_spmd

F32 = mybir.dt.float32
BF16 = mybir.dt.bfloat16
AF = mybir.ActivationFunctionType
ALU = mybir.AluOpType

D = 1024
NCORES = 8
EPS = 1e-6
NEGM = -30000.0

ENGS = ("pe", "act", "dve", "pool", "sp")
NDMA_SEM = 16
SAME_ENGINE_SYNC = True
PS_EXCL = True
NPT = 3
DBG_SETS = False
KSTOP = 1e9


class Op:
    __slots__ = ("eng", "fn", "dma", "idx", "deps", "waited", "sem", "val", "dq")

    def __init__(self, eng, fn, dma):
        self.eng = eng
        self.fn = fn
        self.dma = dma
        self.deps = []
        self.waited = False
        self.sem = None
        self.val = 0
        self.dq = -1


class Sched:
    def __init__(self, nc):
        self.nc = nc
        self.ops = []
        self.by_eng = {e: [] for e in ENGS}
        self.state = {}

    def _ents(self, name, sub):
        d = self.state.setdefault(name, {})
        if sub is None:
            if None not in d:
                d[None] = [None, []]
            return list(d.values())
        if sub not in d:
            w = d[None][0] if None in d else None
            d[sub] = [w, []]
        ents = [d[sub]]
        if None in d:
            ents.append(d[None])
        return ents

    @staticmethod
    def _k(key):
        return key if isinstance(key, tuple) else (key, None)

    def add(self, eng, fn, reads=(), writes=(), dma=False):
        op = Op(eng, fn, dma)
        op.idx = len(self.ops)
        deps = {}
        if PS_EXCL and eng != "pe":
            writes = list(writes) + [k_ for k_ in reads if isinstance(k_, tuple) and k_[0] == "ps"]
        for key in reads:
            name, sub = self._k(key)
            for ent in self._ents(name, sub):
                if ent[0] is not None:
                    deps[ent[0].idx] = ent[0]
        for key in writes:
            name, sub = self._k(key)
            for ent in self._ents(name, sub):
                if ent[0] is not None:
                    deps[ent[0].idx] = ent[0]
                for r in ent[1]:
                    deps[r.idx] = r
        for key in reads:
            name, sub = self._k(key)
            d = self.state[name]
            if sub is None:
                for ent in d.values():
                    ent[1].append(op)
            else:
                d[sub][1].append(op)
        for key in writes:
            name, sub = self._k(key)
            d = self.state[name]
            if sub is None:
                for ent in d.values():
                    ent[0] = op
                    ent[1] = []
            else:
                d[sub][0] = op
                d[sub][1] = []
        deps.pop(op.idx, None)
        op.deps = [deps[i] for i in sorted(deps)]
        self.by_eng[eng].append(op)
        self.ops.append(op)
        return op

    @staticmethod
    def _need_wait(op, d):
        if d.dma:
            return True
        if d.eng == op.eng:
            if op.eng == "pe":
                return False
            if op.dma:
                return True
            return SAME_ENGINE_SYNC
        return True

    def emit(self, sems, dma_sems):
        nc = self.nc
        for op in self.ops:
            for d in op.deps:
                if self._need_wait(op, d):
                    d.waited = True
        final_waits = []
        for e in ENGS:
            cnt = 0
            dcnt = 0
            for op in self.by_eng[e]:
                if op.dma:
                    op.dq = dcnt
                    op.sem = dma_sems[e][dcnt % NDMA_SEM]
                    op.val = 16 * (dcnt // NDMA_SEM + 1)
                    dcnt += 1
                elif op.waited:
                    cnt += 1
                    op.sem = sems[e]
                    op.val = cnt
            for i in range(min(dcnt, NDMA_SEM)):
                uses = (dcnt - 1 - i) // NDMA_SEM + 1
                final_waits.append((dma_sems[e][i], 16 * uses))
        stats = {}

        def run(e):
            def body(eng):
                seen = {}
                nw = 0
                for op in self.by_eng[e]:
                    waits = []
                    if op.dma and op.dq >= NDMA_SEM:
                        waits.append((op.sem, op.val - 16))
                    for d in op.deps:
                        if self._need_wait(op, d):
                            waits.append((d.sem, d.val))
                    best = {}
                    for s, v in waits:
                        kk = id(s)
                        if seen.get(kk, 0) >= v:
                            continue
                        if kk not in best or best[kk][1] < v:
                            best[kk] = (s, v)
                    for kk, (s, v) in best.items():
                        eng.wait_ge(s, v)
                        seen[kk] = v
                        nw += 1
                    ins = op.fn(eng)
                    if op.dma:
                        ins.then_inc(op.sem, 16)
                    elif op.waited:
                        ins.then_inc(op.sem, 1)
                if e == "sp":
                    for s, v in final_waits:
                        eng.wait_ge(s, v)
                stats[e] = (len(self.by_eng[e]), nw)
            return body

        with nc.Block() as block:
            block.tensor(run("pe"))
            block.scalar(run("act"))
            block.vector(run("dve"))
            block.gpsimd(run("pool"))
            block.sync(run("sp"))
        return stats


class _Stop(Exception):
    pass


class Builder:
    def chk(self, n):
        if n > KSTOP:
            raise _Stop()

    def __init__(self, NS):
        self.NS = NS
        self.NP = NS * 512
        self.NTOK = 256 + 512 + self.NP
        self.NOUT = 256 + self.NP
        self.nc = bass.Bass("TRN2", target_bir_lowering=False)
        self.es = ExitStack()

    def sb(self, name, shape, dt):
        return self.es.enter_context(self.nc.sbuf_tensor(name, shape, dt))

    def din(self, name, shape):
        return self.nc.dram_tensor(name, shape, F32, kind="ExternalInput").ap()

    def dout(self, name, shape):
        return self.nc.dram_tensor(name, shape, F32, kind="ExternalOutput").ap()

    def op(self, eng, fn, r=(), w=(), dma=False):
        return self.S.add(eng, fn, reads=r, writes=w, dma=dma)

    def mm(self, out, lhsT, rhs, start, stop, r, w):
        self.op("pe", lambda e: e.matmul(out, lhsT=lhsT, rhs=rhs, start=start, stop=stop), r, w)

    def act(self, out, in_, func, r, w, scale=1.0, bias=None):
        if bias is None:
            self.op("act", lambda e: e.activation(out=out, in_=in_, func=func, scale=scale), r, w)
        else:
            self.op("act", lambda e: e.activation(out=out, in_=in_, func=func, scale=scale, bias=bias), r, w)

    def tt(self, out, in0, in1, op, r, w, eng="dve"):
        self.op(eng, lambda e: e.tensor_tensor(out=out, in0=in0, in1=in1, op=op), r, w)

    def ts(self, out, in0, s1, s2, op0, op1, r, w, eng="dve"):
        if s2 is None:
            self.op(eng, lambda e: e.tensor_scalar(out=out, in0=in0, scalar1=s1, scalar2=None, op0=op0), r, w)
        else:
            self.op(eng, lambda e: e.tensor_scalar(out=out, in0=in0, scalar1=s1, scalar2=s2, op0=op0, op1=op1), r, w)

    def stt(self, out, in0, scalar, in1, op0, op1, r, w):
        self.op("dve", lambda e: e.scalar_tensor_tensor(out=out, in0=in0, scalar=scalar, in1=in1, op0=op0, op1=op1), r, w)

    def cp(self, out, in_, r, w, eng="dve"):
        if eng == "act":
            self.op("act", lambda e: e.activation(out=out, in_=in_, func=AF.Copy), r, w)
        else:
            self.op(eng, lambda e: e.tensor_copy(out=out, in_=in_), r, w)

    def memset(self, ap, val, w, eng="pool"):
        self.op(eng, lambda e: e.memset(ap, val), (), w)

    def dma(self, eng, out, in_, r, w):
        self.op(eng, lambda e: e.dma_start(out=out, in_=in_, max_dma_last_dim=2048), r, w, dma=True)

    def pa(self):
        assert self.pfree, "out of PSUM banks"
        return self.pfree.pop(0)

    def pf(self, b):
        self.pfree.append(b)

    def PB(self, b):
        return self.ps[b]

    def wld(self, name, blk, k0=0, kn=8, width=512):
        slot = self.wslot
        self.wslot = (self.wslot + 1) % len(self.wb)
        t = self.wb[slot]
        src = self.dr[name]
        key = ("wb", slot)
        self.dma("pool", t[:, 0:kn, 0:width], src[blk, :, k0:k0 + kn, 0:width], (), [key])
        return t, key

    def build(self):
        nc = self.nc
        with self.es:
            self._declare()
            self.S = Sched(nc)
            self._setup()
            sts = []
            sts.append(dict(kind="sample", col0=0, ncol=256, ocol=0,
                            segs=[(i * 64, 64, 1 + i) for i in range(4)], cur=0))
            sts.append(dict(kind="halo", col0=256, ncol=512, segs=[(0, 512, 0)], cur=1))
            for s in range(self.NS):
                sts.append(dict(kind="prompt", col0=768 + 512 * s, ncol=512, ocol=256 + 512 * s,
                                segs=[(0, 512, 0)], cur=s % 2, first=(s == 0), last=(s == self.NS - 1)))
            try:
                self.chk(1)
                for st in sts:
                    self._supertile(st)
                self._finish()
            except _Stop:
                pass
            sems = {e: self.es.enter_context(nc.semaphore("s_" + e)) for e in ENGS}
            dsems = {e: [self.es.enter_context(nc.semaphore("d_%s%d" % (e, i))) for i in range(NDMA_SEM)]
                     for e in ENGS}
            self.stats = self.S.emit(sems, dsems)
        return nc

    def _declare(self):
        nc = self.nc
        dr = {}
        dr["xT"] = self.din("xT", [D, self.NTOK])
        dr["cT"] = self.din("cT", [128, 8, 8])
        dr["flags"] = self.din("flags", [128, 4])
        dr["sin"] = self.din("sin", [128, 4, 4, 128])
        dr["ckT"] = self.din("ckT", [4, 128, 4, 512])
        dr["cv"] = self.din("cv", [4, 128, 4, 512])
        dr["w_ada"] = self.din("w_ada", [12, 128, 8, 512])
        dr["w_in"] = self.din("w_in", [11, 128, 8, 512])
        dr["w_br"] = self.din("w_br", [2, 128, 8, 512])
        dr["w_out"] = self.din("w_out", [2, 128, 8, 512])
        dr["w_fi"] = self.din("w_fi", [12, 128, 8, 512])
        dr["w_fo"] = self.din("w_fo", [2, 128, 22, 512])
        dr["b_ada"] = self.din("b_ada", [128, 48])
        dr["nmix"] = self.din("nmix", [128, 8])
        dr["nffn"] = self.din("nffn", [128, 8])
        dr["nfin"] = self.din("nfin", [128, 8])
        dr["onorm"] = self.din("onorm", [128, 4])
        dr["lbl"] = self.din("lbl", [128, 4, 2])
        dr["rb"] = self.din("rb", [8, 192])
        dr["yT"] = self.dout("yT", [D, self.NOUT])
        dr["s_out"] = self.dout("s_out", [128, 5, 4, 128])
        dr["kTs"] = self.dout("kTs", [128, 4, 256])
        dr["vs"] = self.dout("vs", [128, 2, 512])
        dr["kTp"] = self.dout("kTp", [128, 4, 512])
        dr["vp"] = self.dout("vp", [128, 4, 512])
        self.ext_d = nc.dram_tensor("ext_d", [8, 383], F32)
        self.dr = dr
        sb = self.sb
        self.onesf = sb("onesf", [128, 128], F32)
        self.identb = sb("identb", [128, 128], BF16)
        self.onesb = sb("onesb", [128, 128], BF16)
        self.resetm = sb("resetm", [128, 512], F32)
        self.trim = sb("trim", [128, 4, 128], F32)
        self.epsc = sb("epsc", [128, 1], F32)
        self.cT = sb("cT_sb", [128, 8, 8], F32)
        self.scT = sb("scT", [128, 8, 8], BF16)
        self.flags = sb("flags_sb", [128, 4], F32)
        self.b_ada = sb("b_ada_sb", [128, 48], F32)
        self.nmix = sb("nmix_sb", [128, 8], F32)
        self.nffn = sb("nffn_sb", [128, 8], F32)
        self.nfin = sb("nfin_sb", [128, 8], F32)
        self.onorm = sb("onorm_sb", [128, 4], F32)
        self.lbl = sb("lbl_sb", [128, 4, 2], F32)
        self.lbd = sb("lbd", [128, 4], F32)
        self.lb = sb("lb", [128, 4], F32)
        self.oml = sb("oml", [128, 4], F32)
        self.modT = sb("modT", [128, 48, 8], F32)
        self.gm1 = sb("gm1", [128, 8, 8], F32)
        self.gm2 = sb("gm2", [128, 8, 8], F32)
        self.BT3 = sb("BT3", [128, 8, 128], F32)
        self.BT4 = sb("BT4", [128, 8, 128], F32)
        self.BT5 = sb("BT5", [128, 8, 64], F32)
        self.BT0 = sb("BT0", [128, 128], F32)
        self.xT = sb("xT_sb", [128, 8, 512], F32)
        self.hT = sb("hT", [128, 8, 512], BF16)
        self.mT = sb("mT", [128, 8, 512], BF16)
        self.rstd = sb("rstd", [128, 512], F32)
        self.wb = [sb("wb%d" % i, [128, 8, 512], BF16) for i in range(4)]
        self.wslot = 0
        self.arena = sb("arena", [128, 22 * 512], BF16)
        ar32 = self.arena.bitcast(F32)
        self.sT = self.arena[:, :].rearrange("p (j t) -> p j t", t=512)
        self.sqa4 = ar32[:, 0:2048].rearrange("p (h t) -> p h t", t=512)
        self.fT4 = ar32[:, 2048:4096].rearrange("p (h t) -> p h t", t=512)
        self.kTf = ar32[:, 0:2048].rearrange("p (h t) -> p h t", t=512)
        self.vf = ar32[:, 2048:4096].rearrange("p (h t) -> p h t", t=512)
        self.lgf = sb("lgf", [128, 512], F32)
        self.cum = sb("cum", [128, 512], F32)
        self.dd = self.lgf
        self.e1 = sb("e1", [128, 512], F32)
        self.e2 = self.lgf
        self.qpT = sb("qpT", [128, 4, 512], BF16)
        self.kpT = sb("kpT", [128, 4, 512], BF16)
        self.vtok = sb("vtok", [128, 4, 512], BF16)
        self.sga = sb("sga", [128, 4, 512], BF16)
        self.AT = sb("AT", [128, 4, 512], BF16)
        self.ktokA = sb("ktokA", [128, 4, 4, 128], BF16)
        self.ktokB = sb("ktokB", [128, 4, 4, 128], BF16)
        self.Sst = sb("Sst", [128, 4, 128], F32)
        self.sin = sb("sin_sb", [128, 4, 4, 128], F32)
        self.Ssc = sb("Ssc", [128, 8, 128], BF16)
        self.tmpS = sb("tmpS", [128, 8, 128], F32)
        self.eref = sb("eref", [128, 4, 8], F32)
        self.etot = sb("etot", [128, 4, 8], F32)
        self.etr = sb("etr", [128, 4, 8], F32)
        self.oaT = sb("oaT", [128, 4, 512], BF16)
        self.rso = self.e1
        self.t1 = sb("t1", [128, 512], F32)
        self.t2 = sb("t2", [128, 512], F32)
        self.rbt = self.t2[0:8, 0:192]
        self.ext = self.t1[0:8, 0:383]
        self.sa = sb("sa", [128, 512], F32)
        self.sbg = sb("sbg", [128, 512], F32)
        self.identf = self.sa[:, 0:128]
        self.Jf = self.sbg[:, 0:128]
        m32 = self.mT.bitcast(F32)[:, :, :].rearrange("p a b -> p (a b)")
        self.hset = [(self.lgf, self.cum, self.e1),
                     (ar32[:, 4096:4608], ar32[:, 4608:5120], ar32[:, 5120:5632]),
                     (m32[:, 0:512], m32[:, 512:1024], m32[:, 1024:1536]),
                     (self.sa, self.sbg, self.rstd)]
        self.hkeys = [(["lgf"], ["cum"], ["e1"]),
                      ([("arena", 16), ("arena", 17)], [("arena", 18), ("arena", 19)], [("arena", 20), ("arena", 21)]),
                      ([("mT", 0), ("mT", 1)], [("mT", 2), ("mT", 3)], [("mT", 4), ("mT", 5)]),
                      (["sa"], ["sbg"], ["rstd"])]
        if DBG_SETS:
            self.hset[2], self.hkeys[2] = self.hset[0], self.hkeys[0]
            self.hset[3], self.hkeys[3] = self.hset[1], self.hkeys[1]
        self.qTA = sb("qTA", [128, 4, 512], BF16)
        self.qTB = sb("qTB", [128, 4, 512], BF16)
        self.kT = [sb("kT%d" % i, [128, 4, 512], BF16) for i in range(2)]
        self.Va = [sb("Va%d" % i, [128, 4, 8, 65], BF16) for i in range(2)]
        self.sbias = sb("sbias", [128, NPT, 384], F32)
        self.PTf = [sb("PTf%d" % i, [128, 5, 128], BF16) for i in range(NPT)]
        self.PTh = [sb("PTh%d" % i, [128, 5, 128], BF16) for i in range(2)]
        self.rden = sb("rden", [128, 8], F32)
        self.ob = self.AT
        self.obT = self.vtok
        self.ps = [self.es.enter_context(nc.psum_tensor("ps%d" % i, [128, 512], F32)) for i in range(8)]
        self.pfree = list(range(8))

    def _setup(self):
        dr = self.dr
        for name, t in (("cT", self.cT), ("flags", self.flags), ("b_ada", self.b_ada), ("nmix", self.nmix),
                        ("nffn", self.nffn), ("nfin", self.nfin), ("onorm", self.onorm), ("lbl", self.lbl),
                        ("rb", self.rbt), ("sin", self.sin)):
            self.dma("sp", t[:], dr[name], (), [t.name])
        self.memset(self.onesf[:], 1.0, ["onesf"])
        self.memset(self.onesb[:], 1.0, ["onesb"])
        self.memset(self.epsc[:], EPS, ["epsc"])
        self.memset(self.identf[:], 0.0, ["sa"])
        self.op("pool", lambda e: e.affine_select(out=self.identf[:], in_=self.onesf[:], pattern=[[-1, 128]],
                                                  compare_op=ALU.is_equal, fill=0.0, base=0, channel_multiplier=1),
                ["onesf"], ["sa"])
        self.cp(self.identb[:], self.identf[:], ["sa"], ["identb"])
        self.memset(self.Jf[:], 0.0, ["sbg"])
        self.op("pool", lambda e: e.affine_select(out=self.Jf[:], in_=self.onesf[:], pattern=[[1, 128]],
                                                  compare_op=ALU.is_equal, fill=0.0, base=-127, channel_multiplier=1),
                ["onesf"], ["sbg"])
        self.memset(self.resetm[:], 1.0, ["resetm"])
        self.memset(self.resetm[:].rearrange("p (c t) -> p c t", t=64)[:, :, 0:1], 0.0, ["resetm"])
        self.memset(self.trim[:], 1.0, ["trim"])
        self.op("pool", lambda e: e.affine_select(out=self.trim[:], in_=self.trim[:], pattern=[[0, 4], [1, 128]],
                                                  compare_op=ALU.is_ge, fill=0.0, base=0, channel_multiplier=-1),
                ["trim"], ["trim"])
        self.memset(self.trim[0:64, :, 64:128], 0.0, ["trim"])
        for t in (self.ktokA, self.ktokB, self.qTA, self.qTB, self.PTh[0], self.PTh[1]):
            self.memset(t[:], 0.0, [t.name])
        for t in self.Va:
            self.memset(t[:], 1.0, [t.name])
        self.memset(self.Sst[:], 0.0, ["Sst"])
        self.tt(self.lbd[:], self.lbl[:, :, 0], self.lbl[:, :, 1], ALU.subtract, ["lbl_sb"], ["lbd"])
        self.act(self.lb[:], self.lbd[:], AF.Sigmoid, ["lbd"], ["lb"])
        self.act(self.oml[:], self.lbd[:], AF.Sigmoid, ["lbd"], ["oml"], scale=-1.0)
        self.act(self.scT[:], self.cT[:], AF.Silu, ["cT_sb"], ["scT"])
        bm = self.pa()
        for blk in range(12):
            w, wk = self.wld("w_ada", blk)
            for t4 in range(4):
                t = blk * 4 + t4
                for k in range(8):
                    self.mm(self.PB(bm)[:, t * 8:(t + 1) * 8], w[:, k, t4 * 128:(t4 + 1) * 128], self.scT[:, k, :],
                            k == 0, k == 7, [wk, "scT"], [("ps", bm)])
        self.tt(self.modT[:], self.PB(bm)[:, 0:384].rearrange("p (t s) -> p t s", s=8),
                self.b_ada[:].unsqueeze(2).to_broadcast([128, 48, 8]), ALU.add, [("ps", bm), "b_ada_sb"], ["modT"])
        self.pf(bm)
        self.stt(self.gm1[:], self.modT[:, 8:16, :], 1.0, self.nmix[:].unsqueeze(2).to_broadcast([128, 8, 8]),
                 ALU.add, ALU.mult, ["modT", "nmix_sb"], ["gm1"])
        self.stt(self.gm2[:], self.modT[:, 32:40, :], 1.0, self.nffn[:].unsqueeze(2).to_broadcast([128, 8, 8]),
                 ALU.add, ALU.mult, ["modT", "nffn_sb"], ["gm2"])
        self.memset(self.ext[:], 0.0, ["t1"])
        self.ts(self.ext[:, 64:256], self.rbt[:], self.rbt[:, 191:192], None, ALU.subtract, None, ["t2"], ["t1"])
        self.ts(self.ext[:, 0:64], self.ext[:, 256:320], self.rbt[:, 0:1], self.rbt[:, 191:192], ALU.add, ALU.subtract,
                ["t2", "t1"], ["t1"])
        self.dma("sp", self.ext_d.ap(), self.ext[:], ["t1"], ["ext_d"])
        hank = self.sqa4
        for which, off, nq, BT in ((3, 128, 128, self.BT3), (4, 0, 128, self.BT4), (5, 64, 64, self.BT5)):
            hv = self.arena.bitcast(F32)[:, 0:8 * nq].rearrange("p (h q) -> p h q", q=nq)
            src = bass.AP(self.ext_d, off, [[1, 128], [383, 8], [1, nq]])
            self.dma("sp", hv, src, ["ext_d"], [("arena", u) for u in range(8)])
            nb = (8 * nq) // 512
            for b2 in range(nb):
                bk = self.pa()
                hpb = 512 // nq
                for hh in range(hpb):
                    h = b2 * hpb + hh
                    self.mm(self.PB(bk)[:, hh * nq:(hh + 1) * nq], self.Jf[:], hv[:, h, :], True, True,
                            ["sbg"] + [("arena", u) for u in range(8)], [("ps", bk)])
                self.cp(BT[:, b2 * hpb:(b2 + 1) * hpb, :], self.PB(bk)[:, :].rearrange("p (h q) -> p h q", q=nq),
                        [("ps", bk)], [BT.name])
                self.pf(bk)
        self.memset(self.BT0[:], 0.0, ["BT0"])
        self.memset(self.BT0[0:64, 64:128], NEGM, ["BT0"])
        self.memset(self.BT4[64:128, :, 0:64], NEGM, ["BT4"])
        self.memset(self.BT5[0:64, :, :], NEGM, ["BT5"])

    def _norm_stats(self, ncol):
        sq = self.mT
        b = self.pa()
        for k in range(8):
            self.act(sq[:, k, 0:ncol], self.xT[:, k, 0:ncol], AF.Square, [("xT_sb", k)], [("mT", k)])
            self.mm(self.PB(b)[:, 0:ncol], self.onesb[:], sq[:, k, 0:ncol], k == 0, k == 7, ["onesb", ("mT", k)], [("ps", b)])
        self.act(self.rstd[:, 0:ncol], self.PB(b)[:, 0:ncol], AF.Ln, [("ps", b), "epsc"], ["rstd"], scale=1.0 / D,
                 bias=self.epsc[:, 0:1])
        self.act(self.PB(b)[:, 0:ncol], self.rstd[:, 0:ncol], AF.Exp, ["rstd"], [("ps", b)], scale=-0.5)
        return b

    def _norm_apply(self, st, gm, sh_off, rb_):
        i = 0
        for k in range(8):
            for (c0, n, seq) in st["segs"]:
                sl = i % 2
                i += 1
                tb = self.t1 if sl == 0 else self.t2
                self.stt(tb[:, 0:n], self.xT[:, k, c0:c0 + n], gm[:, k, seq:seq + 1], self.PB(rb_)[:, c0:c0 + n],
                         ALU.mult, ALU.mult, [("xT_sb", k), ("ps", rb_), gm.name], [tb.name])
                self.act(self.hT[:, k, c0:c0 + n], tb[:, 0:n], AF.Identity, [tb.name, "modT"],
                         [("hT", k)], bias=self.modT[:, sh_off + k, seq:seq + 1])
        self.pf(rb_)

    def _supertile(self, st):
        dr = self.dr
        ncol = st["ncol"]
        kind = st["kind"]
        cur = st["cur"]
        ntt = ncol // 128
        nch = ncol // 64
        segs = st["segs"]
        xsrc = dr["xT"].rearrange("(k p) n -> p k n", p=128)[:, :, st["col0"]:st["col0"] + ncol]
        for k in range(8):
            self.dma("sp", self.xT[:, k, 0:ncol], xsrc[:, k, :], (), [("xT_sb", k)])
        self.chk(2)
        rb_ = self._norm_stats(ncol)
        self.chk(3)
        self._norm_apply(st, self.gm1, 0, rb_)
        self.chk(4)
        want_kv = kind == "sample" or (kind == "prompt" and st["last"])

        self._hgrn(st, kind == "halo")

        self.chk(6)
        if kind != "halo":
            w, wk = self.wld("w_in", 4)
            for hp in range(4):
                b = self.pa()
                for k in range(8):
                    self.mm(self.PB(b)[:, 0:ncol], w[:, k, hp * 128:(hp + 1) * 128], self.hT[:, k, 0:ncol], k == 0, k == 7,
                            [wk, ("hT", k)], [("ps", b)])
                self.act(self.qTA[0:64, hp, 0:ncol], self.PB(b)[0:64, 0:ncol], AF.Identity, [("ps", b)], [("qTA", hp)], scale=0.125)
                self.ts(self.qTB[64:128, hp, 0:ncol], self.PB(b)[64:128, 0:ncol], 0.125, None, ALU.mult, None,
                        [("ps", b)], [("qTB", hp)])
                self.pf(b)
        self.chk(6.1)
        w, wk = self.wld("w_in", 5)
        for hp in range(4):
            b = self.pa()
            for k in range(8):
                self.mm(self.PB(b)[:, 0:ncol], w[:, k, hp * 128:(hp + 1) * 128], self.hT[:, k, 0:ncol], k == 0, k == 7,
                        [wk, ("hT", k)], [("ps", b)])
            self.cp(self.kT[cur][:, hp, 0:ncol], self.PB(b)[:, 0:ncol], [("ps", b)], [(self.kT[cur].name, hp)], eng="act")
            if want_kv:
                self.cp(self.kTf[:, hp, 0:ncol], self.PB(b)[:, 0:ncol], [("ps", b)], [("arena", 2 * hp), ("arena", 2 * hp + 1)])
            self.pf(b)
        self.chk(6.2)
        w, wk = self.wld("w_in", 6)
        if kind != "halo":
            self.memset(self.Va[cur][:, :, :, 64:65], 1.0, [self.Va[cur].name])
        for tt_ in range(ntt):
            b = self.pa()
            for k in range(8):
                self.mm(self.PB(b)[:, 0:512], self.hT[:, k, tt_ * 128:(tt_ + 1) * 128], w[:, k, :], k == 0, k == 7,
                        [wk, ("hT", k)], [("ps", b)])
            self.cp(self.Va[cur][:, tt_, :, 0:64], self.PB(b)[:, :].rearrange("p (h d) -> p h d", d=64), [("ps", b)],
                    [(self.Va[cur].name, tt_)], eng="act")
            if want_kv:
                self.cp(self.vf[:, tt_, :], self.PB(b)[:, :], [("ps", b)], [("arena", 8 + 2 * tt_), ("arena", 9 + 2 * tt_)])
            self.pf(b)
        self.chk(6.4)
        if want_kv:
            ko, vo = ("kTs", "vs") if kind == "sample" else ("kTp", "vp")
            self.dma("sp", dr[ko], self.kTf[:, :, 0:ncol], [("arena", u) for u in range(8)], [ko])
            self.dma("sp", dr[vo], self.vf[:, 0:ntt, :], [("arena", u) for u in range(8, 16)], [vo])
        if kind == "halo":
            va = self.Va[cur]
            self.ts(va[:], va[:], self.flags[:, 0:1], None, ALU.mult, None, [va.name, "flags_sb"], [va.name])
            return

        self.chk(7)
        self._attention(st)
        self.chk(8)

        for cb in range(2):
            wbr, kbr = self.wld("w_br", cb)
            wga, kga = self.wld("w_in", 7 + cb)
            wgb, kgb = self.wld("w_in", 9 + cb)
            for m4 in range(4):
                m = cb * 4 + m4
                cs = slice(m4 * 128, (m4 + 1) * 128)
                bya, byb, bga, bgb = self.pa(), self.pa(), self.pa(), self.pa()
                for k in range(4):
                    self.mm(self.PB(bya)[:, 0:ncol], wbr[:, k, cs], self.oaT[:, k, 0:ncol], k == 0, k == 3,
                            [kbr, ("oaT", k)], [("ps", bya)])
                for k in range(4):
                    self.mm(self.PB(byb)[:, 0:ncol], wbr[:, 4 + k, cs], self.obT[:, k, 0:ncol], k == 0, k == 3,
                            [kbr, ("vtok", k)], [("ps", byb)])
                for k in range(8):
                    self.mm(self.PB(bga)[:, 0:ncol], wga[:, k, cs], self.hT[:, k, 0:ncol], k == 0, k == 7,
                            [kga, ("hT", k)], [("ps", bga)])
                for k in range(8):
                    self.mm(self.PB(bgb)[:, 0:ncol], wgb[:, k, cs], self.hT[:, k, 0:ncol], k == 0, k == 7,
                            [kgb, ("hT", k)], [("ps", bgb)])
                self.act(self.sa[:, 0:ncol], self.PB(bga)[:, 0:ncol], AF.Sigmoid, [("ps", bga)], ["sa"])
                self.act(self.sbg[:, 0:ncol], self.PB(bgb)[:, 0:ncol], AF.Sigmoid, [("ps", bgb)], ["sbg"])
                self.pf(bga)
                self.pf(bgb)
                self.tt(self.t1[:, 0:ncol], self.PB(bya)[:, 0:ncol], self.sa[:, 0:ncol], ALU.mult, [("ps", bya), "sa"], ["t1"])
                self.tt(self.t2[:, 0:ncol], self.PB(byb)[:, 0:ncol], self.sbg[:, 0:ncol], ALU.mult, [("ps", byb), "sbg"], ["t2"])
                self.pf(bya)
                self.pf(byb)
                self.tt(self.mT[:, m, 0:ncol], self.t1[:, 0:ncol], self.t2[:, 0:ncol], ALU.add, ["t1", "t2"], [("mT", m)])
        self.chk(9)
        for cb in range(2):
            w, wk = self.wld("w_out", cb)
            for m4 in range(4):
                m = cb * 4 + m4
                b = self.pa()
                for k in range(8):
                    self.mm(self.PB(b)[:, 0:ncol], w[:, k, m4 * 128:(m4 + 1) * 128], self.mT[:, k, 0:ncol], k == 0, k == 7,
                            [wk, ("mT", k)], [("ps", b)])
                for (c0, n, seq) in segs:
                    self.stt(self.xT[:, m, c0:c0 + n], self.PB(b)[:, c0:c0 + n], self.modT[:, 16 + m, seq:seq + 1],
                             self.xT[:, m, c0:c0 + n], ALU.mult, ALU.add, [("ps", b), "modT", ("xT_sb", m)], [("xT_sb", m)])
                self.pf(b)
        self.chk(10)
        rb_ = self._norm_stats(ncol)
        self._norm_apply(st, self.gm2, 24, rb_)
        for bi in range(6):
            width = 512 if bi < 5 else 256
            wa, ka = self.wld("w_fi", bi, width=width)
            wu, ku = self.wld("w_fi", 6 + bi, width=width)
            for j4 in range(width // 128):
                j = bi * 4 + j4
                cs = slice(j4 * 128, (j4 + 1) * 128)
                ba, bu = self.pa(), self.pa()
                for k in range(8):
                    self.mm(self.PB(ba)[:, 0:ncol], wa[:, k, cs], self.hT[:, k, 0:ncol], k == 0, k == 7,
                            [ka, ("hT", k)], [("ps", ba)])
                for k in range(8):
                    self.mm(self.PB(bu)[:, 0:ncol], wu[:, k, cs], self.hT[:, k, 0:ncol], k == 0, k == 7,
                            [ku, ("hT", k)], [("ps", bu)])
                sl = j % 2
                tsl = self.t1 if sl == 0 else self.t2
                self.act(tsl[:, 0:ncol], self.PB(ba)[:, 0:ncol], AF.Silu, [("ps", ba)], [tsl.name])
                self.pf(ba)
                self.tt(self.sT[:, j, 0:ncol], self.PB(bu)[:, 0:ncol], tsl[:, 0:ncol], ALU.mult, [("ps", bu), tsl.name],
                        [("arena", j)])
                self.pf(bu)
        for cb in range(2):
            banks = [self.pa() for _ in range(4)]
            for (k0, kn) in ((0, 8), (8, 8), (16, 6)):
                w, wk = self.wld("w_fo", cb, k0=k0, kn=kn)
                for m4 in range(4):
                    for kk in range(kn):
                        kg = k0 + kk
                        self.mm(self.PB(banks[m4])[:, 0:ncol], w[:, kk, m4 * 128:(m4 + 1) * 128], self.sT[:, kg, 0:ncol],
                                kg == 0, kg == 21, [wk, ("arena", kg)], [("ps", banks[m4])])
            for m4 in range(4):
                m = cb * 4 + m4
                b = banks[m4]
                for (c0, n, seq) in segs:
                    self.stt(self.xT[:, m, c0:c0 + n], self.PB(b)[:, c0:c0 + n], self.modT[:, 40 + m, seq:seq + 1],
                             self.xT[:, m, c0:c0 + n], ALU.mult, ALU.add, [("ps", b), "modT", ("xT_sb", m)], [("xT_sb", m)])
                self.pf(b)
        self.chk(11)
        rb_ = self._norm_stats(ncol)
        ydst = dr["yT"].rearrange("(k p) n -> p k n", p=128)[:, :, st["ocol"]:st["ocol"] + ncol]
        for k in range(8):
            self.stt(self.xT[:, k, 0:ncol], self.xT[:, k, 0:ncol], self.nfin[:, k:k + 1], self.PB(rb_)[:, 0:ncol],
                     ALU.mult, ALU.mult, [("xT_sb", k), ("ps", rb_), "nfin_sb"], [("xT_sb", k)])
            self.dma("sp", ydst[:, k, :], self.xT[:, k, 0:ncol], [("xT_sb", k)], [("yT", k)])
        self.pf(rb_)

    def _hgrn(self, st, so=False):
        ncol = st["ncol"]
        kind = st["kind"]
        ntt = ncol // 128
        nch = ncol // 64
        npair = ncol // 128
        wblk = {}
        for blk in range(3):
            wblk[blk] = self.wld("w_in", blk)
        for h in range(4):
            for which in ((1,) if so else (0, 1, 2)):
                idx = 3 * h + which
                w, wk = wblk[idx // 4]
                cs = slice((idx % 4) * 128, (idx % 4 + 1) * 128)
                b = self.pa()
                for k in range(8):
                    self.mm(self.PB(b)[:, 0:ncol], w[:, k, cs], self.hT[:, k, 0:ncol], k == 0, k == 7,
                            [wk, ("hT", k)], [("ps", b)])
                if which == 0:
                    self.act(self.sqa4[:, h, 0:ncol], self.PB(b)[:, 0:ncol], AF.Silu, [("ps", b)],
                             [("arena", 2 * h), ("arena", 2 * h + 1)])
                elif which == 1:
                    self.act(self.fT4[:, h, 0:ncol], self.PB(b)[:, 0:ncol], AF.Sigmoid, [("ps", b)],
                             [("arena", 8 + 2 * h), ("arena", 9 + 2 * h)])
                else:
                    self.act(self.sga[:, h, 0:ncol], self.PB(b)[:, 0:ncol], AF.Silu, [("ps", b)], [("sga", h)])
                self.pf(b)
        w, wk = self.wld("w_in", 3)
        for tt_ in range(ntt):
            b = self.pa()
            for k in range(8):
                self.mm(self.PB(b)[:, 0:512], self.hT[:, k, tt_ * 128:(tt_ + 1) * 128], w[:, k, :], k == 0, k == 7,
                        [wk, ("hT", k)], [("ps", b)])
            self.cp(self.vtok[:, tt_, :], self.PB(b)[:, :], [("ps", b)], [("vtok", tt_)])
            self.pf(b)
        gens = [self._hg_pre(st, so, h) for h in range(4)]
        while gens:
            for g in list(gens):
                try:
                    next(g)
                except StopIteration:
                    gens.remove(g)
        bO = [None] * 4 if so else [self.pa() for _ in range(4)]
        for c in range(nch):
            p, half = c // 2, c % 2
            bKV = self.pa()
            kt = self.ktokA if half == 0 else self.ktokB
            for h in range(4):
                hs = slice(h * 128, (h + 1) * 128)
                self.mm(self.PB(bKV)[:, hs], kt[:, h, p, :], self.vtok[:, p, hs], True, True,
                        [(kt.name, h), ("vtok", p)], [("ps", bKV)])
            for h in range(4):
                hs = slice(h * 128, (h + 1) * 128)
                if kind == "sample":
                    Sap = self.sin[:, c, h, :]
                    skey = ("sin_sb", c * 4 + h)
                else:
                    Sap = self.Sst[:, h, :]
                    skey = ("Sst", h)
                sl = h * 2 + c % 2
                if not so:
                    self.act(self.Ssc[:, sl, :], Sap, AF.Identity, [skey, ("eref", h)], [("Ssc", sl)], scale=self.eref[:, h, c:c + 1])
                    if half == 0:
                        ps_ = slice(p * 128, (p + 1) * 128)
                        self.mm(self.PB(bO[h])[:, ps_], self.vtok[:, p, hs], self.AT[:, h, ps_], True, False,
                                [("vtok", p), ("AT", h)], [("ps", bO[h])])
                    cs = slice(c * 64, (c + 1) * 64)
                    self.mm(self.PB(bO[h])[:, cs], self.Ssc[:, sl, :], self.qpT[:, h, cs], False, half == 1,
                            [("Ssc", sl), ("qpT", h)], [("ps", bO[h])])
                self.act(self.tmpS[:, sl, :], Sap, AF.Identity, [skey, ("etot", h)], [("tmpS", sl)], scale=self.etot[:, h, c:c + 1])
                self.stt(Sap, self.PB(bKV)[:, hs], self.etr[:, h, c:c + 1], self.tmpS[:, sl, :], ALU.mult, ALU.add,
                         [("ps", bKV), ("etr", h), ("tmpS", sl)], [skey])
            self.pf(bKV)
        if not so:
            gens = [self._hg_post(st, h, bO[h]) for h in range(4)]
            while gens:
                for g in list(gens):
                    try:
                        next(g)
                    except StopIteration:
                        gens.remove(g)
        if so:
            self.ts(self.Sst[:], self.Sst[:], self.flags[:, 0:1], None, ALU.mult, None, ["Sst", "flags_sb"], ["Sst"])
        if kind == "sample":
            self.dma("sp", self.dr["s_out"][:, 0:4, :, :], self.sin[:], ["sin_sb"], ["s_out"])

    def _hg_pre(self, st, so, h):
        ncol = st["ncol"]
        kind = st["kind"]
        nch = ncol // 64
        npair = ncol // 128
        lgf, cum, e1 = self.hset[h]
        lk, ck, ek = self.hkeys[h]
        dd = lgf
        e2 = lgf
        fk = [("arena", 8 + 2 * h), ("arena", 9 + 2 * h)]
        qk = [("arena", 2 * h), ("arena", 2 * h + 1)]
        fT = self.fT4[:, h, 0:ncol]
        self.ts(fT, fT, self.oml[:, h:h + 1], self.lb[:, h:h + 1], ALU.mult, ALU.add, fk + ["oml", "lb"], fk)
        self.act(lgf[:, 0:ncol], fT, AF.Ln, fk, lk)
        yield
        self.op("dve", lambda e: e.tensor_tensor_scan(out=cum[:, 0:ncol], data0=self.resetm[:, 0:ncol],
                                                      data1=lgf[:, 0:ncol], initial=0.0, op0=ALU.mult, op1=ALU.add),
                ["resetm"] + lk, ck)
        self.act(fT, fT, AF.Identity, fk + ["onesf"], fk, scale=-1.0, bias=self.onesf[:, 0:1])
        yield
        cum3 = cum[:, 0:ncol].rearrange("p (c t) -> p c t", t=64)
        self.tt(dd[:, 0:ncol].rearrange("p (c t) -> p c t", t=64), cum3, cum3[:, :, 32:33].to_broadcast([128, nch, 64]),
                ALU.subtract, ck, lk)
        if not so:
            self.act(e1[:, 0:ncol], dd[:, 0:ncol], AF.Exp, lk, ek)
        self.act(e2[:, 0:ncol], dd[:, 0:ncol], AF.Exp, lk, lk, scale=-1.0)
        yield
        if not so:
            self.tt(self.qpT[:, h, 0:ncol], self.sqa4[:, h, 0:ncol], e1[:, 0:ncol], ALU.mult, qk + ek, [("qpT", h)])
        self.tt(self.kpT[:, h, 0:ncol], fT, e2[:, 0:ncol], ALU.mult, fk + lk, [("kpT", h)])
        if not so:
            self.act(self.eref[:, h, 0:nch], cum3[:, :, 32], AF.Exp, ck, [("eref", h)])
        self.act(self.etot[:, h, 0:nch], cum3[:, :, 63], AF.Exp, ck, [("etot", h)])
        self.tt(self.etr[:, h, 0:nch], cum3[:, :, 63], cum3[:, :, 32], ALU.subtract, ck, [("etr", h)])
        self.act(self.etr[:, h, 0:nch], self.etr[:, h, 0:nch], AF.Exp, [("etr", h)], [("etr", h)])
        yield
        if not so:
            bA = self.pa()
            for p in range(npair):
                ps_ = slice(p * 128, (p + 1) * 128)
                self.mm(self.PB(bA)[:, ps_], self.kpT[:, h, ps_], self.qpT[:, h, ps_], True, True,
                        [("kpT", h), ("qpT", h)], [("ps", bA)])
            self.tt(self.AT[:, h, 0:ncol], self.PB(bA)[:, 0:ncol], self.trim[:, 0:npair, :].rearrange("p a b -> p (a b)"), ALU.mult,
                    [("ps", bA), "trim"], [("AT", h)])
            self.pf(bA)
        bT = self.pa()
        for p in range(npair):
            ps_ = slice(p * 128, (p + 1) * 128)
            self.mm(self.PB(bT)[:, ps_], self.kpT[:, h, ps_], self.identb[:], True, True,
                    [("kpT", h), "identb"], [("ps", bT)])
        bT3 = self.PB(bT)[:, 0:ncol].rearrange("p (a b) -> p a b", b=128)
        self.cp(self.ktokA[0:64, h, 0:npair, :], bT3[0:64, :, :], [("ps", bT)], [("ktokA", h)], eng="act")
        self.cp(self.ktokB[64:128, h, 0:npair, :], bT3[64:128, :, :], [("ps", bT)], [("ktokB", h)])
        self.pf(bT)
        yield

    def _hg_post(self, st, h, bO):
        ncol = st["ncol"]
        e1 = self.hset[h][2]
        ek = self.hkeys[h][2]
        sqo = self.hset[h][1][:, 0:256].bitcast(BF16)
        sk = self.hkeys[h][1]
        tb = self.hset[h][0]
        tk = self.hkeys[h][0]
        self.act(sqo[:, 0:ncol], self.PB(bO)[:, 0:ncol], AF.Square, [("ps", bO)], sk)
        bS = self.pa()
        self.mm(self.PB(bS)[:, 0:ncol], self.onesb[:], sqo[:, 0:ncol], True, True, ["onesb"] + sk, [("ps", bS)])
        self.act(e1[:, 0:ncol], self.PB(bS)[:, 0:ncol], AF.Ln, [("ps", bS), "epsc"], ek, scale=1.0 / 128,
                 bias=self.epsc[:, 0:1])
        self.pf(bS)
        yield
        self.act(e1[:, 0:ncol], e1[:, 0:ncol], AF.Exp, ek, ek, scale=-0.5)
        self.tt(tb[:, 0:ncol], self.PB(bO)[:, 0:ncol], e1[:, 0:ncol], ALU.mult, [("ps", bO)] + ek, tk)
        self.pf(bO)
        yield
        self.stt(self.oaT[:, h, 0:ncol], tb[:, 0:ncol], self.onorm[:, h:h + 1], self.sga[:, h, 0:ncol], ALU.mult, ALU.mult,
                 tk + ["onorm_sb", ("sga", h)], [("oaT", h)])

    def _attn_unit(self, q0, nq, qcol, tiles, PTs, ob_tt):
        nt = len(tiles)
        plain = [i for i, t in enumerate(tiles) if t["bias"] is None]
        biased = [i for i, t in enumerate(tiles) if t["bias"] is not None]
        nbi = len(biased)
        assert len(plain) <= 3 and nbi in (2, 3)
        npl = len(plain)
        assert plain == list(range(plain[0], plain[0] + npl)) and biased == list(range(biased[0], biased[0] + nbi))
        bO = [self.pa(), self.pa()]
        pipelined = isinstance(PTs, list)

        def stageA(h):
            hp = h // 2
            qsel = self.qTA if h % 2 == 0 else self.qTB
            qk = (qsel.name, hp)
            PT = PTs[h % NPT] if pipelined else PTs
            b1 = self.pa()
            for n_, i in enumerate(plain):
                kb, kt = tiles[i]["k"]
                self.mm(self.PB(b1)[:, n_ * nq:(n_ + 1) * nq], self.kT[kb][:, hp, kt * 128:(kt + 1) * 128],
                        qsel[:, hp, qcol:qcol + nq], True, True, [(self.kT[kb].name, hp), qk], [("ps", b1)])
            b2 = self.pa()
            for n_, i in enumerate(biased):
                kb, kt = tiles[i]["k"]
                self.mm(self.PB(b2)[:, n_ * nq:(n_ + 1) * nq], self.kT[kb][:, hp, kt * 128:(kt + 1) * 128],
                        qsel[:, hp, qcol:qcol + nq], True, True, [(self.kT[kb].name, hp), qk], [("ps", b2)])
            self.act(PT[:, plain[0]:plain[0] + npl, q0:q0 + nq], self.PB(b1)[:, 0:npl * nq].rearrange("p (a b) -> p a b", b=nq),
                     AF.Exp, [("ps", b1)], [(PT.name, 0)])
            for i in plain:
                if tiles[i]["zero"]:
                    self.memset(PT[0:64, i, 64:128], 0.0, [(PT.name, 0)])
            self.pf(b1)
            sl = h % NPT
            for n_, i in enumerate(biased):
                self.tt(self.sbias[:, sl, n_ * nq:(n_ + 1) * nq], self.PB(b2)[:, n_ * nq:(n_ + 1) * nq], tiles[i]["bias"](h), ALU.add,
                        [("ps", b2), "BT0", "BT3", "BT4", "BT5"], [("sbias", sl)])
            self.pf(b2)
            self.act(PT[:, biased[0]:biased[0] + nbi, q0:q0 + nq], self.sbias[:, sl, 0:nbi * nq].rearrange("p (a b) -> p a b", b=nq),
                     AF.Exp, [("sbias", sl)], [(PT.name, 0)])

        def stageB(h):
            PT = PTs[h % NPT] if pipelined else PTs
            ob_ = bO[h // 4]
            oc = (h % 4) * 65
            for i, t in enumerate(tiles):
                vb, vt = t["k"]
                self.mm(self.PB(ob_)[:, oc:oc + 65], PT[:, i, :], self.Va[vb][:, vt, h, :], i == 0, i == nt - 1,
                        [(PT.name, 0), (self.Va[vb].name, vt)], [("ps", ob_)])

        if pipelined:
            for h in range(NPT - 1):
                stageA(h)
            for h in range(8):
                if h + NPT - 1 < 8:
                    stageA(h + NPT - 1)
                stageB(h)
        else:
            for h in range(8):
                stageA(h)
                stageB(h)
        rows = slice(q0, q0 + nq)
        for g in range(2):
            o3 = self.PB(bO[g])[rows, 0:260].rearrange("p (h d) -> p h d", d=65)
            self.op("dve", lambda e, o3=o3, g=g: e.reciprocal(out=self.rden[rows, 4 * g:4 * g + 4], in_=o3[:, :, 64]),
                    [("ps", bO[g])], [("rden", g)])
            self.tt(self.ob[rows, ob_tt, 256 * g:256 * (g + 1)].rearrange("p (h d) -> p h d", d=64), o3[:, :, 0:64],
                    self.rden[rows, 4 * g:4 * g + 4].unsqueeze(2).to_broadcast([nq, 4, 64]), ALU.mult,
                    [("ps", bO[g]), ("rden", g)], [("AT", ob_tt)])
            self.pf(bO[g])

    def _attention(self, st):
        ncol = st["ncol"]
        kind = st["kind"]
        cur = st["cur"]
        prev = 1 - cur
        npair = ncol // 128
        if kind == "prompt":
            for p in range(npair):
                tiles = []
                for j in (1, 2, 0, 3, 4):
                    kk = (prev, p + j) if p + j <= 3 else (cur, p + j - 4)
                    if j == 3:
                        bias = (lambda h: self.BT3[:, h, :])
                    elif j == 4:
                        bias = (lambda h: self.BT4[:, h, :])
                    elif j == 0:
                        bias = (lambda h: self.BT0[:, :])
                    else:
                        bias = None
                    tiles.append(dict(k=kk, bias=bias, zero=False))
                self._attn_unit(0, 128, p * 128, tiles, self.PTf, p)
        else:
            for i in range(4):
                pr, half = i // 2, i % 2
                kdst, vdst = self.kT[prev], self.Va[prev]
                self.dma("pool", kdst[:], self.dr["ckT"][i], (), [kdst.name])
                self.dma("pool", vdst[:, :, :, 0:64], self.dr["cv"][i].rearrange("p t (h d) -> p t h d", d=64), (), [vdst.name])
                tiles = []
                for j in range(4):
                    bias = (lambda h: self.BT3[:, h, 0:64]) if j == 3 else None
                    tiles.append(dict(k=(prev, j), bias=bias, zero=False))
                own_bias = (lambda h: self.BT4[:, h, 0:64]) if half == 0 else (lambda h: self.BT5[:, h, 0:64])
                tiles.append(dict(k=(cur, pr), bias=own_bias, zero=False))
                self._attn_unit(half * 64, 64, i * 64, tiles, self.PTh[half], pr)
        for p in range(npair):
            b = self.pa()
            for vt in range(4):
                self.mm(self.PB(b)[:, vt * 128:(vt + 1) * 128], self.ob[:, p, vt * 128:(vt + 1) * 128], self.identb[:], True, True,
                        [("AT", p), "identb"], [("ps", b)])
            self.cp(self.obT[:, :, p * 128:(p + 1) * 128], self.PB(b)[:, :].rearrange("p (a b) -> p a b", b=128), [("ps", b)],
                    [("vtok", k) for k in range(4)], eng="act" if p % 2 else "dve")
            self.pf(b)

    def _finish(self):
        self.dma("sp", self.dr["s_out"][:, 4, :, :], self.Sst[:], ["Sst"], ["s_out4"])


def _blk(w, ncols_list=None):
    K, N = w.shape
    nb = N // 512
    return np.ascontiguousarray(w.reshape(K // 128, 128, nb, 512).transpose(2, 1, 0, 3))


_WIN_TILES = None


def _win_perm():
    tiles = []
    for h in range(4):
        tiles += [0 + h, 4 + h, 12 + h]
    tiles += [8, 9, 10, 11]
    tiles += list(range(16, 44))
    cols = np.concatenate([np.arange(t * 128, (t + 1) * 128) for t in tiles])
    return cols


_PROG = {}


def _get_prog(NS):
    if NS not in _PROG:
        b = Builder(NS)
        nc = b.build()
        _PROG[NS] = (nc, b)
    return _PROG[NS]


def make_in_maps(x_prompt, x_sample, c_prompt, c_sample, state_hgrn, cache_k, cache_v, w_ada, b_ada, norm_mix, w_in,
                 hgrn_lb_logits, hgrn_out_norm, w_branch_a, rel_bias, w_branch_b, w_out, norm_ffn, w_ffn_in, w_ffn_out,
                 norm_final):
    f32 = np.float32
    x_prompt = np.asarray(x_prompt, f32)
    x_sample = np.asarray(x_sample, f32)
    B, L, _ = x_prompt.shape
    SEG = L // 4
    NS = SEG // 512
    assert B == 2 and SEG % 512 == 0 and x_sample.shape[0] == 32 and x_sample.shape[1] == 64
    w_in_p = np.asarray(w_in[0], f32)[:, _win_perm()]
    wfi = np.asarray(w_ffn_in[0], f32)
    DFF = wfi.shape[1] // 2
    wfi_p = np.zeros((1024, 12 * 512), f32)
    wfi_p[:, 0:DFF] = wfi[:, 0:DFF]
    wfi_p[:, 6 * 512:6 * 512 + DFF] = wfi[:, DFF:]
    w_br = np.concatenate([np.asarray(w_branch_a[0], f32), np.asarray(w_branch_b[0], f32)], axis=0)
    shared = {
        "w_ada": _blk(np.asarray(w_ada[0], f32)),
        "w_in": _blk(w_in_p),
        "w_br": _blk(w_br),
        "w_out": _blk(np.asarray(w_out[0], f32)),
        "w_fi": _blk(wfi_p),
        "w_fo": _blk(np.asarray(w_ffn_out[0], f32)),
        "b_ada": np.ascontiguousarray(np.asarray(b_ada[0], f32).reshape(48, 128).T),
        "nmix": np.ascontiguousarray(np.asarray(norm_mix[0], f32).reshape(8, 128).T),
        "nffn": np.ascontiguousarray(np.asarray(norm_ffn[0], f32).reshape(8, 128).T),
        "nfin": np.ascontiguousarray(np.asarray(norm_final, f32).reshape(8, 128).T),
        "onorm": np.ascontiguousarray(np.asarray(hgrn_out_norm[0], f32).reshape(4, 128).T),
        "lbl": np.ascontiguousarray(np.asarray(hgrn_lb_logits, f32)[0:2].reshape(2, 4, 128).transpose(2, 1, 0)),
        "rb": np.ascontiguousarray(np.asarray(rel_bias[0], f32)),
    }
    c_prompt = np.asarray(c_prompt, f32)
    c_sample = np.asarray(c_sample, f32)
    state_hgrn = np.asarray(state_hgrn, f32)
    cache_k = np.asarray(cache_k, f32)
    cache_v = np.asarray(cache_v, f32)
    in_maps = []
    for r in range(NCORES):
        b, kseg = r // 4, r % 4
        xs = x_sample[4 * r:4 * r + 4].reshape(256, D)
        t0 = kseg * SEG
        if kseg == 0:
            xh = np.zeros((512, D), f32)
        else:
            xh = x_prompt[b, t0 - 512:t0]
        xp = x_prompt[b, t0:t0 + SEG]
        xT = np.ascontiguousarray(np.concatenate([xs, xh, xp], axis=0).T)
        cs = np.zeros((8, D), f32)
        cs[0] = c_prompt[b]
        cs[1:5] = c_sample[4 * r:4 * r + 4]
        cT = np.ascontiguousarray(cs.reshape(8, 8, 128).transpose(2, 1, 0))
        flags = np.zeros((128, 4), f32)
        flags[:, 0] = 0.0 if kseg == 0 else 1.0
        sin = np.ascontiguousarray(state_hgrn[0, 4 * r:4 * r + 4].transpose(2, 0, 1, 3))
        ck = cache_k[0, 4 * r:4 * r + 4]
        ckT = np.ascontiguousarray(ck.reshape(4, 512, 4, 128).transpose(0, 3, 2, 1))
        cv = np.ascontiguousarray(cache_v[0, 4 * r:4 * r + 4].reshape(4, 4, 128, 512).transpose(0, 2, 1, 3))
        m = dict(shared)
        m.update({"xT": xT, "cT": cT, "flags": flags, "sin": sin, "ckT": ckT, "cv": cv})
        in_maps.append(m)
    return NS, in_maps


def kernel(**inputs):
    f32 = np.float32
    NS, in_maps = make_in_maps(**inputs)
    L = NS * 512 * 4
    SEG = L // 4
    nc, bld = _get_prog(NS)
    res = run_bass_kernel_spmd(nc, in_maps, core_ids=list(range(NCORES)))
    return assemble(res.results, L)


def assemble(R, L):
    f32 = np.float32
    SEG = L // 4
    y_prompt = np.zeros((2, L, D), f32)
    y_sample = np.zeros((32, 64, D), f32)
    s_p = np.zeros((1, 2, 4, 128, 128), f32)
    k_p = np.zeros((1, 2, 512, 8, 64), f32)
    v_p = np.zeros((1, 2, 512, 8, 64), f32)
    s_s = np.zeros((1, 32, 4, 128, 128), f32)
    k_s = np.zeros((1, 32, 64, 8, 64), f32)
    v_s = np.zeros((1, 32, 64, 8, 64), f32)
    for r in range(NCORES):
        b, kseg = r // 4, r % 4
        o = R[r]
        yT = np.asarray(o["yT"])
        y_sample[4 * r:4 * r + 4] = yT[:, 0:256].T.reshape(4, 64, D)
        y_prompt[b, kseg * SEG:(kseg + 1) * SEG] = yT[:, 256:].T
        so = np.asarray(o["s_out"])
        s_s[0, 4 * r:4 * r + 4] = so[:, 0:4].transpose(1, 2, 0, 3)
        kTs = np.asarray(o["kTs"])
        k_s[0, 4 * r:4 * r + 4] = kTs.transpose(2, 1, 0).reshape(4, 64, 8, 64)
        vs = np.asarray(o["vs"])
        v_s[0, 4 * r:4 * r + 4] = vs.transpose(1, 0, 2).reshape(4, 64, 8, 64)
        if kseg == 3:
            s_p[0, b] = so[:, 4].transpose(1, 0, 2)
            k_p[0, b] = np.asarray(o["kTp"]).transpose(2, 1, 0).reshape(512, 8, 64)
            v_p[0, b] = np.asarray(o["vp"]).transpose(1, 0, 2).reshape(512, 8, 64)
    return (y_prompt, y_sample, s_p, k_p, v_p, s_s, k_s, v_s)
```
